# Optimizing a Trainium2 kernel written in Bass

```python
import math
import jax, jax.numpy as jnp
from jax import lax
import numpy as np

D_MODEL = 1024
BATCH = 4
SEQ = 4096
DEPTH = 4

GRID_W = 64
CTX_LEN = 256
N_MIXERS = 4
Q_BLOCK = 128
NORM_EPS = 1e-6
NEG_INF = -1e30
ROPE_BASE = 10000.0
ROPE_DIM = 64

MLA_HEADS = 8
MLA_Q_RANK = 384
MLA_KV_RANK = 256
MLA_NOPE = 128
MLA_ROPE = ROPE_DIM
MLA_V = 128
MLA_WIDTH = MLA_HEADS * MLA_V

HY_WIDTH = D_MODEL
HY_ORDER = 2
HY_SHORT = 3
HY_EMB = 33
HY_BANDS = (HY_EMB - 1) // 2
HY_FILT_HIDDEN = 64
HY_DECAY_TARGET = 1e-2
HY_FAST_DECAY = 0.3
HY_SLOW_DECAY = 1.5

SWA_Q_HEADS = 16
SWA_KV_HEADS = 4
SWA_GROUP = SWA_Q_HEADS // SWA_KV_HEADS
SWA_HEAD_DIM = ROPE_DIM
SWA_WINDOW = 128
SWA_WIDTH = SWA_Q_HEADS * SWA_HEAD_DIM

CF_WIDTH = D_MODEL
CF_KERNEL = 31

kernel_name = "hybrid_mla_hyena_swa_conformer_flow_trunk"


def n_layers_of(m):
    return len(range(m, DEPTH, N_MIXERS))


def rmsnorm(x, g):
    xf = x.astype(jnp.float32)
    y = xf * lax.rsqrt(jnp.mean(xf * xf, axis=-1, keepdims=True) + NORM_EPS)
    return (y * g.astype(jnp.float32)).astype(x.dtype)


def layernorm(x, g, b):
    xf = x.astype(jnp.float32)
    mu = jnp.mean(xf, axis=-1, keepdims=True)
    xc = xf - mu
    var = jnp.mean(xc * xc, axis=-1, keepdims=True)
    return (xc * lax.rsqrt(var + NORM_EPS) * g.astype(jnp.float32) + b.astype(jnp.float32)).astype(x.dtype)


def modulate(h, shift, scale):
    return h * (1 + scale) + shift


def grid_rope(L, dim, dtype):
    rows = L // GRID_W
    row = jnp.repeat(jnp.arange(rows, dtype=jnp.float32), GRID_W)
    col = jnp.tile(jnp.arange(GRID_W, dtype=jnp.float32), rows)
    n_freq = dim // 4
    inv = ROPE_BASE ** (-jnp.arange(n_freq, dtype=jnp.float32) / n_freq)
    ang = jnp.concatenate([row[:, None] * inv, col[:, None] * inv], axis=-1)
    return jnp.cos(ang).astype(dtype), jnp.sin(ang).astype(dtype)


def apply_rope(x, cos, sin):
    half = x.shape[-1] // 2
    x1, x2 = x[..., :half], x[..., half:]
    return jnp.concatenate([x1 * cos - x2 * sin, x1 * sin + x2 * cos], axis=-1)


def depthwise_conv(u, w, b):
    K, C = w.shape
    pad = (K - 1) // 2
    y = lax.conv_general_dilated(u, w[:, None, :].astype(u.dtype), window_strides=(1,),
                                 padding=[(pad, pad)], dimension_numbers=('NWC', 'WIO', 'NWC'),
                                 feature_group_count=C)
    return y + b


def to_blocks(t):
    B, L = t.shape[:2]
    return jnp.moveaxis(t.reshape(B, L // Q_BLOCK, Q_BLOCK, *t.shape[2:]), 1, 0)


def from_blocks(o):
    nb, B, Q = o.shape[:3]
    return jnp.moveaxis(o, 0, 1).reshape(B, nb * Q, *o.shape[3:])


def mla_mixer(a, ac, cos, sin, w_in, q_norm_g, kv_norm_g, w_uq, w_ukv, w_out, ctx_out):
    B, L, _ = a.shape
    splits = [MLA_Q_RANK, MLA_Q_RANK + MLA_KV_RANK, MLA_Q_RANK + MLA_KV_RANK + MLA_ROPE]

    def project(t):
        n = t.shape[1]
        cq, ckv, k_rope, gate = jnp.split(t @ w_in, splits, axis=-1)
        q = (rmsnorm(cq, q_norm_g) @ w_uq).reshape(B, n, MLA_HEADS, MLA_NOPE + MLA_ROPE)
        kv = (rmsnorm(ckv, kv_norm_g) @ w_ukv).reshape(B, n, MLA_HEADS, MLA_NOPE + MLA_V)
        return q[..., :MLA_NOPE], q[..., MLA_NOPE:], kv[..., :MLA_NOPE], k_rope, kv[..., MLA_NOPE:], gate

    qn, qr, kn, kr, v, gate = project(a)
    qr = apply_rope(qr, cos[:, None, :], sin[:, None, :])
    kr = apply_rope(kr, cos, sin)
    qnc, qrc, knc, krc, vc, gatec = project(ac)
    scale = (MLA_NOPE + MLA_ROPE) ** -0.5

    def scores(qn_, qr_, kn_, kr_):
        s = jnp.einsum('bqhd,bkhd->bhqk', qn_, kn_) + jnp.einsum('bqhr,bkr->bhqk', qr_, kr_)
        return s.astype(jnp.float32) * scale

    def attend(blk):
        qn_b, qr_b = blk
        s = jnp.concatenate([scores(qn_b, qr_b, kn, kr), scores(qn_b, qr_b, knc, krc)], axis=-1)
        p = jax.nn.softmax(s, axis=-1).astype(v.dtype)
        return (jnp.einsum('bhqk,bkhd->bqhd', p[..., :L], v)
                + jnp.einsum('bhqk,bkhd->bqhd', p[..., L:], vc))

    o = from_blocks(lax.map(attend, (to_blocks(qn), to_blocks(qr)))).reshape(B, L, MLA_WIDTH)
    y = (o * jax.nn.silu(gate)) @ w_out
    yc = None
    if ctx_out:
        pc = jax.nn.softmax(scores(qnc, qrc, knc, krc), axis=-1).astype(vc.dtype)
        oc = jnp.einsum('bhqk,bkhd->bqhd', pc, vc).reshape(B, -1, MLA_WIDTH)
        yc = (oc * jax.nn.silu(gatec)) @ w_out
    return y, yc


def hyena_filter_spectra(L, w_in, w_hid, b, freq, w_out):
    f32 = jnp.float32
    t = jnp.linspace(0.0, 1.0, L, dtype=f32)[:, None]
    wpos = (2.0 * math.pi / L) * jnp.arange(L, dtype=f32)[:, None]
    bands = jnp.linspace(1e-4, HY_BANDS - 1, HY_BANDS, dtype=f32)[None, :]
    hdn = jnp.concatenate([t, jnp.cos(bands * wpos), -jnp.sin(bands * wpos)], axis=-1)
    b, freq = b.astype(f32), freq.astype(f32)
    for k, w in enumerate((w_in, w_hid[0], w_hid[1])):
        hdn = jnp.sin(freq[k] * (hdn @ w.astype(f32) + b[k]))
    h = (hdn @ w_out.astype(f32)).reshape(L, HY_ORDER, 2, HY_WIDTH)
    deltas = jnp.abs(jnp.linspace(math.log(HY_DECAY_TARGET) / HY_FAST_DECAY,
                                  math.log(HY_DECAY_TARGET) / HY_SLOW_DECAY, HY_WIDTH, dtype=f32))
    h = h * jnp.exp(-t * deltas)[:, None, None, :]
    hf, hb = h[:, :, 0], h[:, :, 1]
    k_circ = jnp.concatenate([hf[:1] + hb[:1], hf[1:], jnp.zeros_like(hf[:1]), hb[:0:-1]], axis=0)
    return jnp.fft.rfft(k_circ, axis=0)


def fft_long_conv(u, spec):
    L = u.shape[1]
    U = jnp.fft.rfft(u.astype(jnp.float32), n=2 * L, axis=1)
    return jnp.fft.irfft(U * spec, n=2 * L, axis=1)[:, :L].astype(u.dtype)


def hyena_seq(t, w_in, conv_w, conv_b, spec, bias, w_out):
    u, gate = jnp.split(t @ w_in, [(HY_ORDER + 1) * HY_WIDTH], axis=-1)
    u = depthwise_conv(u, conv_w, conv_b)
    v, *gates = jnp.split(u, HY_ORDER + 1, axis=-1)
    z = v
    for o in range(HY_ORDER):
        z = gates[o] * (fft_long_conv(z, spec[:, o]) + z * bias[o])
    return (z * jax.nn.silu(gate)) @ w_out


def swa_mixer(a, ac, cos, sin, w_in, sink, w_out, ctx_out):
    B, L, _ = a.shape
    nq = SWA_Q_HEADS * SWA_HEAD_DIM
    nkv = SWA_KV_HEADS * SWA_HEAD_DIM

    def project(t):
        n = t.shape[1]
        q, k, v, gate = jnp.split(t @ w_in, [nq, nq + nkv, nq + 2 * nkv], axis=-1)
        return (q.reshape(B, n, SWA_KV_HEADS, SWA_GROUP, SWA_HEAD_DIM),
                k.reshape(B, n, SWA_KV_HEADS, SWA_HEAD_DIM),
                v.reshape(B, n, SWA_KV_HEADS, SWA_HEAD_DIM), gate)

    q, k, v, gate = project(a)
    q = apply_rope(q, cos[:, None, None, :], sin[:, None, None, :])
    k = apply_rope(k, cos[:, None, :], sin[:, None, :])
    qc, kc, vc, gatec = project(ac)
    scale = SWA_HEAD_DIM ** -0.5
    sink_logit = sink.astype(jnp.float32).reshape(SWA_KV_HEADS, SWA_GROUP, 1, 1)

    def softmax_with_sink(parts, n_q):
        s_sink = jnp.broadcast_to(sink_logit, (B, SWA_KV_HEADS, SWA_GROUP, n_q, 1))
        p = jax.nn.softmax(jnp.concatenate([*parts, s_sink], axis=-1), axis=-1)
        return p[..., :-1].astype(v.dtype)

    span = Q_BLOCK + 2 * SWA_WINDOW
    pad = ((0, 0), (SWA_WINDOW, SWA_WINDOW), (0, 0), (0, 0))
    kp, vp = jnp.pad(k, pad), jnp.pad(v, pad)
    rel = jnp.arange(span)[None, :] - jnp.arange(Q_BLOCK)[:, None]
    band = (rel >= 0) & (rel <= 2 * SWA_WINDOW)

    def attend(args):
        j, q_b = args
        start = j * Q_BLOCK
        k_b = lax.dynamic_slice_in_dim(kp, start, span, axis=1)
        v_b = lax.dynamic_slice_in_dim(vp, start, span, axis=1)
        kpos = start - SWA_WINDOW + jnp.arange(span)
        valid = band & ((kpos >= 0) & (kpos < L))[None, :]
        s_win = jnp.einsum('bqhgd,bkhd->bhgqk', q_b, k_b).astype(jnp.float32) * scale
        s_win = jnp.where(valid, s_win, NEG_INF)
        s_ctx = jnp.einsum('bqhgd,bkhd->bhgqk', q_b, kc).astype(jnp.float32) * scale
        p = softmax_with_sink([s_win, s_ctx], Q_BLOCK)
        return (jnp.einsum('bhgqk,bkhd->bqhgd', p[..., :span], v_b)
                + jnp.einsum('bhgqk,bkhd->bqhgd', p[..., span:], vc))

    o = from_blocks(lax.map(attend, (jnp.arange(L // Q_BLOCK), to_blocks(q)))).reshape(B, L, SWA_WIDTH)
    y = (o * jax.nn.silu(gate)) @ w_out
    yc = None
    if ctx_out:
        s = jnp.einsum('bqhgd,bkhd->bhgqk', qc, kc).astype(jnp.float32) * scale
        pc = softmax_with_sink([s], qc.shape[1])
        oc = jnp.einsum('bhgqk,bkhd->bqhgd', pc, vc).reshape(B, -1, SWA_WIDTH)
        yc = (oc * jax.nn.silu(gatec)) @ w_out
    return y, yc


def conformer_seq(t, w_in, dw_w, dw_b, ln_g, ln_b, w_out):
    a, b, gate = jnp.split(t @ w_in, 3, axis=-1)
    u = a * jax.nn.sigmoid(b)
    u = depthwise_conv(u, dw_w, dw_b)
    u = jax.nn.silu(layernorm(u, ln_g, ln_b))
    return (u * jax.nn.silu(gate)) @ w_out


def setup_inputs(seed: int = 0) -> dict:
    key = jax.random.key(seed)
    ks = iter(jax.random.split(key, 40))
    D = D_MODEL
    f32 = jnp.float32

    def nrm(shape, std):
        return jax.random.normal(next(ks), shape, f32) * std

    def gain(shape):
        return 1.0 + nrm(shape, 0.02)

    nA, nB, nC, nD = (n_layers_of(m) for m in range(N_MIXERS))
    mla_in = MLA_Q_RANK + MLA_KV_RANK + MLA_ROPE + MLA_WIDTH
    swa_in = SWA_Q_HEADS * SWA_HEAD_DIM + 2 * SWA_KV_HEADS * SWA_HEAD_DIM + SWA_WIDTH
    return {
        "x": nrm((BATCH, SEQ, D), 1.0),
        "c": nrm((BATCH, D), 1.0),
        "ctx": nrm((BATCH, CTX_LEN, D), 1.0),
        "c_ctx": nrm((D,), 1.0),
        "norm_g": gain((DEPTH, D)),
        "ada_w": nrm((DEPTH, D, 3 * D), 0.5 * D ** -0.5),
        "ada_b": nrm((DEPTH, 3 * D), 0.02),
        "final_g": gain((D,)),
        "mla_w_in": nrm((nA, D, mla_in), D ** -0.5),
        "mla_q_norm_g": gain((nA, MLA_Q_RANK)),
        "mla_kv_norm_g": gain((nA, MLA_KV_RANK)),
        "mla_w_uq": nrm((nA, MLA_Q_RANK, MLA_HEADS * (MLA_NOPE + MLA_ROPE)), MLA_Q_RANK ** -0.5),
        "mla_w_ukv": nrm((nA, MLA_KV_RANK, MLA_HEADS * (MLA_NOPE + MLA_V)), MLA_KV_RANK ** -0.5),
        "mla_w_out": nrm((nA, MLA_WIDTH, D), MLA_WIDTH ** -0.5),
        "hy_w_in": nrm((nB, D, (HY_ORDER + 2) * HY_WIDTH), D ** -0.5),
        "hy_conv_w": nrm((nB, HY_SHORT, (HY_ORDER + 1) * HY_WIDTH), HY_SHORT ** -0.5),
        "hy_conv_b": nrm((nB, (HY_ORDER + 1) * HY_WIDTH), 0.02),
        "hy_filt_w_in": nrm((nB, HY_EMB, HY_FILT_HIDDEN), HY_EMB ** -0.5),
        "hy_filt_w_hid": nrm((nB, 2, HY_FILT_HIDDEN, HY_FILT_HIDDEN), HY_FILT_HIDDEN ** -0.5),
        "hy_filt_b": nrm((nB, 3, HY_FILT_HIDDEN), 0.2),
        "hy_filt_freq": gain((nB, 3, HY_FILT_HIDDEN)),
        "hy_filt_w_out": nrm((nB, HY_FILT_HIDDEN, HY_ORDER * 2 * HY_WIDTH), 0.005),
        "hy_bias": nrm((nB, HY_ORDER, HY_WIDTH), 0.5),
        "hy_w_out": nrm((nB, HY_WIDTH, D), HY_WIDTH ** -0.5),
        "swa_w_in": nrm((nC, D, swa_in), D ** -0.5),
        "swa_sink": nrm((nC, SWA_Q_HEADS), 0.5),
        "swa_w_out": nrm((nC, SWA_WIDTH, D), SWA_WIDTH ** -0.5),
        "cf_w_in": nrm((nD, D, 3 * CF_WIDTH), D ** -0.5),
        "cf_dw_w": nrm((nD, CF_KERNEL, CF_WIDTH), CF_KERNEL ** -0.5),
        "cf_dw_b": nrm((nD, CF_WIDTH), 0.02),
        "cf_ln_g": gain((nD, CF_WIDTH)),
        "cf_ln_b": nrm((nD, CF_WIDTH), 0.02),
        "cf_w_out": nrm((nD, CF_WIDTH, D), CF_WIDTH ** -0.5),
    }


def reference(x, c, ctx, c_ctx, norm_g, ada_w, ada_b, final_g,
              mla_w_in, mla_q_norm_g, mla_kv_norm_g, mla_w_uq, mla_w_ukv, mla_w_out,
              hy_w_in, hy_conv_w, hy_conv_b, hy_filt_w_in, hy_filt_w_hid, hy_filt_b, hy_filt_freq,
              hy_filt_w_out, hy_bias, hy_w_out,
              swa_w_in, swa_sink, swa_w_out,
              cf_w_in, cf_dw_w, cf_dw_b, cf_ln_g, cf_ln_b, cf_w_out):
    B, L, _ = x.shape
    Lc = ctx.shape[1]
    cos, sin = grid_rope(L, ROPE_DIM, x.dtype)
    silu_c = jax.nn.silu(c)
    silu_cc = jax.nn.silu(c_ctx)
    xc = ctx
    for i in range(DEPTH):
        m, j = i % N_MIXERS, i // N_MIXERS
        ctx_out = i < DEPTH - 1
        shift, scale, gate = jnp.split((silu_c @ ada_w[i] + ada_b[i])[:, None, :], 3, axis=-1)
        a = modulate(rmsnorm(x, norm_g[i]), shift, scale)
        if ctx_out or m in (0, 2):
            shift_c, scale_c, gate_c = jnp.split(silu_cc @ ada_w[i] + ada_b[i], 3, axis=-1)
            ac = modulate(rmsnorm(xc, norm_g[i]), shift_c, scale_c)
        yc = None
        if m == 0:
            y, yc = mla_mixer(a, ac, cos, sin, mla_w_in[j], mla_q_norm_g[j], mla_kv_norm_g[j],
                              mla_w_uq[j], mla_w_ukv[j], mla_w_out[j], ctx_out)
        elif m == 1:
            filt = (hy_filt_w_in[j], hy_filt_w_hid[j], hy_filt_b[j], hy_filt_freq[j], hy_filt_w_out[j])
            y = hyena_seq(a, hy_w_in[j], hy_conv_w[j], hy_conv_b[j],
                          hyena_filter_spectra(L, *filt), hy_bias[j], hy_w_out[j])
            if ctx_out:
                yc = hyena_seq(ac, hy_w_in[j], hy_conv_w[j], hy_conv_b[j],
                               hyena_filter_spectra(Lc, *filt), hy_bias[j], hy_w_out[j])
        elif m == 2:
            y, yc = swa_mixer(a, ac, cos, sin, swa_w_in[j], swa_sink[j], swa_w_out[j], ctx_out)
        else:
            y = conformer_seq(a, cf_w_in[j], cf_dw_w[j], cf_dw_b[j], cf_ln_g[j], cf_ln_b[j], cf_w_out[j])
            if ctx_out:
                yc = conformer_seq(ac, cf_w_in[j], cf_dw_w[j], cf_dw_b[j], cf_ln_g[j], cf_ln_b[j], cf_w_out[j])
        x = x + gate * y
        if ctx_out:
            xc = xc + gate_c * yc
    return rmsnorm(x, final_g)
```

```python
import numpy as np
import concourse.bass as bass
import concourse.mybir as mybir
from concourse.bass_utils import run_bass_kernel_spmd

F32 = mybir.dt.float32
BF16 = mybir.dt.bfloat16
AF = mybir.ActivationFunctionType
ALU = mybir.AluOpType
AX = mybir.AxisListType


class _Op:
    __slots__ = ("eng", "fn", "deps", "needs_inc", "signal", "is_dma", "idx", "is_barrier")

    def __init__(self, eng, fn, is_dma):
        self.eng = eng
        self.fn = fn
        self.deps = set()
        self.needs_inc = False
        self.signal = None
        self.is_dma = is_dma
        self.is_barrier = False


class FW:
    N_DSEM = 40

    def __init__(self):
        nc = self.nc = bass.Bass("TRN2", target_bir_lowering=False)
        self.eng = dict(pe=nc.tensor, act=nc.scalar, dve=nc.vector, pool=nc.gpsimd, sp=nc.sync)
        self.esem = {k: nc.alloc_semaphore("s_" + k) for k in ("pe", "act", "dve", "pool")}
        self.dsem = [nc.alloc_semaphore("d%d" % i) for i in range(self.N_DSEM)]
        self.dcnt = [0] * self.N_DSEM
        self.dlast = [None] * self.N_DSEM
        self.dnext = 0
        self.ops = []
        self.reg = {}
        self.n_alloc = 0
        self.last_op = {}
        self.dma_pending = []
        self.scopes = []

    def sb(self, shape, dt=F32, name=None):
        self.n_alloc += 1
        nm = (name or "sb") + "_%d" % self.n_alloc
        if self.scopes:
            g = self.nc.sbuf_tensor(nm, list(shape), dt)
            t = g.__enter__()
            self.scopes[-1].append(g)
            return t
        return self.nc.alloc_sbuf_tensor(nm, list(shape), dt)

    def push(self):
        self.scopes.append([])

    def pop(self):
        self.barrier()
        for g in reversed(self.scopes.pop()):
            g.__exit__(None, None, None)

    def barrier(self):
        o = _Op("sp", None, False)
        o.is_barrier = True
        o.deps = set(self.last_op.values()) | set(self.dma_pending)
        self.dma_pending = []
        self.ops.append(o)
        self.reg = {}

    def ps(self, name=None, shape=(128, 512), dt=F32):
        self.n_alloc += 1
        return self.nc.alloc_psum_tensor(name or ("ps%d" % self.n_alloc), list(shape), dt)

    def dram(self, name, shape, dt=F32, kind="Internal"):
        return self.nc.dram_tensor(name, list(shape), dt, kind=kind)

    @staticmethod
    def _box(ap):
        name = ap.tensor.name
        sp = str(ap.space)
        aps = ap.ap
        off = int(ap.offset)
        if "DRAM" in sp:
            lo = hi = off
            for st, cnt in aps:
                if st > 0:
                    hi += st * (cnt - 1)
                else:
                    lo += st * (cnt - 1)
            return (name, 0, 1, lo, hi + 1)
        if "PSUM" in sp:
            return (name, 0, 128, 0, 1 << 30)
        pst, pn = aps[0]
        p0 = off // pst
        f0 = off % pst
        lo = hi = f0
        for st, cnt in aps[1:]:
            if st > 0:
                hi += st * (cnt - 1)
            else:
                lo += st * (cnt - 1)
        return (name, p0, p0 + pn, lo, hi + 1)

    def _track(self, op, opi, reads, writes):
        deps = op.deps
        rkey = opi if op.is_dma else op.eng
        for ap in reads:
            b = self._box(ap)
            for r in self.reg.get(b[0], ()):
                if r[0] < b[2] and b[1] < r[1] and r[2] < b[4] and b[3] < r[3]:
                    if r[4] is not None:
                        deps.add(r[4])
                    r[5][rkey] = opi
        for ap in writes:
            b = self._box(ap)
            recs = self.reg.get(b[0], [])
            new = []
            for r in recs:
                if r[0] < b[2] and b[1] < r[1] and r[2] < b[4] and b[3] < r[3]:
                    if r[4] is not None:
                        deps.add(r[4])
                    deps.update(r[5].values())
                    if b[1] <= r[0] and r[1] <= b[2] and b[3] <= r[2] and r[3] <= b[4]:
                        continue
                new.append(r)
            new.append([b[1], b[2], b[3], b[4], opi, {}])
            self.reg[b[0]] = new
        deps.discard(opi)

    def op(self, eng, fn, reads, writes):
        o = _Op(eng, fn, False)
        opi = len(self.ops)
        self.ops.append(o)
        self.last_op[eng] = opi
        self._track(o, opi, reads, writes)
        return o

    def dma(self, out, in_, q="sp", **kw):
        o = _Op(q, None, True)
        opi = len(self.ops)
        s = self.dnext
        self.dnext = (self.dnext + 1) % self.N_DSEM
        self.dcnt[s] += 1
        o.signal = (self.dsem[s], 16 * self.dcnt[s])
        if self.dlast[s] is not None:
            o.deps.add(self.dlast[s])
        self.dlast[s] = opi
        e = self.eng[q]
        o.fn = lambda: e.dma_start(out=out, in_=in_, **kw)
        self.ops.append(o)
        self.dma_pending.append(opi)
        self._track(o, opi, [in_], [out])
        return o

    def collective(self, kind, out, in_, groups):
        o = _Op("pool", None, True)
        opi = len(self.ops)
        s = self.dnext
        self.dnext = (self.dnext + 1) % self.N_DSEM
        self.dcnt[s] += 1
        o.signal = (self.dsem[s], 16 * self.dcnt[s])
        if self.dlast[s] is not None:
            o.deps.add(self.dlast[s])
        self.dlast[s] = opi
        e = self.eng["pool"]
        aop = ALU.bypass if kind in ("AllGather", "AllToAll") else ALU.add
        o.fn = lambda: e.collective_compute(kind, aop, replica_groups=groups, ins=[in_], outs=[out])
        self.ops.append(o)
        self.dma_pending.append(opi)
        self._track(o, opi, [in_], [out])
        return o

    def finish(self):
        ops = self.ops
        for o in ops:
            for d in o.deps:
                p = ops[d]
                if p.is_dma:
                    continue
                if p.eng == "pe" and o.eng == "pe" and not o.is_dma and not o.is_barrier:
                    continue
                p.needs_inc = True
        cnt = {k: 0 for k in self.esem}
        for o in ops:
            if not o.is_dma and o.needs_inc and not o.is_barrier:
                cnt[o.eng] += 1
                o.signal = (self.esem[o.eng], cnt[o.eng])
        waited = {k: {} for k in self.eng}
        for o in ops:
            if o.is_barrier:
                for en, e in self.eng.items():
                    w = waited[en]
                    need = {}
                    for d in o.deps:
                        p = ops[d]
                        if p.signal is None:
                            continue
                        sem, val = p.signal
                        if w.get(sem.name, 0) < val and need.get(sem.name, (None, 0))[1] < val:
                            need[sem.name] = (sem, val)
                    for sem, val in need.values():
                        e.wait_ge(sem, val)
                        w[sem.name] = val
                continue
            e = self.eng[o.eng]
            w = waited[o.eng]
            need = {}
            for d in o.deps:
                p = ops[d]
                if p.signal is None:
                    continue
                if (not p.is_dma) and (not p.is_barrier) and p.eng == "pe" and o.eng == "pe" and not o.is_dma:
                    continue
                sem, val = p.signal
                if w.get(sem.name, 0) < val and need.get(sem.name, (None, 0))[1] < val:
                    need[sem.name] = (sem, val)
            for sem, val in need.values():
                e.wait_ge(sem, val)
                w[sem.name] = val
            ins = o.fn()
            if o.signal is not None:
                ins.then_inc(o.signal[0], 16 if o.is_dma else 1)
        sp = self.eng["sp"]
        for i, s in enumerate(self.dsem):
            if self.dcnt[i]:
                sp.wait_ge(s, 16 * self.dcnt[i])
        for k, s in self.esem.items():
            if cnt[k]:
                sp.wait_ge(s, cnt[k])
        self.stats = dict(n_ops=len(ops), cnt=cnt)
        return self.nc

    def mm(self, out, lhsT, rhs, start=True, stop=True):
        pe = self.eng["pe"]
        return self.op("pe", lambda: pe.matmul(out, lhsT, rhs, start=start, stop=stop),
                       [lhsT, rhs] + ([] if start else [out]), [out])

    def transpose(self, out, in_, ident):
        pe = self.eng["pe"]
        return self.op("pe", lambda: pe.transpose(out, in_, ident), [in_, ident], [out])

    def act(self, out, in_, func, bias=None, scale=None, accum_out=None):
        a = self.eng["act"]
        kw = {}
        reads = [in_]
        writes = [out]
        if bias is not None:
            kw["bias"] = bias
            if not isinstance(bias, (int, float)):
                reads.append(bias)
        if scale is not None:
            kw["scale"] = scale
            if not isinstance(scale, (int, float)):
                reads.append(scale)
        if accum_out is not None:
            kw["accum_out"] = accum_out
            writes.append(accum_out)
        return self.op("act", lambda: a.activation(out, in_, func, **kw), reads, writes)

    def tt(self, out, in0, in1, op, eng="dve"):
        e = self.eng[eng]
        return self.op(eng, lambda: e.tensor_tensor(out, in0, in1, op), [in0, in1], [out])

    def ts(self, out, in0, s1, op0, s2=None, op1=None, eng="dve", accum_out=None):
        e = self.eng[eng]
        reads = [in0]
        for s in (s1, s2):
            if s is not None and not isinstance(s, (int, float)):
                reads.append(s)
        writes = [out]
        kw = {}
        if accum_out is not None:
            kw["accum_out"] = accum_out
            writes.append(accum_out)
        if op1 is None:
            return self.op(eng, lambda: e.tensor_scalar(out, in0, s1, None, op0, **kw), reads, writes)
        return self.op(eng, lambda: e.tensor_scalar(out, in0, s1, s2, op0, op1, **kw), reads, writes)

    def stt(self, out, in0, scalar, in1, op0, op1, eng="dve"):
        e = self.eng[eng]
        reads = [in0, in1]
        if not isinstance(scalar, (int, float)):
            reads.append(scalar)
        return self.op(eng, lambda: e.scalar_tensor_tensor(out, in0, scalar, in1, op0, op1), reads, [out])

    def copy(self, out, in_, eng="dve"):
        e = self.eng[eng]
        if eng == "act":
            return self.op(eng, lambda: e.copy(out, in_), [in_], [out])
        return self.op(eng, lambda: e.tensor_copy(out, in_), [in_], [out])

    def memset(self, ap, val, eng="dve"):
        e = self.eng[eng]
        return self.op(eng, lambda: e.memset(ap, val), [], [ap])

    def recip(self, out, in_):
        e = self.eng["dve"]
        return self.op("dve", lambda: e.reciprocal(out, in_), [in_], [out])

    def reduce(self, out, in_, op, axis=AX.X):
        e = self.eng["dve"]
        return self.op("dve", lambda: e.tensor_reduce(out, in_, axis, op), [in_], [out])


import numpy as np

D = 1024
NKC = 8
EPS = 1e-6


def emit_ada(fw, P, cT, layers, ada_w, ada_b, norm_g):
    outs = {}
    for li in layers:
        outs[li] = (fw.sb([128, 24, 2], F32, "ada_l%d" % li), fw.sb([128, 8, 2], F32, "geff_l%d" % li))
    fw.push()
    cs = fw.sb([128, NKC, 2], F32, "cs")
    fw.dma(cs[:], cT.ap().rearrange("(c p) r -> p c r", p=128))
    sc = fw.sb([128, NKC, 2], F32, "silu_c")
    fw.act(sc[:], cs[:], AF.Silu)
    stg = [fw.sb([128, NKC, 512], F32, "adastg%d" % i) for i in range(2)]
    for n, li in enumerate(layers):
        ada, geff = outs[li]
        ps = P[n % 2]
        for cchunk in range(6):
            st = stg[cchunk % 2]
            fw.dma(st[:], ada_w.ap()[n].rearrange("(c p) n -> p c n", p=128)[:, :, cchunk * 512:(cchunk + 1) * 512])
            for jj in range(4):
                j = cchunk * 4 + jj
                for kc in range(NKC):
                    fw.mm(ps[:, 2 * j:2 * j + 2], st[:, kc, jj * 128:(jj + 1) * 128], sc[:, kc, :],
                          start=(kc == 0), stop=(kc == NKC - 1))
        bsb = fw.sb([128, 24], F32, "adab")
        fw.dma(bsb[:], ada_b.ap()[n])
        gsb = fw.sb([128, 8], F32, "ng")
        fw.dma(gsb[:], norm_g.ap()[n])
        psv = ps[:, 0:48].rearrange("p (j r) -> p j r", r=2)
        for r in range(2):
            fw.tt(ada[:, :, r], psv[:, :, r], bsb[:], ALU.add)
        for r in range(2):
            fw.stt(geff[:, :, r], ada[:, 8:16, r], 1.0, gsb[:], ALU.add, ALU.mult)
    fw.pop()
    return outs


def norm_mod(fw, ps_ss, xs, T, geff_r, shift_r, a_bf, sq, rstd, ones_bf, eps_t):
    fw.act(sq[:, :, :T], xs[:, :, :T], AF.Square)
    for kc in range(NKC):
        fw.mm(ps_ss[:, :T], ones_bf[:], sq[:, kc, :T], start=(kc == 0), stop=(kc == NKC - 1))
    fw.act(rstd[:, :T], ps_ss[:, :T], AF.Sqrt, bias=eps_t[:, 0:1], scale=1.0 / D)
    fw.recip(rstd[:, :T], rstd[:, :T])
    for kc in range(NKC):
        fw.tt(xs[:, kc, :T], xs[:, kc, :T], rstd[:, :T], ALU.mult)
        fw.act(a_bf[:, kc, :T], xs[:, kc, :T], AF.Identity, bias=shift_r[:, kc:kc + 1], scale=geff_r[:, kc:kc + 1])


def build_A():
    fw = FW()
    NQ, NKV, NCX = 2048, 4096, 256
    NTOK = NKV + NCX
    NQT = NQ + NCX
    T1 = 256
    xT = fw.dram("xT", [D, NTOK], F32, "ExternalInput")
    cT = fw.dram("cT", [D, 2], F32, "ExternalInput")
    ropeC = fw.dram("ropeC", [64, NTOK], F32, "ExternalInput")
    ropeS = fw.dram("ropeS", [64, NTOK], F32, "ExternalInput")
    norm_g = fw.dram("norm_g", [1, 128, 8], F32, "ExternalInput")
    ada_w = fw.dram("ada_w", [1, D, 3 * D], F32, "ExternalInput")
    ada_b = fw.dram("ada_b", [1, 128, 24], F32, "ExternalInput")
    w_in = fw.dram("mla_w_in", [D, 1728], F32, "ExternalInput")
    qng = fw.dram("q_norm_g", [128, 3], F32, "ExternalInput")
    kvng = fw.dram("kv_norm_g", [128, 2], F32, "ExternalInput")
    w_uq = fw.dram("mla_w_uq", [384, 1536], F32, "ExternalInput")
    w_ukv = fw.dram("mla_w_ukv", [256, 2048], F32, "ExternalInput")
    w_out = fw.dram("mla_w_out", [D, D], F32, "ExternalInput")
    x1T = fw.dram("x1T", [D, NQT], F32, "ExternalOutput")

    P = [fw.ps("P%d" % i) for i in range(8)]
    ones_bf = fw.sb([128, 128], BF16, "ones")
    fw.memset(ones_bf[:], 1.0)
    eps_t = fw.sb([128, 1], F32, "eps")
    fw.memset(eps_t[:], EPS)
    eps_q = fw.sb([128, 1], F32, "epsq")
    fw.memset(eps_q[:], EPS)

    ada = emit_ada(fw, P, cT, [0], ada_w, ada_b, norm_g)
    ada0, geff0 = ada[0]

    cqn = fw.sb([128, 3, NQT], BF16, "cqn")
    ckvn = fw.sb([128, 2, NTOK], BF16, "ckvn")
    krT = fw.sb([64, NTOK], BF16, "krT")
    sgate = fw.sb([128, 8, NQT], BF16, "sgate")
    qg = fw.sb([128, 3], F32, "qg")
    kg = fw.sb([128, 2], F32, "kg")
    fw.dma(qg[:], qng.ap())
    fw.dma(kg[:], kvng.ap())

    fw.push()
    w_in_sb = fw.sb([128, NKC, 1728], BF16, "w_in")
    for kc in range(NKC):
        fw.dma(w_in_sb[:, kc, :], w_in.ap()[kc * 128:(kc + 1) * 128, :], q="pool")
    w_krrot = fw.sb([128, NKC, 64], BF16, "w_krrot")
    fw.ts(w_krrot[:, :, 0:32], w_in_sb[:, :, 672:704], -1.0, ALU.mult)
    fw.copy(w_krrot[:, :, 32:64], w_in_sb[:, :, 640:672])
    xs2 = [fw.sb([128, NKC, T1], F32, "xs%d" % i) for i in range(2)]
    a2 = [fw.sb([128, NKC, T1], BF16, "abf%d" % i) for i in range(2)]
    sq = fw.sb([128, NKC, T1], BF16, "sq")
    rstd = fw.sb([128, T1], F32, "rstd")
    raw = fw.sb([128, 3, T1], F32, "raw")
    sq2 = fw.sb([128, 3, T1], BF16, "sq2")
    rstd2 = fw.sb([128, T1], F32, "rstd2")
    cc = fw.sb([64, T1], F32, "cc")
    ss = fw.sb([64, T1], F32, "ss")
    ra = fw.sb([64, T1], F32, "ra")
    rb = fw.sb([64, T1], F32, "rb")
    xTv = xT.ap().rearrange("(c p) n -> p c n", p=128)
    chunks = []
    for c0 in range(0, NTOK, T1):
        kind = "own" if c0 < NQ else ("other" if c0 < NKV else "ctx")
        chunks.append((c0, kind))
    pi = 0
    for ci, (c0, kind) in enumerate(chunks):
        xs = xs2[ci % 2]
        a_bf = a2[ci % 2]
        r = 1 if kind == "ctx" else 0
        fw.dma(xs[:], xTv[:, :, c0:c0 + T1])
        fw.dma(cc[:], ropeC.ap()[:, c0:c0 + T1])
        fw.dma(ss[:], ropeS.ap()[:, c0:c0 + T1])
        norm_mod(fw, P[0], xs, T1, geff0[:, :, r], ada0[:, 0:8, r], a_bf, sq, rstd, ones_bf, eps_t)

        def proj(ps_ap, wcols, m):
            for kc in range(NKC):
                fw.mm(ps_ap, wcols(kc), a_bf[:, kc, :], start=(kc == 0), stop=(kc == NKC - 1))

        def rms_group(nf, col_base, gvec, dst, dcol0):
            fw.tt(sq2[:, :nf, :], raw[:, :nf, :], raw[:, :nf, :], ALU.mult)
            for j in range(nf):
                fw.mm(P[1][:, :T1], ones_bf[:], sq2[:, j, :], start=(j == 0), stop=(j == nf - 1))
            fw.act(rstd2[:], P[1][:, :T1], AF.Sqrt, bias=eps_q[:, 0:1], scale=1.0 / (nf * 128))
            fw.recip(rstd2[:], rstd2[:])
            for j in range(nf):
                fw.stt(dst[:, j, dcol0:dcol0 + T1], raw[:, j, :], gvec[:, j:j + 1], rstd2[:], ALU.mult, ALU.mult)

        for j in range(2):
            ps = P[2 + (pi % 4)]; pi += 1
            proj(ps[:, :T1], lambda kc, j=j: w_in_sb[:, kc, 384 + j * 128:384 + (j + 1) * 128], 128)
            fw.copy(raw[:, j, :], ps[:, :T1], eng="act")
        rms_group(2, 384, kg, ckvn, c0)
        ps = P[2 + (pi % 4)]; pi += 1
        psr = P[2 + (pi % 4)]; pi += 1
        proj(ps[0:64, 0:T1], lambda kc: w_in_sb[:, kc, 640:704], 64)
        proj(psr[0:64, 0:T1], lambda kc: w_krrot[:, kc, :], 64)
        fw.tt(ra[:], ps[0:64, 0:T1], cc[:], ALU.mult)
        fw.tt(rb[:], psr[0:64, 0:T1], ss[:], ALU.mult)
        fw.tt(krT[:, c0:c0 + T1], ra[:], rb[:], ALU.add)
        if kind != "other":
            q0 = c0 if kind == "own" else NQ + (c0 - NKV)
            for j in range(3):
                ps = P[2 + (pi % 4)]; pi += 1
                proj(ps[:, :T1], lambda kc, j=j: w_in_sb[:, kc, j * 128:(j + 1) * 128], 128)
                fw.copy(raw[:, j, :], ps[:, :T1], eng="act")
            rms_group(3, 0, qg, cqn, q0)
            for j in range(8):
                ps = P[2 + (pi % 4)]; pi += 1
                proj(ps[:, :T1], lambda kc, j=j: w_in_sb[:, kc, 704 + j * 128:704 + (j + 1) * 128], 128)
                fw.act(sgate[:, j, q0:q0 + T1], ps[:, :T1], AF.Silu)
    fw.pop()

    fw.push()
    w_uq_sb = fw.sb([128, 3, 1536], BF16, "w_uq")
    for kc in range(3):
        fw.dma(w_uq_sb[:, kc, :], w_uq.ap()[kc * 128:(kc + 1) * 128, :], q="pool")
    w_ukv_sb = fw.sb([128, 2, 2048], BF16, "w_ukv")
    for kc in range(2):
        fw.dma(w_ukv_sb[:, kc, :], w_ukv.ap()[kc * 128:(kc + 1) * 128, :], q="pool")
    uqv = w_uq_sb[:].rearrange("p k (h f) -> p k h f", f=192)
    w_uqrot = fw.sb([128, 3, 8, 64], BF16, "w_uqrot")
    fw.ts(w_uqrot[:, :, :, 0:32], uqv[:, :, :, 160:192], -1.0, ALU.mult)
    fw.copy(w_uqrot[:, :, :, 32:64], uqv[:, :, :, 128:160])
    TQ = 512
    knT = [fw.sb([128, NTOK], BF16, "knT%d" % i) for i in range(2)]
    vh = [fw.sb([128, NTOK // 128, 128], BF16, "vh%d" % i) for i in range(2)]
    qnT = [fw.sb([128, NQT], BF16, "qnT%d" % i) for i in range(2)]
    qrT = [fw.sb([64, NQT], BF16, "qrT%d" % i) for i in range(2)]
    pT = [fw.sb([128, TQ], BF16, "pT%d" % i) for i in range(3)]
    ccq = fw.sb([64, NQT], F32, "ccq")
    ssq = fw.sb([64, NQT], F32, "ssq")
    fw.dma(ccq[:, 0:NQ], ropeC.ap()[:, 0:NQ]); fw.dma(ccq[:, NQ:NQT], ropeC.ap()[:, NKV:NTOK])
    fw.dma(ssq[:, 0:NQ], ropeS.ap()[:, 0:NQ]); fw.dma(ssq[:, NQ:NQT], ropeS.ap()[:, NKV:NTOK])
    ra = fw.sb([64, TQ], F32, "ra2")
    rb = fw.sb([64, TQ], F32, "rb2")
    rden = fw.sb([128, TQ], F32, "rden")
    accD = fw.sb([128, TQ], F32, "accD")
    accP = fw.sb([128, TQ], F32, "accP")
    ones_f = fw.sb([128, 128], F32, "onesf")
    fw.memset(ones_f[:], 1.0)
    onorm = fw.sb([128, TQ], F32, "onorm")
    sc_att = float(192 ** -0.5)
    NKT = NTOK // 128
    ev = 0
    pcount = 0
    for h in range(8):
        kn, v_h, qn, qr = knT[h % 2], vh[h % 2], qnT[h % 2], qrT[h % 2]
        for c0 in range(0, NTOK, 512):
            T = min(512, NTOK - c0)
            ps = P[6 + (ev % 2)]; ev += 1
            for kc in range(2):
                fw.mm(ps[:, :T], w_ukv_sb[:, kc, h * 256:h * 256 + 128], ckvn[:, kc, c0:c0 + T], start=(kc == 0), stop=(kc == 1))
            fw.copy(kn[:, c0:c0 + T], ps[:, :T], eng=("act" if ev % 2 else "dve"))
        for t0 in range(0, NKT, 4):
            nt = min(4, NKT - t0)
            ps = P[6 + (ev % 2)]; ev += 1
            for t in range(nt):
                for kc in range(2):
                    fw.mm(ps[:, t * 128:(t + 1) * 128], ckvn[:, kc, (t0 + t) * 128:(t0 + t + 1) * 128],
                          w_ukv_sb[:, kc, h * 256 + 128:h * 256 + 256], start=(kc == 0), stop=(kc == 1))
            fw.copy(v_h[:, t0:t0 + nt, :], ps[:, :nt * 128].rearrange("p (t f) -> p t f", f=128), eng=("act" if ev % 2 else "dve"))
        for c0 in range(0, NQT, 512):
            T = min(512, NQT - c0)
            ps = P[6 + (ev % 2)]; ev += 1
            for kc in range(3):
                fw.mm(ps[:, :T], w_uq_sb[:, kc, h * 192:h * 192 + 128], cqn[:, kc, c0:c0 + T], start=(kc == 0), stop=(kc == 2))
            fw.copy(qn[:, c0:c0 + T], ps[:, :T], eng=("act" if ev % 2 else "dve"))
            ps = P[6 + (ev % 2)]; ev += 1
            for kc in range(3):
                fw.mm(ps[0:64, :T], w_uq_sb[:, kc, h * 192 + 128:h * 192 + 192], cqn[:, kc, c0:c0 + T], start=(kc == 0), stop=(kc == 2))
            fw.tt(ra[:, :T], ps[0:64, :T], ccq[:, c0:c0 + T], ALU.mult)
            ps = P[6 + (ev % 2)]; ev += 1
            for kc in range(3):
                fw.mm(ps[0:64, :T], w_uqrot[:, kc, h, :], cqn[:, kc, c0:c0 + T], start=(kc == 0), stop=(kc == 2))
            fw.tt(rb[:, :T], ps[0:64, :T], ssq[:, c0:c0 + T], ALU.mult)
            fw.tt(qr[:, c0:c0 + T], ra[:, :T], rb[:, :T], ALU.add)
        qchunks = [(c0, 512, list(range(NKT))) for c0 in range(0, NQ, 512)] + [(NQ, NCX, list(range(NKV // 128, NKT)))]
        for qi, (q0, T, ktiles) in enumerate(qchunks):
            po = P[2 + (pcount % 2)]
            pd = P[4 + (pcount % 2)]
            pcount += 1
            def s_mm(ki):
                kt = ktiles[ki]
                ps = P[ki % 2]
                fw.mm(ps[:, :T], kn[:, kt * 128:(kt + 1) * 128], qn[:, q0:q0 + T], start=True, stop=False)
                fw.mm(ps[:, :T], krT[:, kt * 128:(kt + 1) * 128], qr[:, q0:q0 + T], start=False, stop=True)
            s_mm(0)
            for ki, kt in enumerate(ktiles):
                if ki + 1 < len(ktiles):
                    s_mm(ki + 1)
                ps = P[ki % 2]
                p_t = pT[ki % 3]
                fw.act(p_t[:, :T], ps[:, :T], AF.Exp, scale=sc_att)
                fw.mm(po[:, :T], v_h[:, kt, :], p_t[:, :T], start=(ki == 0), stop=(ki == len(ktiles) - 1))
                if ki % 3 == 2:
                    if ki == 2:
                        fw.copy(accP[:, :T], p_t[:, :T], eng="pool")
                    else:
                        fw.tt(accP[:, :T], accP[:, :T], p_t[:, :T], ALU.add, eng="pool")
                else:
                    if ki == 0:
                        fw.copy(accD[:, :T], p_t[:, :T])
                    else:
                        fw.tt(accD[:, :T], accD[:, :T], p_t[:, :T], ALU.add)
            if len(ktiles) > 2:
                fw.tt(accD[:, :T], accD[:, :T], accP[:, :T], ALU.add)
            fw.mm(pd[:, :T], ones_f[:], accD[:, :T], start=True, stop=True)
            fw.recip(rden[:, :T], pd[:, :T])
            fw.tt(onorm[:, :T], po[:, :T], rden[:, :T], ALU.mult)
            fw.tt(sgate[:, h, q0:q0 + T], onorm[:, :T], sgate[:, h, q0:q0 + T], ALU.mult)
    fw.pop()

    fw.push()
    w_out_sb = fw.sb([128, NKC, D], BF16, "w_out")
    for kc in range(NKC):
        fw.dma(w_out_sb[:, kc, :], w_out.ap()[kc * 128:(kc + 1) * 128, :], q="pool")
    xs2 = [fw.sb([128, NKC, 512], F32, "xr%d" % i) for i in range(2)]
    x1v = x1T.ap().rearrange("(c p) n -> p c n", p=128)
    ochunks = [(c0, 512, c0, 0) for c0 in range(0, NQ, 512)] + [(NQ, NCX, NKV, 1)]
    for ci, (q0, T, xc0, r) in enumerate(ochunks):
        xs = xs2[ci % 2]
        fw.dma(xs[:, :, :T], xTv[:, :, xc0:xc0 + T])
        for fj in range(8):
            ps = P[fj % 4]
            for kc in range(NKC):
                fw.mm(ps[:, :T], w_out_sb[:, kc, fj * 128:(fj + 1) * 128], sgate[:, kc, q0:q0 + T], start=(kc == 0), stop=(kc == NKC - 1))
            fw.stt(xs[:, fj, :T], ps[:, :T], ada0[:, 16 + fj:17 + fj, r], xs[:, fj, :T], ALU.mult, ALU.add)
        fw.dma(x1v[:, :, q0:q0 + T], xs[:, :, :T])
    fw.pop()
    return fw


def host_inputs_A(inp, b, h):
    L = 4096
    own = slice(h * 2048, (h + 1) * 2048)
    oth = slice((1 - h) * 2048, (2 - h) * 2048)
    x = inp["x"][b]
    xT = np.concatenate([x[own].T, x[oth].T, inp["ctx"][b].T], axis=1)
    cos, sin = rope_tables(L)
    C = np.concatenate([cos[:, own], cos[:, oth], np.ones((64, 256), np.float32)], axis=1)
    S = np.concatenate([sin[:, own], sin[:, oth], np.zeros((64, 256), np.float32)], axis=1)
    return {
        "xT": np.ascontiguousarray(xT, np.float32),
        "cT": np.ascontiguousarray(np.stack([inp["c"][b], inp["c_ctx"]], axis=1), np.float32),
        "ropeC": np.ascontiguousarray(C), "ropeS": np.ascontiguousarray(S),
        "norm_g": np.ascontiguousarray(inp["norm_g"].reshape(4, 8, 128).transpose(0, 2, 1)[0:1]),
        "ada_w": np.ascontiguousarray(inp["ada_w"][0:1]),
        "ada_b": np.ascontiguousarray(inp["ada_b"].reshape(4, 24, 128).transpose(0, 2, 1)[0:1]),
        "mla_w_in": inp["mla_w_in"][0],
        "q_norm_g": np.ascontiguousarray(inp["mla_q_norm_g"][0].reshape(3, 128).T),
        "kv_norm_g": np.ascontiguousarray(inp["mla_kv_norm_g"][0].reshape(2, 128).T),
        "mla_w_uq": inp["mla_w_uq"][0], "mla_w_ukv": inp["mla_w_ukv"][0], "mla_w_out": inp["mla_w_out"][0],
    }


_ROPE = {}


def rope_tables(L):
    if L not in _ROPE:
        rows = L // 64
        row = np.repeat(np.arange(rows, dtype=np.float32), 64)
        col = np.tile(np.arange(64, dtype=np.float32), rows)
        inv = (np.float32(10000.0) ** (-np.arange(16, dtype=np.float32) / np.float32(16))).astype(np.float32)
        ang = np.concatenate([row[:, None] * inv, col[:, None] * inv], axis=-1)
        cos = np.cos(ang).astype(np.float32).T
        sin = np.sin(ang).astype(np.float32).T
        _ROPE[L] = (np.concatenate([cos, cos], 0), np.concatenate([sin, sin], 0))
    return _ROPE[L]


import numpy as np


def build_A2():
    fw = FW()
    NQ, NCX = 2048, 256
    WL = NQ + 2
    WC = NCX + 2
    x1T = fw.dram("x1T", [D, WL], F32, "ExternalInput")
    xc1T = fw.dram("xc1T", [D, NCX], F32, "ExternalInput")
    hmask = fw.dram("hmask", [128, 2], F32, "ExternalInput")
    cT = fw.dram("cT", [D, 2], F32, "ExternalInput")
    norm_g = fw.dram("norm_g", [1, 128, 8], F32, "ExternalInput")
    ada_w = fw.dram("ada_w", [1, D, 3 * D], F32, "ExternalInput")
    ada_b = fw.dram("ada_b", [1, 128, 24], F32, "ExternalInput")
    w_in = fw.dram("hy_w_in", [D, 4096], F32, "ExternalInput")
    conv_w = fw.dram("hy_conv_w", [3, 3072], F32, "ExternalInput")
    conv_b = fw.dram("hy_conv_b", [1, 3072], F32, "ExternalInput")
    U = fw.dram("U", [NQ, 4096], BF16, "ExternalOutput")
    Uc = fw.dram("Uc", [NCX, 4096], BF16, "ExternalOutput")

    P = [fw.ps("P%d" % i) for i in range(8)]
    ones_bf = fw.sb([128, 128], BF16, "ones")
    fw.memset(ones_bf[:], 1.0)
    eps_t = fw.sb([128, 1], F32, "eps")
    fw.memset(eps_t[:], EPS)
    ada = emit_ada(fw, P, cT, [1], ada_w, ada_b, norm_g)
    ada1, geff1 = ada[1]
    hm = fw.sb([128, 2], F32, "hm")
    fw.dma(hm[:], hmask.ap())

    aL = fw.sb([128, NKC, WL], BF16, "aL")
    aC = fw.sb([128, NKC, WC], BF16, "aC")
    fw.memset(aC[:, :, 0:1], 0.0)
    fw.memset(aC[:, :, WC - 1:WC], 0.0)
    fw.push()
    T1 = 512
    xs2 = [fw.sb([128, NKC, T1], F32, "xs%d" % i) for i in range(2)]
    ab = fw.sb([128, NKC, T1], BF16, "ab")
    sq = fw.sb([128, NKC, T1], BF16, "sq")
    rstd = fw.sb([128, T1], F32, "rstd")
    x1v = x1T.ap().rearrange("(c p) n -> p c n", p=128)
    xcv = xc1T.ap().rearrange("(c p) n -> p c n", p=128)
    jobs = [(x1v, c0, min(T1, WL - c0), aL, c0, 0) for c0 in range(0, WL, T1)] + [(xcv, 0, NCX, aC, 1, 1)]
    for ci, (src, c0, T, dst, d0, r) in enumerate(jobs):
        xs = xs2[ci % 2]
        fw.dma(xs[:, :, :T], src[:, :, c0:c0 + T])
        norm_mod(fw, P[0], xs, T, geff1[:, :, r], ada1[:, 0:8, r], dst[:, :, d0:d0 + T], sq, rstd, ones_bf, eps_t)
    fw.ts(aL[:, :, 0:1], aL[:, :, 0:1], hm[:, 0:1], ALU.mult)
    fw.ts(aL[:, :, WL - 1:WL], aL[:, :, WL - 1:WL], hm[:, 1:2], ALU.mult)
    fw.pop()
    fw.push()
    w_sb = fw.sb([128, NKC, 4096], BF16, "w_in")
    for kc in range(NKC):
        fw.dma(w_sb[:, kc, :], w_in.ap()[kc * 128:(kc + 1) * 128, :], q="pool")
    wt = [fw.sb([128, 3, NKC, 512], BF16, "wt%d" % i) for i in range(2)]
    cwb = [fw.sb([128, 3, 512], F32, "cwb%d" % i) for i in range(2)]
    cbb = [fw.sb([128, 512], F32, "cbb%d" % i) for i in range(2)]
    ost = [fw.sb([128, 512], BF16, "ost%d" % i) for i in range(3)]
    tiles = [(aL, t0, U, t0) for t0 in range(0, NQ, 128)] + [(aC, t0, Uc, t0) for t0 in range(0, NCX, 128)]
    n = 0
    for cc in range(8):
        c0 = cc * 512
        if cc < 6:
            w3 = wt[cc % 2]
            cw = cwb[cc % 2]
            cb = cbb[cc % 2]
            for j in range(3):
                fw.dma(cw[:, j, :], conv_w.ap()[j:j + 1, c0:c0 + 512].partition_broadcast(128))
            fw.dma(cb[:], conv_b.ap()[0:1, c0:c0 + 512].partition_broadcast(128))
            for j in range(3):
                for kc in range(NKC):
                    fw.tt(w3[:, j, kc, :], w_sb[:, kc, c0:c0 + 512], cw[:, j, :], ALU.mult,
                          eng=("pool" if (kc % 2) else "dve"))
        for (asrc, t0, dst, r0) in tiles:
            ps = P[1 + (n % 4)]
            o = ost[n % 3]
            n += 1
            if cc < 6:
                k = 0
                for j in range(3):
                    for kc in range(NKC):
                        fw.mm(ps[:], asrc[:, kc, t0 + j:t0 + j + 128], w3[:, j, kc, :], start=(k == 0), stop=(k == 23))
                        k += 1
                fw.tt(o[:], ps[:], cb[:], ALU.add)
            else:
                for kc in range(NKC):
                    fw.mm(ps[:], asrc[:, kc, t0 + 1:t0 + 129], w_sb[:, kc, c0:c0 + 512], start=(kc == 0), stop=(kc == NKC - 1))
                fw.copy(o[:], ps[:], eng="act")
            fw.dma(dst.ap()[r0:r0 + 128, c0:c0 + 512], o[:])
    fw.pop()
    return fw


def host_inputs_A2(inp, x1T_full, xc1T, b, h):
    s = h * 2048
    cols = np.zeros((D, 2050), np.float32)
    lo, hi = s - 1, s + 2049
    a, e = max(lo, 0), min(hi, 4096)
    cols[:, a - lo:e - lo] = x1T_full[:, a:e]
    hm = np.zeros((128, 2), np.float32)
    hm[:, 0] = 1.0 if lo >= 0 else 0.0
    hm[:, 1] = 1.0 if hi <= 4096 else 0.0
    return {
        "x1T": cols, "xc1T": np.ascontiguousarray(xc1T), "hmask": hm,
        "cT": np.ascontiguousarray(np.stack([inp["c"][b], inp["c_ctx"]], axis=1), np.float32),
        "norm_g": np.ascontiguousarray(inp["norm_g"].reshape(4, 8, 128).transpose(0, 2, 1)[1:2]),
        "ada_w": np.ascontiguousarray(inp["ada_w"][1:2]),
        "ada_b": np.ascontiguousarray(inp["ada_b"].reshape(4, 24, 128).transpose(0, 2, 1)[1:2]),
        "hy_w_in": inp["hy_w_in"][0], "hy_conv_w": inp["hy_conv_w"][0],
        "hy_conv_b": inp["hy_conv_b"][0].reshape(1, 3072),
    }


import math
import numpy as np

PI = float(np.pi)


def sin_act(fw, dst, arg, tmp):
    for _ in range(2):
        fw.ts(tmp, arg, PI, ALU.is_gt, -2.0 * PI, ALU.mult)
        fw.tt(arg, arg, tmp, ALU.add)
        fw.ts(tmp, arg, -PI, ALU.is_lt, 2.0 * PI, ALU.mult)
        fw.tt(arg, arg, tmp, ALU.add)
    fw.act(dst, arg, AF.Sin)


def gen_filter(fw, P, L, tabs, mlp, Kd, bias_sb):
    hdn_d, hdnrev_d, dec_d, decrev_d = tabs
    w1, w2, w3, fr, bfr, wo = mlp
    T = min(512, L)
    CH = min(2048, L)
    fw.push()
    Kx = fw.sb([128, 2, 2 * L], F32, "Kx")
    fw.memset(Kx[:, :, 2 * L - 1:2 * L], 0.0)
    h0 = fw.sb([33, L], F32, "h0")
    dc = fw.sb([128, L], F32, "dc")
    arg = fw.sb([64, L], F32, "arg")
    tmp = fw.sb([64, L], F32, "tmpf")
    hh = [fw.sb([64, L], F32, "hh%d" % i) for i in range(2)]
    kt = fw.sb([128, T], F32, "ktmp")
    n = 0
    for rev in (1, 0):
        for c0 in range(0, L, CH):
            fw.dma(h0[:, c0:c0 + CH], (hdnrev_d if rev else hdn_d).ap()[:, c0:c0 + CH])
            fw.dma(dc[:, c0:c0 + CH], (decrev_d if rev else dec_d).ap()[:, c0:c0 + CH], q="act")
        src = h0
        for k, w in enumerate((w1, w2, w3)):
            kin = 33 if k == 0 else 64
            dst = hh[k % 2]
            for t0 in range(0, L, T):
                ps = P[n % 2]; n += 1
                fw.mm(ps[0:64, :T], w[0:kin, :], src[0:kin, t0:t0 + T], start=True, stop=True)
                fw.ts(arg[:, t0:t0 + T], ps[0:64, :T], fr[:, k:k + 1], ALU.mult, bfr[:, k:k + 1], ALU.add)
            for c0 in range(0, L, CH):
                sin_act(fw, dst[:, c0:c0 + CH], arg[:, c0:c0 + CH], tmp[:, c0:c0 + CH])
            src = dst
        for o in range(2):
            for t0 in range(0, L, T):
                ps = P[2 + n % 2]; n += 1
                d = dc[:, t0:t0 + T]
                fw.mm(ps[:, :T], wo[:, o * 2 + rev, :], src[:, t0:t0 + T], start=True, stop=True)
                if rev:
                    fw.tt(Kx[:, o, t0:t0 + T], ps[:, :T], d, ALU.mult)
                else:
                    if t0 == 0:
                        fw.tt(kt[:], ps[:, :T], d, ALU.mult)
                        fw.copy(Kx[:, o, L:L + T - 1], kt[:, 1:T])
                        fw.tt(Kx[:, o, L - 1:L], Kx[:, o, L - 1:L], kt[:, 0:1], ALU.add)
                        fw.tt(Kx[:, o, L - 1:L], Kx[:, o, L - 1:L], bias_sb[:, o:o + 1], ALU.add)
                    else:
                        fw.tt(Kx[:, o, L - 1 + t0:L - 1 + t0 + T], ps[:, :T], d, ALU.mult)
    kb = [fw.sb([128, 2048], BF16, "Kb%d" % i) for i in range(2)]
    k = 0
    for o in range(2):
        for m0 in range(0, 2 * L, 2048):
            m1 = min(2 * L, m0 + 2048)
            b = kb[k % 2]
            fw.copy(b[:, :m1 - m0], Kx[:, o, m0:m1], eng=("act" if k % 2 else "dve"))
            fw.dma(Kd.ap()[:, o, m0:m1], b[:, :m1 - m0], q=("sp" if k % 2 else "act"))
            k += 1
    fw.pop()


def long_conv_stage(fw, P, NB, L, Kd, Uv, J_bf, ident_bf, Pbf, ZgT, name):
    NBI = 4 * NB
    ND = 2 * NB - 1
    fw.push()
    znat = fw.sb([128, NBI, 128], BF16, name + "znat")
    zr = fw.sb([128, 128, NBI], BF16, name + "zr")
    gA = fw.sb([128, NBI, 128], BF16, name + "gA")
    gB = fw.sb([128, NBI, 128], BF16, name + "gB")
    batched = (ND * 128 * 128 * 2) <= 100 * 1024
    if batched:
        Tall = fw.sb([128, 128, ND * 128], BF16, name + "Tall")
    else:
        Tt = [fw.sb([128, ND * 128], BF16, name + "T%d" % i) for i in range(4)]
    ost = [fw.sb([128, 512], F32, name + "ost%d" % i) for i in range(2)]
    def load_group(dst, g):
        step = max(1, NBI // 2)
        for k, n0 in enumerate(range(0, NBI, step)):
            fw.dma(dst[:, n0:n0 + step, :], Uv.ap()[:, g, n0:n0 + step, :], q=("sp" if k % 2 == 0 else "act"))

    def reverse_rows():
        cpb = 512 // 128
        k = 0
        for n0 in range(0, NBI, cpb):
            nn = min(cpb, NBI - n0)
            ps = P[4 + (k % 2)]
            fw.mm(ps[:, :nn * 128], J_bf[:], znat[:, n0:n0 + nn, :], start=True, stop=True)
            fw.copy(zr[:, :, n0:n0 + nn], ps[:, :nn * 128].rearrange("p (n c) -> p c n", c=128),
                    eng=("act" if k % 2 else "dve"))
            k += 1

    load_group(znat, 0)
    load_group(gA, 1)
    reverse_rows()
    dorder = [0] + [d for k in range(1, NB) for d in (k, -k)]
    ti = 0
    for o in range(2):
        if o == 1:
            load_group(gA, 2)
            load_group(gB, 3)
            for n0 in range(0, NBI, 16):
                n1 = min(NBI, n0 + 16)
                fw.act(gB[:, n0:n1, :], gB[:, n0:n1, :], AF.Silu)
                fw.tt(gA[:, n0:n1, :], gA[:, n0:n1, :], gB[:, n0:n1, :], ALU.mult)
            reverse_rows()
        if batched:
            for cg in range(0, 128, 32):
                src = bass.AP(Kd, (cg * 2 + o) * 2 * L, [[1, 128], [2 * 2 * L, 32], [1, ND * 128]])
                fw.dma(Tall[:, cg:cg + 32, :], src, q=("sp" if (cg // 32) % 2 == 0 else "act"))
        for c0 in range(0, 128, 4):
            ps = P[2 + ((c0 // 4) % 2)]
            for s in range(4):
                c = c0 + s
                if batched:
                    T = Tall[:, c, :]
                else:
                    T = Tt[ti % 4]
                    src = bass.AP(Kd, (c * 2 + o) * 2 * L, [[1, 128], [1, ND * 128]])
                    fw.dma(T[:], src, q=("sp" if ti % 2 == 0 else "act"))
                    ti += 1
                outv = ps[:, s * NBI:(s + 1) * NBI].rearrange("p (b i) -> p b i", b=4)
                inv = zr[:, c, :].rearrange("p (b i) -> p b i", b=4)
                for di, d in enumerate(dorder):
                    lo = max(0, d)
                    n = NB - abs(d)
                    fw.mm(outv[:, :, lo:lo + n], T[:, (d + NB - 1) * 128:(d + NB) * 128], inv[:, :, lo - d:lo - d + n],
                          start=(di == 0), stop=(di == ND - 1))
            pv = ps[:, 0:4 * NBI].rearrange("p (s n) -> p n s", s=4)
            fw.tt(znat[:, :, c0:c0 + 4], pv, gA[:, :, c0:c0 + 4], ALU.mult)
    k = 0
    for n0 in range(0, NBI, 4):
        nn = min(4, NBI - n0)
        for j in range(nn):
            fw.transpose(Pbf[:, j * 128:(j + 1) * 128], znat[:, n0 + j, :], ident_bf[:])
        o = ost[k % 2]
        k += 1
        fw.copy(o[:, :nn * 128], Pbf[:, :nn * 128])
        fw.dma(ZgT.ap()[:, n0 * 128:(n0 + nn) * 128], o[:, :nn * 128])
    fw.pop()


def build_B(stages=("fc", "fl", "cc", "cl")):
    fw = FW()
    L, LC = 4096, 256
    Uv = fw.dram("Uv", [128, 4, 4 * (L // 128), 128], BF16, "ExternalInput")
    Ucv = fw.dram("Ucv", [128, 4, 4 * (LC // 128), 128], BF16, "ExternalInput")
    f_w1 = fw.dram("f_w1", [33, 64], F32, "ExternalInput")
    f_w2 = fw.dram("f_w2", [64, 64], F32, "ExternalInput")
    f_w3 = fw.dram("f_w3", [64, 64], F32, "ExternalInput")
    f_b = fw.dram("f_b", [64, 3], F32, "ExternalInput")
    f_fr = fw.dram("f_fr", [64, 3], F32, "ExternalInput")
    f_wo = fw.dram("f_wo", [64, 4, 128], F32, "ExternalInput")
    hy_bias = fw.dram("hy_bias", [128, 2], F32, "ExternalInput")
    tabsL = [fw.dram(n, s, F32, "ExternalInput") for n, s in
             (("hdnL", [33, L]), ("hdnLr", [33, L]), ("decL", [128, L]), ("decLr", [128, L]))]
    tabsC = [fw.dram(n, s, F32, "ExternalInput") for n, s in
             (("hdnC", [33, LC]), ("hdnCr", [33, LC]), ("decC", [128, LC]), ("decCr", [128, LC]))]
    Jd = fw.dram("Jmat", [128, 128], F32, "ExternalInput")
    Id = fw.dram("Imat", [128, 128], F32, "ExternalInput")
    ZgT = fw.dram("ZgT", [128, 4 * L], F32, "ExternalOutput")
    ZgcT = fw.dram("ZgcT", [128, 4 * LC], F32, "ExternalOutput")
    KdL = fw.dram("KdL", [128, 2, 2 * L], BF16, "ExternalOutput")
    KdC = fw.dram("KdC", [128, 2, 2 * LC], BF16, "ExternalOutput")

    P = [fw.ps("P%d" % i) for i in range(6)]
    Pbf = fw.ps("Pbf", (128, 1024), BF16)
    J_bf = fw.sb([128, 128], BF16, "J")
    I_bf = fw.sb([128, 128], BF16, "I")
    fw.dma(J_bf[:], Jd.ap(), q="pool")
    fw.dma(I_bf[:], Id.ap(), q="pool")
    w1 = fw.sb([33, 64], F32, "w1"); fw.dma(w1[:], f_w1.ap())
    w2 = fw.sb([64, 64], F32, "w2"); fw.dma(w2[:], f_w2.ap())
    w3 = fw.sb([64, 64], F32, "w3"); fw.dma(w3[:], f_w3.ap())
    fb = fw.sb([64, 3], F32, "fb"); fw.dma(fb[:], f_b.ap())
    fr = fw.sb([64, 3], F32, "fr"); fw.dma(fr[:], f_fr.ap())
    wo = fw.sb([64, 4, 128], F32, "wo"); fw.dma(wo[:], f_wo.ap())
    bias_sb = fw.sb([128, 2], F32, "hbias"); fw.dma(bias_sb[:], hy_bias.ap())
    bfr = fw.sb([64, 3], F32, "bfr")
    fw.tt(bfr[:], fb[:], fr[:], ALU.mult)
    mlp = (w1, w2, w3, fr, bfr, wo)
    if "fc" in stages:
        gen_filter(fw, P, LC, tabsC, mlp, KdC, bias_sb)
    if "fl" in stages:
        gen_filter(fw, P, L, tabsL, mlp, KdL, bias_sb)
    if "cc" in stages:
        long_conv_stage(fw, P, LC // 128, LC, KdC, Ucv, J_bf, I_bf, Pbf, ZgcT, "c_")
    if "cl" in stages:
        long_conv_stage(fw, P, L // 128, L, KdL, Uv, J_bf, I_bf, Pbf, ZgT, "l_")
    return fw


_TAB = {}


def hyena_tables(Lx):
    if Lx not in _TAB:
        f32 = np.float32
        t = np.linspace(0.0, 1.0, Lx, dtype=f32)[:, None]
        wpos = (f32(2.0 * math.pi / Lx)) * np.arange(Lx, dtype=f32)[:, None]
        bands = np.linspace(1e-4, 15, 16, dtype=f32)[None, :]
        hdn = np.concatenate([t, np.cos(bands * wpos), -np.sin(bands * wpos)], axis=-1).astype(f32).T
        deltas = np.abs(np.linspace(math.log(1e-2) / 0.3, math.log(1e-2) / 1.5, 1024, dtype=f32))
        dec = np.exp(-t * deltas).astype(f32).T
        _TAB[Lx] = (np.ascontiguousarray(hdn), np.ascontiguousarray(hdn[:, ::-1]), dec)
    return _TAB[Lx]


def host_inputs_B(inp, U_all, Uc_all, cb):
    cs = slice(cb * 128, (cb + 1) * 128)
    Uv = U_all.reshape(4, 32, 128, 4, 1024)[..., cs].transpose(2, 3, 0, 1, 4).reshape(128, 4, 128, 128)
    Ucv = Uc_all.reshape(4, 2, 128, 4, 1024)[..., cs].transpose(2, 3, 0, 1, 4).reshape(128, 4, 8, 128)
    hL, hLr, dL = hyena_tables(4096)
    hC, hCr, dC = hyena_tables(256)
    eye = np.eye(128, dtype=np.float32)
    return {
        "Uv": np.ascontiguousarray(Uv), "Ucv": np.ascontiguousarray(Ucv),
        "f_w1": inp["hy_filt_w_in"][0], "f_w2": inp["hy_filt_w_hid"][0][0], "f_w3": inp["hy_filt_w_hid"][0][1],
        "f_b": np.ascontiguousarray(inp["hy_filt_b"][0].T), "f_fr": np.ascontiguousarray(inp["hy_filt_freq"][0].T),
        "f_wo": np.ascontiguousarray(inp["hy_filt_w_out"][0].reshape(64, 4, 1024)[:, :, cs]),
        "hy_bias": np.ascontiguousarray(inp["hy_bias"][0][:, cs].T),
        "hdnL": hL, "hdnLr": hLr, "decL": np.ascontiguousarray(dL[cs]), "decLr": np.ascontiguousarray(dL[cs][:, ::-1]),
        "hdnC": hC, "hdnCr": hCr, "decC": np.ascontiguousarray(dC[cs]), "decCr": np.ascontiguousarray(dC[cs][:, ::-1]),
        "Jmat": np.ascontiguousarray(eye[::-1]), "Imat": eye,
    }


import numpy as np

W = 2432
NCX = 256
WT = W + NCX
Q0, Q1 = 128, 2304
QW = Q1 - Q0
O0, O1 = 192, 2240
U0 = O0 - 15
NU = 2048 + 30


def build_C(debug=False, stop=None):
    fw = FW()
    kindo = "ExternalOutput"
    x1T = fw.dram("x1T", [D, W], F32, "ExternalInput")
    zgT = fw.dram("zgT", [D, W], F32, "ExternalInput")
    xc1T = fw.dram("xc1T", [D, NCX], F32, "ExternalInput")
    zgcT = fw.dram("zgcT", [D, NCX], F32, "ExternalInput")
    cT = fw.dram("cT", [D, 2], F32, "ExternalInput")
    norm_g = fw.dram("norm_g", [3, 128, 8], F32, "ExternalInput")
    ada_w = fw.dram("ada_w", [3, D, 3 * D], F32, "ExternalInput")
    ada_b = fw.dram("ada_b", [3, 128, 24], F32, "ExternalInput")
    hy_w_out = fw.dram("hy_w_out", [D, D], F32, "ExternalInput")
    swa_w_in = fw.dram("swa_w_in", [D, 2560], F32, "ExternalInput")
    swa_sink = fw.dram("swa_sink", [1, 16], F32, "ExternalInput")
    swa_w_out = fw.dram("swa_w_out", [D, D], F32, "ExternalInput")
    cf_w_in = fw.dram("cf_w_in", [D, 3072], F32, "ExternalInput")
    cf_dw_w = fw.dram("cf_dw_w", [128, 8, 31], F32, "ExternalInput")
    cf_vec = fw.dram("cf_vec", [128, 4, 8], F32, "ExternalInput")
    cf_w_out = fw.dram("cf_w_out", [D, D], F32, "ExternalInput")
    ropeC = fw.dram("ropeC", [128, WT], F32, "ExternalInput")
    ropeS = fw.dram("ropeS", [128, WT], F32, "ExternalInput")
    kvalid_d = fw.dram("kvalid", [128, 19], F32, "ExternalInput")
    cmask_d = fw.dram("cmask", [128, NU], F32, "ExternalInput")
    tri_d = fw.dram("negm", [128, 2, 128], F32, "ExternalInput")
    ident_d = fw.dram("ident", [128, 128], F32, "ExternalInput")
    outT = fw.dram("outT", [D, 2048], F32, "ExternalOutput")
    X2 = fw.dram("X2", [D, W], F32, kindo if debug else "Internal")
    X3 = fw.dram("X3", [D, QW], F32, kindo if debug else "Internal")

    P = [fw.ps("P%d" % i) for i in range(8)]
    ones_bf = fw.sb([128, 128], BF16, "ones")
    fw.memset(ones_bf[:], 1.0)
    ones_f = fw.sb([128, 128], F32, "onesf")
    fw.memset(ones_f[:], 1.0)
    eps_t = fw.sb([128, 1], F32, "eps")
    fw.memset(eps_t[:], EPS)
    ada = emit_ada(fw, P, cT, [1, 2, 3], ada_w, ada_b, norm_g)
    (ada1, geff1), (ada2, geff2), (ada3, geff3) = ada[1], ada[2], ada[3]
    cfv = fw.sb([128, 4, 8], F32, "cfv")
    fw.dma(cfv[:], cf_vec.ap())
    x1v = x1T.ap().rearrange("(c p) n -> p c n", p=128)
    zgv = zgT.ap().rearrange("(c p) n -> p c n", p=128)
    xcv = xc1T.ap().rearrange("(c p) n -> p c n", p=128)
    zgcv = zgcT.ap().rearrange("(c p) n -> p c n", p=128)
    X2v = X2.ap().rearrange("(c p) n -> p c n", p=128)
    X3v = X3.ap().rearrange("(c p) n -> p c n", p=128)
    outv = outT.ap().rearrange("(c p) n -> p c n", p=128)

    og = fw.sb([128, NKC, QW], BF16, "og")
    a3 = og
    fw.push()
    a2 = fw.sb([128, NKC, WT], BF16, "a2")

    fw.push()
    wo1 = fw.sb([128, NKC, D], BF16, "wo1")
    for kc in range(NKC):
        fw.dma(wo1[:, kc, :], hy_w_out.ap()[kc * 128:(kc + 1) * 128, :], q="pool")
    xs2 = [fw.sb([128, NKC, 512], F32, "xs%d" % i) for i in range(2)]
    zg2 = [fw.sb([128, NKC, 512], BF16, "zg%d" % i) for i in range(2)]
    sq = fw.sb([128, NKC, 512], BF16, "sq")
    rstd = fw.sb([128, 512], F32, "rstd")
    jobs = [(x1v, zgv, c0, min(512, W - c0), c0, 0, True) for c0 in range(0, W, 512)] + [(xcv, zgcv, 0, NCX, W, 1, False)]
    for ci, (xsrc, zsrc, c0, T, d0, r, store) in enumerate(jobs):
        xs, zg = xs2[ci % 2], zg2[ci % 2]
        fw.dma(xs[:, :, :T], xsrc[:, :, c0:c0 + T])
        fw.dma(zg[:, :, :T], zsrc[:, :, c0:c0 + T], q="pool")
        for fj in range(8):
            ps = P[2 + fj % 4]
            for kc in range(NKC):
                fw.mm(ps[:, :T], wo1[:, kc, fj * 128:(fj + 1) * 128], zg[:, kc, :T], start=(kc == 0), stop=(kc == NKC - 1))
            fw.stt(xs[:, fj, :T], ps[:, :T], ada1[:, 16 + fj:17 + fj, r], xs[:, fj, :T], ALU.mult, ALU.add)
        if store:
            fw.dma(X2v[:, :, c0:c0 + T], xs[:, :, :T])
        norm_mod(fw, P[0], xs, T, geff2[:, :, r], ada2[:, 0:8, r], a2[:, :, d0:d0 + T], sq, rstd, ones_bf, eps_t)
    fw.pop()

    if stop == "c1":
        return fw
    fw.push()
    CC = fw.sb([128, WT], F32, "CC")
    SS = fw.sb([128, WT], F32, "SS")
    fw.dma(CC[:], ropeC.ap())
    fw.dma(SS[:], ropeS.ap())
    kval = fw.sb([128, 19], F32, "kval")
    fw.dma(kval[:], kvalid_d.ap())
    negm = fw.sb([128, 2, 128], BF16, "negm")
    fw.dma(negm[:], tri_d.ap(), q="pool")
    ident_bf = fw.sb([128, 128], BF16, "ident")
    fw.dma(ident_bf[:], ident_d.ap(), q="pool")
    kvm = fw.sb([128, 19, 128], BF16, "kvm")
    for t in range(19):
        fw.ts(kvm[:, t, :], ones_bf[:], kval[:, t:t + 1], ALU.mult)
    esink = fw.sb([128, 16], F32, "esink")
    fw.dma(esink[:], swa_sink.ap().partition_broadcast(128))
    fw.act(esink[:], esink[:], AF.Exp)
    k2 = [fw.sb([128, WT], BF16, "k2_%d" % i) for i in range(4)]
    NT = WT // 128
    Vd = fw.sb([128, NT, 512], BF16, "Vd")
    fw.push()
    wkv = fw.sb([128, NKC, 512], BF16, "wkv")
    for kc in range(NKC):
        fw.dma(wkv[:, kc, :], swa_w_in.ap()[kc * 128:(kc + 1) * 128, 1024:1536], q="pool")
    wk_d = fw.sb([128, NKC, 4, 2, 64], BF16, "wk_d")
    wkr_d = fw.sb([128, NKC, 4, 2, 64], BF16, "wkr_d")
    wv_d = fw.sb([128, NKC, 4, 2, 64], BF16, "wv_d")
    kview = wkv[:, :, 0:256].rearrange("p k (h f) -> p k h f", f=64)
    vview = wkv[:, :, 256:512].rearrange("p k (h f) -> p k h f", f=64)
    for dup in range(2):
        fw.copy(wk_d[:, :, :, dup, :], kview)
        fw.copy(wv_d[:, :, :, dup, :], vview, eng="act")
        fw.ts(wkr_d[:, :, :, dup, 0:32], kview[:, :, :, 32:64], -1.0, ALU.mult)
        fw.copy(wkr_d[:, :, :, dup, 32:64], kview[:, :, :, 0:32], eng="act")
    ra = fw.sb([128, 512], F32, "ra")
    rb = fw.sb([128, 512], F32, "rb")
    ev = 0
    for hk in range(4):
        for c0 in range(0, WT, 512):
            T = min(512, WT - c0)
            ps = P[2 + ev % 2]; ev += 1
            for kc in range(NKC):
                fw.mm(ps[:, :T], wk_d[:, kc, hk].rearrange("p a f -> p (a f)"), a2[:, kc, c0:c0 + T], start=(kc == 0), stop=(kc == NKC - 1))
            fw.tt(ra[:, :T], ps[:, :T], CC[:, c0:c0 + T], ALU.mult)
            ps = P[2 + ev % 2]; ev += 1
            for kc in range(NKC):
                fw.mm(ps[:, :T], wkr_d[:, kc, hk].rearrange("p a f -> p (a f)"), a2[:, kc, c0:c0 + T], start=(kc == 0), stop=(kc == NKC - 1))
            fw.tt(rb[:, :T], ps[:, :T], SS[:, c0:c0 + T], ALU.mult)
            fw.tt(k2[hk][:, c0:c0 + T], ra[:, :T], rb[:, :T], ALU.add)
    for t in range(NT):
        ps = P[2 + ev % 2]; ev += 1
        for kc in range(NKC):
            fw.mm(ps[:], a2[:, kc, t * 128:(t + 1) * 128], wv_d[:, kc].rearrange("p h a f -> p (h a f)"), start=(kc == 0), stop=(kc == NKC - 1))
        if t < 19:
            fw.ts(Vd[:, t, :], ps[:], kval[:, t:t + 1], ALU.mult)
        else:
            fw.copy(Vd[:, t, :], ps[:], eng="act")
    fw.pop()
    if stop == "c2a":
        return fw
    ra = fw.sb([128, 512], F32, "ra_g")
    rb = fw.sb([128, 512], F32, "rb_g")
    wq = fw.sb([128, NKC, 256], BF16, "wq")
    wqr = fw.sb([128, NKC, 4, 2, 32], BF16, "wqr")
    wg = fw.sb([128, NKC, 256], BF16, "wg")
    qz = fw.sb([128, 4, QW], BF16, "qz")
    for g in range(4):
        fw.memset(qz[:, g, :], 0.0)
    sg = fw.sb([128, 2, QW], BF16, "sg")
    pT = [fw.sb([128, 512], BF16, "pT%d" % i) for i in range(3)]
    den = fw.sb([128, 512], F32, "den")
    onrm = fw.sb([128, 512], F32, "onrm")
    sc_att = 0.125
    pc = 0
    for hk in range(4):
        for kc in range(NKC):
            fw.dma(wq[:, kc, :], swa_w_in.ap()[kc * 128:(kc + 1) * 128, hk * 256:(hk + 1) * 256], q="pool")
            fw.dma(wg[:, kc, :], swa_w_in.ap()[kc * 128:(kc + 1) * 128, 1536 + hk * 256:1536 + (hk + 1) * 256], q="pool")
        wqv = wq[:].rearrange("p k (h a f) -> p k h a f", a=2, f=32)
        fw.ts(wqr[:, :, :, 0, :], wqv[:, :, :, 1, :], -1.0, ALU.mult)
        fw.copy(wqr[:, :, :, 1, :], wqv[:, :, :, 0, :])
        wqr_f = wqr[:].rearrange("p k h a f -> p k (h a f)")
        for cc in range(2):
            for c0 in range(0, QW, 512):
                T = min(512, QW - c0)
                ps = P[2 + ev % 2]; ev += 1
                for kc in range(NKC):
                    fw.mm(ps[:, :T], wq[:, kc, cc * 128:(cc + 1) * 128], a2[:, kc, Q0 + c0:Q0 + c0 + T], start=(kc == 0), stop=(kc == NKC - 1))
                fw.tt(ra[:, :T], ps[:, :T], CC[:, Q0 + c0:Q0 + c0 + T], ALU.mult)
                ps = P[2 + ev % 2]; ev += 1
                for kc in range(NKC):
                    fw.mm(ps[:, :T], wqr_f[:, kc, cc * 128:(cc + 1) * 128], a2[:, kc, Q0 + c0:Q0 + c0 + T], start=(kc == 0), stop=(kc == NKC - 1))
                fw.tt(rb[:, :T], ps[:, :T], SS[:, Q0 + c0:Q0 + c0 + T], ALU.mult)
                for half in range(2):
                    hr = slice(half * 64, (half + 1) * 64)
                    fw.tt(qz[hr, cc * 2 + half, c0:c0 + T], ra[hr, :T], rb[hr, :T], ALU.add)
                ps = P[2 + ev % 2]; ev += 1
                for kc in range(NKC):
                    fw.mm(ps[:, :T], wg[:, kc, cc * 128:(cc + 1) * 128], a2[:, kc, Q0 + c0:Q0 + c0 + T], start=(kc == 0), stop=(kc == NKC - 1))
                fw.act(sg[:, cc, c0:c0 + T], ps[:, :T], AF.Silu)
        for j in range(1, 18):
            qc = (j - 1) * 128
            po = P[4 + pc % 2]
            pd = P[6 + pc % 2]
            pc += 1
            ktl = [(j - 1, 0), (j, None), (j + 1, 1), (19, -1), (20, -1)]
            def s_mm(ki):
                kt, tsel = ktl[ki]
                ps = P[ki % 2]
                masked = tsel in (0, 1)
                for g in range(4):
                    fw.mm(ps[:, g * 128:(g + 1) * 128], k2[hk][:, kt * 128:(kt + 1) * 128],
                          qz[:, g, qc:qc + 128], start=True, stop=not masked)
                    if masked:
                        fw.mm(ps[:, g * 128:(g + 1) * 128], ident_bf[:], negm[:, tsel, :], start=False, stop=True)
            s_mm(0)
            for ki, (kt, tsel) in enumerate(ktl):
                if ki + 1 < len(ktl):
                    s_mm(ki + 1)
                ps = P[ki % 2]
                p_t = pT[ki % 3]
                fw.act(p_t[:], ps[:], AF.Exp, scale=sc_att)
                p_m = p_t
                fw.mm(po[:], Vd[:, kt, hk * 128:(hk + 1) * 128], p_m[:], start=(ki == 0), stop=(ki == 4))
                fw.mm(pd[:], (ones_bf[:] if tsel == -1 else kvm[:, kt, :]), p_m[:], start=(ki == 0), stop=(ki == 4))
            for g in range(4):
                h = hk * 4 + g
                fw.ts(den[:, g * 128:(g + 1) * 128], pd[:, g * 128:(g + 1) * 128], esink[:, h:h + 1], ALU.add)
            fw.recip(den[:], den[:])
            fw.tt(onrm[:], po[:], den[:], ALU.mult)
            for g in range(4):
                hf = (g % 2) * 64
                cch = g // 2
                fw.tt(og[hf:hf + 64, hk * 2 + cch, qc:qc + 128], onrm[hf:hf + 64, g * 128:(g + 1) * 128], sg[hf:hf + 64, cch, qc:qc + 128], ALU.mult)
    fw.pop()
    fw.pop()
    if stop == "c2":
        return fw

    fw.push()
    wo2 = fw.sb([128, NKC, D], BF16, "wo2")
    for kc in range(NKC):
        fw.dma(wo2[:, kc, :], swa_w_out.ap()[kc * 128:(kc + 1) * 128, :], q="pool")
    xs2 = [fw.sb([128, NKC, 512], F32, "xs3_%d" % i) for i in range(2)]
    sq = fw.sb([128, NKC, 512], BF16, "sq3")
    rstd = fw.sb([128, 512], F32, "rstd3")
    for ci, c0 in enumerate(range(0, QW, 512)):
        T = min(512, QW - c0)
        xs = xs2[ci % 2]
        fw.dma(xs[:, :, :T], X2v[:, :, Q0 + c0:Q0 + c0 + T])
        for fj in range(8):
            ps = P[2 + fj % 4]
            for kc in range(NKC):
                fw.mm(ps[:, :T], wo2[:, kc, fj * 128:(fj + 1) * 128], og[:, kc, c0:c0 + T], start=(kc == 0), stop=(kc == NKC - 1))
            fw.stt(xs[:, fj, :T], ps[:, :T], ada2[:, 16 + fj:17 + fj, 0], xs[:, fj, :T], ALU.mult, ALU.add)
        fw.dma(X3v[:, :, c0:c0 + T], xs[:, :, :T])
        norm_mod(fw, P[0], xs, T, geff3[:, :, 0], ada3[:, 0:8, 0], a3[:, :, c0:c0 + T], sq, rstd, ones_bf, eps_t)
    fw.pop()

    if stop == "c3":
        return fw
    fw.push()
    UB = U0 - Q0
    OB = O0 - Q0
    ug = fw.sb([128, NKC, 2048], BF16, "ug")
    fw.push()
    uc = fw.sb([128, NKC, 2048], F32, "uc")
    fw.push()
    cm = fw.sb([128, NU], F32, "cmask")
    fw.dma(cm[:], cmask_d.ap())
    dww = fw.sb([128, 8, 31], F32, "dww")
    fw.dma(dww[:], cf_dw_w.ap())
    wab = [fw.sb([128, NKC, 2, 128], BF16, "wab%d" % i) for i in range(2)]
    usb = [fw.sb([128, NU + 2], BF16, "usb%d" % i) for i in range(2)]
    dgs = [fw.sb([128, 31, 128], BF16, "dg%d" % i) for i in range(2)]
    identc = fw.sb([128, 128], BF16, "identc")
    fw.dma(identc[:], ident_d.ap(), q="pool")
    sig = fw.sb([128, 512], F32, "sig")
    for fj in range(8):
        w2 = wab[fj % 2]
        u = usb[fj % 2]
        for kc in range(NKC):
            fw.dma(w2[:, kc, 0, :], cf_w_in.ap()[kc * 128:(kc + 1) * 128, fj * 128:(fj + 1) * 128], q="pool")
            fw.dma(w2[:, kc, 1, :], cf_w_in.ap()[kc * 128:(kc + 1) * 128, 1024 + fj * 128:1024 + (fj + 1) * 128], q="pool")
        for c0 in range(0, NU, 512):
            T = min(512, NU - c0)
            pa, pb = P[2 + (c0 // 512) % 2], P[4 + (c0 // 512) % 2]
            for kc in range(NKC):
                fw.mm(pa[:, :T], w2[:, kc, 0, :], a3[:, kc, UB + c0:UB + c0 + T], start=(kc == 0), stop=(kc == NKC - 1))
            for kc in range(NKC):
                fw.mm(pb[:, :T], w2[:, kc, 1, :], a3[:, kc, UB + c0:UB + c0 + T], start=(kc == 0), stop=(kc == NKC - 1))
            fw.act(sig[:, :T], pb[:, :T], AF.Sigmoid)
            fw.tt(sig[:, :T], sig[:, :T], cm[:, c0:c0 + T], ALU.mult)
            fw.tt(u[:, c0:c0 + T], pa[:, :T], sig[:, :T], ALU.mult)
        dg = dgs[fj % 2]
        for j in range(31):
            fw.ts(dg[:, j, :], identc[:], dww[:, fj, j:j + 1], ALU.mult)
        for c0 in range(0, 2048, 512):
            pc4 = P[6 + (c0 // 512) % 2]
            for j in range(31):
                fw.mm(pc4[:], dg[:, j, :], u[:, c0 + j:c0 + j + 512], start=(j == 0), stop=(j == 30))
            fw.act(uc[:, fj, c0:c0 + 512], pc4[:], AF.Identity, bias=cfv[:, 0, fj:fj + 1], scale=1.0)
    fw.pop()
    fw.push()
    mean = fw.sb([128, 2048], F32, "mean")
    rs = fw.sb([128, 2048], F32, "rs")
    sqf = fw.sb([128, 512], F32, "sqf")
    for c0 in range(0, 2048, 512):
        for fj in range(8):
            fw.mm(P[2][:], ones_f[:], uc[:, fj, c0:c0 + 512], start=(fj == 0), stop=(fj == 7))
        for fj in range(8):
            fw.tt(sqf[:], uc[:, fj, c0:c0 + 512], uc[:, fj, c0:c0 + 512], ALU.mult)
            fw.mm(P[3][:], ones_f[:], sqf[:], start=(fj == 0), stop=(fj == 7))
        fw.ts(mean[:, c0:c0 + 512], P[2][:], 1.0 / D, ALU.mult)
        fw.tt(sqf[:], mean[:, c0:c0 + 512], mean[:, c0:c0 + 512], ALU.mult)
        fw.stt(sqf[:], P[3][:], 1.0 / D, sqf[:], ALU.mult, ALU.subtract)
        fw.act(rs[:, c0:c0 + 512], sqf[:], AF.Sqrt, bias=eps_t[:, 0:1], scale=1.0)
        fw.recip(rs[:, c0:c0 + 512], rs[:, c0:c0 + 512])
    wgt = [fw.sb([128, NKC, 128], BF16, "wgt%d" % i) for i in range(2)]
    sg3 = fw.sb([128, 512], F32, "sg3")
    for fj in range(8):
        w2 = wgt[fj % 2]
        for kc in range(NKC):
            fw.dma(w2[:, kc, :], cf_w_in.ap()[kc * 128:(kc + 1) * 128, 2048 + fj * 128:2048 + (fj + 1) * 128], q="pool")
        for c0 in range(0, 2048, 512):
            ps = P[4 + (c0 // 512) % 2]
            for kc in range(NKC):
                fw.mm(ps[:], w2[:, kc, :], a3[:, kc, OB + c0:OB + c0 + 512], start=(kc == 0), stop=(kc == NKC - 1))
            fw.act(sg3[:], ps[:], AF.Silu)
            fw.tt(uc[:, fj, c0:c0 + 512], uc[:, fj, c0:c0 + 512], mean[:, c0:c0 + 512], ALU.subtract)
            fw.tt(uc[:, fj, c0:c0 + 512], uc[:, fj, c0:c0 + 512], rs[:, c0:c0 + 512], ALU.mult)
            fw.act(uc[:, fj, c0:c0 + 512], uc[:, fj, c0:c0 + 512], AF.Silu, bias=cfv[:, 2, fj:fj + 1], scale=cfv[:, 1, fj:fj + 1])
            fw.tt(ug[:, fj, c0:c0 + 512], uc[:, fj, c0:c0 + 512], sg3[:], ALU.mult)
    fw.pop()
    fw.pop()
    wo3 = fw.sb([128, NKC, D], BF16, "wo3")
    for kc in range(NKC):
        fw.dma(wo3[:, kc, :], cf_w_out.ap()[kc * 128:(kc + 1) * 128, :], q="pool")
    xs2 = [fw.sb([128, NKC, 512], F32, "xs4_%d" % i) for i in range(2)]
    sq = fw.sb([128, NKC, 512], BF16, "sq4")
    rstd = fw.sb([128, 512], F32, "rstd4")
    for ci, c0 in enumerate(range(0, 2048, 512)):
        xs = xs2[ci % 2]
        fw.dma(xs[:], X3v[:, :, OB + c0:OB + c0 + 512])
        for fj in range(8):
            ps = P[4 + fj % 4]
            for kc in range(NKC):
                fw.mm(ps[:], wo3[:, kc, fj * 128:(fj + 1) * 128], ug[:, kc, c0:c0 + 512], start=(kc == 0), stop=(kc == NKC - 1))
            fw.stt(xs[:, fj, :], ps[:], ada3[:, 16 + fj:17 + fj, 0], xs[:, fj, :], ALU.mult, ALU.add)
        fw.act(sq[:], xs[:], AF.Square)
        for kc in range(NKC):
            fw.mm(P[0][:], ones_bf[:], sq[:, kc, :], start=(kc == 0), stop=(kc == NKC - 1))
        fw.act(rstd[:], P[0][:], AF.Sqrt, bias=eps_t[:, 0:1], scale=1.0 / D)
        fw.recip(rstd[:], rstd[:])
        for kc in range(NKC):
            fw.stt(xs[:, kc, :], xs[:, kc, :], cfv[:, 3, kc:kc + 1], rstd[:], ALU.mult, ALU.mult)
        fw.dma(outv[:, :, c0:c0 + 512], xs[:])
    fw.pop()
    return fw


def host_inputs_C(inp, x1T_full, zgT_full, xc1T, zgcT, b, h):
    s = h * 2048
    lo, hi = s - 192, s - 192 + W
    a, e = max(lo, 0), min(hi, 4096)

    def win(src):
        o = np.zeros((D, W), np.float32)
        o[:, a - lo:e - lo] = src[:, a:e]
        return o
    cos, sin = rope_tables(4096)
    C = np.ones((64, WT), np.float32)
    S = np.zeros((64, WT), np.float32)
    C[:, a - lo:e - lo] = cos[:, a:e]
    S[:, a - lo:e - lo] = sin[:, a:e]
    valid = np.zeros(W, np.float32)
    valid[a - lo:e - lo] = 1.0
    kq = np.arange(128)
    tri = np.stack([(kq[None, :] <= kq[:, None]), (kq[:, None] <= kq[None, :])], axis=1).astype(np.float32)
    fm = lambda v: np.ascontiguousarray(v.reshape(8, 128).T)
    return {
        "x1T": win(x1T_full), "zgT": win(zgT_full), "xc1T": np.ascontiguousarray(xc1T), "zgcT": np.ascontiguousarray(zgcT),
        "cT": np.ascontiguousarray(np.stack([inp["c"][b], inp["c_ctx"]], axis=1), np.float32),
        "norm_g": np.ascontiguousarray(inp["norm_g"].reshape(4, 8, 128).transpose(0, 2, 1)[1:4]),
        "ada_w": np.ascontiguousarray(inp["ada_w"][1:4]),
        "ada_b": np.ascontiguousarray(inp["ada_b"].reshape(4, 24, 128).transpose(0, 2, 1)[1:4]),
        "hy_w_out": inp["hy_w_out"][0], "swa_w_in": inp["swa_w_in"][0],
        "swa_sink": inp["swa_sink"][0].reshape(1, 16), "swa_w_out": inp["swa_w_out"][0],
        "cf_w_in": inp["cf_w_in"][0],
        "cf_dw_w": np.ascontiguousarray(inp["cf_dw_w"][0].T.reshape(8, 128, 31).transpose(1, 0, 2)),
        "cf_vec": np.ascontiguousarray(np.stack([fm(inp["cf_dw_b"][0]), fm(inp["cf_ln_g"][0]), fm(inp["cf_ln_b"][0]), fm(inp["final_g"])], axis=1)),
        "cf_w_out": inp["cf_w_out"][0],
        "ropeC": np.ascontiguousarray(np.concatenate([C, C], 0)), "ropeS": np.ascontiguousarray(np.concatenate([S, S], 0)),
        "kvalid": np.ascontiguousarray(valid.reshape(19, 128).T),
        "cmask": np.ascontiguousarray(np.broadcast_to(valid[U0:U0 + NU], (128, NU))),
        "negm": np.ascontiguousarray((tri - 1.0) * 30000.0), "ident": np.eye(128, dtype=np.float32),
    }


import numpy as np

_NC = {}


def _prog(name, builder):
    if name not in _NC:
        _NC[name] = builder().finish()
    return _NC[name]


def kernel(**inputs):
    inp = {k: np.asarray(v) for k, v in inputs.items()}
    cores = list(range(8))
    rA = run_bass_kernel_spmd(_prog("A", build_A), [host_inputs_A(inp, c // 2, c % 2) for c in cores], core_ids=cores).results
    x1T = [np.concatenate([rA[2 * b]["x1T"][:, :2048], rA[2 * b + 1]["x1T"][:, :2048]], axis=1) for b in range(4)]
    xc1T = [rA[2 * b]["x1T"][:, 2048:] for b in range(4)]
    rA2 = run_bass_kernel_spmd(_prog("A2", build_A2), [host_inputs_A2(inp, x1T[c // 2], xc1T[c // 2], c // 2, c % 2) for c in cores],
                               core_ids=cores).results
    U_all = np.stack([np.concatenate([rA2[2 * b]["U"], rA2[2 * b + 1]["U"]], axis=0) for b in range(4)])
    Uc_all = np.stack([rA2[2 * b]["Uc"] for b in range(4)])
    rB = run_bass_kernel_spmd(_prog("B", build_B), [host_inputs_B(inp, U_all, Uc_all, c) for c in cores], core_ids=cores).results
    zgT = [np.concatenate([rB[c]["ZgT"][:, b * 4096:(b + 1) * 4096] for c in cores], axis=0) for b in range(4)]
    zgcT = [np.concatenate([rB[c]["ZgcT"][:, b * 256:(b + 1) * 256] for c in cores], axis=0) for b in range(4)]
    rC = run_bass_kernel_spmd(_prog("C", build_C), [host_inputs_C(inp, x1T[c // 2], zgT[c // 2], xc1T[c // 2], zgcT[c // 2], c // 2, c % 2) for c in cores],
                              core_ids=cores).results
    out = np.empty((4, 4096, 1024), np.float32)
    for c in cores:
        out[c // 2, (c % 2) * 2048:(c % 2 + 1) * 2048, :] = rC[c]["outT"].T
    return out
```

```python
import numpy as np
import concourse.bass as bass
import concourse.mybir as mybir
from concourse.bass_utils import run_bass_kernel_spmd

F32 = mybir.dt.float32
BF16 = mybir.dt.bfloat16
AF = mybir.ActivationFunctionType
ALU = mybir.AluOpType
AX = mybir.AxisListType


class _Op:
    __slots__ = ("eng", "fn", "deps", "needs_inc", "signal", "is_dma", "idx", "is_barrier")

    def __init__(self, eng, fn, is_dma):
        self.eng = eng
        self.fn = fn
        self.deps = set()
        self.needs_inc = False
        self.signal = None
        self.is_dma = is_dma
        self.is_barrier = False


class FW:
    N_DSEM = 40

    def __init__(self):
        nc = self.nc = bass.Bass("TRN2", target_bir_lowering=False)
        self.eng = dict(pe=nc.tensor, act=nc.scalar, dve=nc.vector, pool=nc.gpsimd, sp=nc.sync)
        self.esem = {k: nc.alloc_semaphore("s_" + k) for k in ("pe", "act", "dve", "pool")}
        self.dsem = [nc.alloc_semaphore("d%d" % i) for i in range(self.N_DSEM)]
        self.dcnt = [0] * self.N_DSEM
        self.dlast = [None] * self.N_DSEM
        self.dnext = 0
        self.ops = []
        self.reg = {}
        self.n_alloc = 0
        self.last_op = {}
        self.dma_pending = []
        self.scopes = []

    def sb(self, shape, dt=F32, name=None):
        self.n_alloc += 1
        nm = (name or "sb") + "_%d" % self.n_alloc
        if self.scopes:
            g = self.nc.sbuf_tensor(nm, list(shape), dt)
            t = g.__enter__()
            self.scopes[-1].append(g)
            return t
        return self.nc.alloc_sbuf_tensor(nm, list(shape), dt)

    def push(self):
        self.scopes.append([])

    def pop(self):
        self.barrier()
        for g in reversed(self.scopes.pop()):
            g.__exit__(None, None, None)

    def barrier(self):
        o = _Op("sp", None, False)
        o.is_barrier = True
        o.deps = set(self.last_op.values()) | set(self.dma_pending)
        self.dma_pending = []
        self.ops.append(o)
        self.reg = {}

    def ps(self, name=None, shape=(128, 512), dt=F32):
        self.n_alloc += 1
        return self.nc.alloc_psum_tensor(name or ("ps%d" % self.n_alloc), list(shape), dt)

    def dram(self, name, shape, dt=F32, kind="Internal"):
        return self.nc.dram_tensor(name, list(shape), dt, kind=kind)

    @staticmethod
    def _box(ap):
        name = ap.tensor.name
        sp = str(ap.space)
        aps = ap.ap
        off = int(ap.offset)
        if "DRAM" in sp:
            lo = hi = off
            for st, cnt in aps:
                if st > 0:
                    hi += st * (cnt - 1)
                else:
                    lo += st * (cnt - 1)
            return (name, 0, 1, lo, hi + 1)
        if "PSUM" in sp:
            return (name, 0, 128, 0, 1 << 30)
        pst, pn = aps[0]
        p0 = off // pst
        f0 = off % pst
        lo = hi = f0
        for st, cnt in aps[1:]:
            if st > 0:
                hi += st * (cnt - 1)
            else:
                lo += st * (cnt - 1)
        return (name, p0, p0 + pn, lo, hi + 1)

    def _track(self, op, opi, reads, writes):
        deps = op.deps
        rkey = opi if op.is_dma else op.eng
        for ap in reads:
            b = self._box(ap)
            for r in self.reg.get(b[0], ()):
                if r[0] < b[2] and b[1] < r[1] and r[2] < b[4] and b[3] < r[3]:
                    if r[4] is not None:
                        deps.add(r[4])
                    r[5][rkey] = opi
        for ap in writes:
            b = self._box(ap)
            recs = self.reg.get(b[0], [])
            new = []
            for r in recs:
                if r[0] < b[2] and b[1] < r[1] and r[2] < b[4] and b[3] < r[3]:
                    if r[4] is not None:
                        deps.add(r[4])
                    deps.update(r[5].values())
                    if b[1] <= r[0] and r[1] <= b[2] and b[3] <= r[2] and r[3] <= b[4]:
                        continue
                new.append(r)
            new.append([b[1], b[2], b[3], b[4], opi, {}])
            self.reg[b[0]] = new
        deps.discard(opi)

    def op(self, eng, fn, reads, writes):
        o = _Op(eng, fn, False)
        opi = len(self.ops)
        self.ops.append(o)
        self.last_op[eng] = opi
        self._track(o, opi, reads, writes)
        return o

    def dma(self, out, in_, q="sp", **kw):
        o = _Op(q, None, True)
        opi = len(self.ops)
        s = self.dnext
        self.dnext = (self.dnext + 1) % self.N_DSEM
        self.dcnt[s] += 1
        o.signal = (self.dsem[s], 16 * self.dcnt[s])
        if self.dlast[s] is not None:
            o.deps.add(self.dlast[s])
        self.dlast[s] = opi
        e = self.eng[q]
        o.fn = lambda: e.dma_start(out=out, in_=in_, **kw)
        self.ops.append(o)
        self.dma_pending.append(opi)
        self._track(o, opi, [in_], [out])
        return o

    def collective(self, kind, out, in_, groups):
        o = _Op("pool", None, True)
        opi = len(self.ops)
        s = self.dnext
        self.dnext = (self.dnext + 1) % self.N_DSEM
        self.dcnt[s] += 1
        o.signal = (self.dsem[s], 16 * self.dcnt[s])
        if self.dlast[s] is not None:
            o.deps.add(self.dlast[s])
        self.dlast[s] = opi
        e = self.eng["pool"]
        aop = ALU.bypass if kind in ("AllGather", "AllToAll") else ALU.add
        o.fn = lambda: e.collective_compute(kind, aop, replica_groups=groups, ins=[in_], outs=[out])
        self.ops.append(o)
        self.dma_pending.append(opi)
        self._track(o, opi, [in_], [out])
        return o

    def finish(self):
        ops = self.ops
        for o in ops:
            for d in o.deps:
                p = ops[d]
                if p.is_dma:
                    continue
                if p.eng == "pe" and o.eng == "pe" and not o.is_dma and not o.is_barrier:
                    continue
                p.needs_inc = True
        cnt = {k: 0 for k in self.esem}
        for o in ops:
            if not o.is_dma and o.needs_inc and not o.is_barrier:
                cnt[o.eng] += 1
                o.signal = (self.esem[o.eng], cnt[o.eng])
        waited = {k: {} for k in self.eng}
        for o in ops:
            if o.is_barrier:
                for en, e in self.eng.items():
                    w = waited[en]
                    need = {}
                    for d in o.deps:
                        p = ops[d]
                        if p.signal is None:
                            continue
                        sem, val = p.signal
                        if w.get(sem.name, 0) < val and need.get(sem.name, (None, 0))[1] < val:
                            need[sem.name] = (sem, val)
                    for sem, val in need.values():
                        e.wait_ge(sem, val)
                        w[sem.name] = val
                continue
            e = self.eng[o.eng]
            w = waited[o.eng]
            need = {}
            for d in o.deps:
                p = ops[d]
                if p.signal is None:
                    continue
                if (not p.is_dma) and (not p.is_barrier) and p.eng == "pe" and o.eng == "pe" and not o.is_dma:
                    continue
                sem, val = p.signal
                if w.get(sem.name, 0) < val and need.get(sem.name, (None, 0))[1] < val:
                    need[sem.name] = (sem, val)
            for sem, val in need.values():
                e.wait_ge(sem, val)
                w[sem.name] = val
            ins = o.fn()
            if o.signal is not None:
                ins.then_inc(o.signal[0], 16 if o.is_dma else 1)
        sp = self.eng["sp"]
        for i, s in enumerate(self.dsem):
            if self.dcnt[i]:
                sp.wait_ge(s, 16 * self.dcnt[i])
        for k, s in self.esem.items():
            if cnt[k]:
                sp.wait_ge(s, cnt[k])
        self.stats = dict(n_ops=len(ops), cnt=cnt)
        return self.nc

    def mm(self, out, lhsT, rhs, start=True, stop=True):
        pe = self.eng["pe"]
        return self.op("pe", lambda: pe.matmul(out, lhsT, rhs, start=start, stop=stop),
                       [lhsT, rhs] + ([] if start else [out]), [out])

    def transpose(self, out, in_, ident):
        pe = self.eng["pe"]
        return self.op("pe", lambda: pe.transpose(out, in_, ident), [in_, ident], [out])

    def act(self, out, in_, func, bias=None, scale=None, accum_out=None):
        a = self.eng["act"]
        kw = {}
        reads = [in_]
        writes = [out]
        if bias is not None:
            kw["bias"] = bias
            if not isinstance(bias, (int, float)):
                reads.append(bias)
        if scale is not None:
            kw["scale"] = scale
            if not isinstance(scale, (int, float)):
                reads.append(scale)
        if accum_out is not None:
            kw["accum_out"] = accum_out
            writes.append(accum_out)
        return self.op("act", lambda: a.activation(out, in_, func, **kw), reads, writes)

    def tt(self, out, in0, in1, op, eng="dve"):
        e = self.eng[eng]
        return self.op(eng, lambda: e.tensor_tensor(out, in0, in1, op), [in0, in1], [out])

    def ts(self, out, in0, s1, op0, s2=None, op1=None, eng="dve", accum_out=None):
        e = self.eng[eng]
        reads = [in0]
        for s in (s1, s2):
            if s is not None and not isinstance(s, (int, float)):
                reads.append(s)
        writes = [out]
        kw = {}
        if accum_out is not None:
            kw["accum_out"] = accum_out
            writes.append(accum_out)
        if op1 is None:
            return self.op(eng, lambda: e.tensor_scalar(out, in0, s1, None, op0, **kw), reads, writes)
        return self.op(eng, lambda: e.tensor_scalar(out, in0, s1, s2, op0, op1, **kw), reads, writes)

    def stt(self, out, in0, scalar, in1, op0, op1, eng="dve"):
        e = self.eng[eng]
        reads = [in0, in1]
        if not isinstance(scalar, (int, float)):
            reads.append(scalar)
        return self.op(eng, lambda: e.scalar_tensor_tensor(out, in0, scalar, in1, op0, op1), reads, [out])

    def copy(self, out, in_, eng="dve"):
        e = self.eng[eng]
        if eng == "act":
            return self.op(eng, lambda: e.copy(out, in_), [in_], [out])
        return self.op(eng, lambda: e.tensor_copy(out, in_), [in_], [out])

    def memset(self, ap, val, eng="dve"):
        e = self.eng[eng]
        return self.op(eng, lambda: e.memset(ap, val), [], [ap])

    def recip(self, out, in_):
        e = self.eng["dve"]
        return self.op("dve", lambda: e.reciprocal(out, in_), [in_], [out])

    def reduce(self, out, in_, op, axis=AX.X):
        e = self.eng["dve"]
        return self.op("dve", lambda: e.tensor_reduce(out, in_, axis, op), [in_], [out])


import numpy as np

D = 1024
NKC = 8
EPS = 1e-6


def emit_ada(fw, P, cT, layers, ada_w, ada_b, norm_g):
    outs = {}
    for li in layers:
        outs[li] = (fw.sb([128, 24, 2], F32, "ada_l%d" % li), fw.sb([128, 8, 2], F32, "geff_l%d" % li))
    fw.push()
    cs = fw.sb([128, NKC, 2], F32, "cs")
    fw.dma(cs[:], cT.ap().rearrange("(c p) r -> p c r", p=128))
    sc = fw.sb([128, NKC, 2], F32, "silu_c")
    fw.act(sc[:], cs[:], AF.Silu)
    stg = [fw.sb([128, NKC, 512], F32, "adastg%d" % i) for i in range(2)]
    for n, li in enumerate(layers):
        ada, geff = outs[li]
        ps = P[n % 2]
        for cchunk in range(6):
            st = stg[cchunk % 2]
            fw.dma(st[:], ada_w.ap()[n].rearrange("(c p) n -> p c n", p=128)[:, :, cchunk * 512:(cchunk + 1) * 512])
            for jj in range(4):
                j = cchunk * 4 + jj
                for kc in range(NKC):
                    fw.mm(ps[:, 2 * j:2 * j + 2], st[:, kc, jj * 128:(jj + 1) * 128], sc[:, kc, :],
                          start=(kc == 0), stop=(kc == NKC - 1))
        bsb = fw.sb([128, 24], F32, "adab")
        fw.dma(bsb[:], ada_b.ap()[n])
        gsb = fw.sb([128, 8], F32, "ng")
        fw.dma(gsb[:], norm_g.ap()[n])
        psv = ps[:, 0:48].rearrange("p (j r) -> p j r", r=2)
        for r in range(2):
            fw.tt(ada[:, :, r], psv[:, :, r], bsb[:], ALU.add)
        for r in range(2):
            fw.stt(geff[:, :, r], ada[:, 8:16, r], 1.0, gsb[:], ALU.add, ALU.mult)
    fw.pop()
    return outs


def norm_mod(fw, ps_ss, xs, T, geff_r, shift_r, a_bf, sq, rstd, ones_bf, eps_t):
    fw.act(sq[:, :, :T], xs[:, :, :T], AF.Square)
    for kc in range(NKC):
        fw.mm(ps_ss[:, :T], ones_bf[:], sq[:, kc, :T], start=(kc == 0), stop=(kc == NKC - 1))
    fw.act(rstd[:, :T], ps_ss[:, :T], AF.Sqrt, bias=eps_t[:, 0:1], scale=1.0 / D)
    fw.recip(rstd[:, :T], rstd[:, :T])
    for kc in range(NKC):
        fw.tt(xs[:, kc, :T], xs[:, kc, :T], rstd[:, :T], ALU.mult)
        fw.act(a_bf[:, kc, :T], xs[:, kc, :T], AF.Identity, bias=shift_r[:, kc:kc + 1], scale=geff_r[:, kc:kc + 1])


def build_A():
    fw = FW()
    NQ, NKV, NCX = 2048, 4096, 256
    NTOK = NKV + NCX
    NQT = NQ + NCX
    T1 = 256
    xT = fw.dram("xT", [D, NTOK], F32, "ExternalInput")
    cT = fw.dram("cT", [D, 2], F32, "ExternalInput")
    ropeC = fw.dram("ropeC", [64, NTOK], F32, "ExternalInput")
    ropeS = fw.dram("ropeS", [64, NTOK], F32, "ExternalInput")
    norm_g = fw.dram("norm_g", [1, 128, 8], F32, "ExternalInput")
    ada_w = fw.dram("ada_w", [1, D, 3 * D], F32, "ExternalInput")
    ada_b = fw.dram("ada_b", [1, 128, 24], F32, "ExternalInput")
    w_in = fw.dram("mla_w_in", [D, 1728], F32, "ExternalInput")
    qng = fw.dram("q_norm_g", [128, 3], F32, "ExternalInput")
    kvng = fw.dram("kv_norm_g", [128, 2], F32, "ExternalInput")
    w_uq = fw.dram("mla_w_uq", [384, 1536], F32, "ExternalInput")
    w_ukv = fw.dram("mla_w_ukv", [256, 2048], F32, "ExternalInput")
    w_out = fw.dram("mla_w_out", [D, D], F32, "ExternalInput")
    x1T = fw.dram("x1T", [D, NQT], F32, "ExternalOutput")

    P = [fw.ps("P%d" % i) for i in range(8)]
    ones_bf = fw.sb([128, 128], BF16, "ones")
    fw.memset(ones_bf[:], 1.0)
    eps_t = fw.sb([128, 1], F32, "eps")
    fw.memset(eps_t[:], EPS)
    eps_q = fw.sb([128, 1], F32, "epsq")
    fw.memset(eps_q[:], EPS)

    ada = emit_ada(fw, P, cT, [0], ada_w, ada_b, norm_g)
    ada0, geff0 = ada[0]

    cqn = fw.sb([128, 3, NQT], BF16, "cqn")
    ckvn = fw.sb([128, 2, NTOK], BF16, "ckvn")
    krT = fw.sb([64, NTOK], BF16, "krT")
    sgate = fw.sb([128, 8, NQT], BF16, "sgate")
    qg = fw.sb([128, 3], F32, "qg")
    kg = fw.sb([128, 2], F32, "kg")
    fw.dma(qg[:], qng.ap())
    fw.dma(kg[:], kvng.ap())

    fw.push()
    w_in_sb = fw.sb([128, NKC, 1728], BF16, "w_in")
    for kc in range(NKC):
        fw.dma(w_in_sb[:, kc, :], w_in.ap()[kc * 128:(kc + 1) * 128, :], q="pool")
    w_krrot = fw.sb([128, NKC, 64], BF16, "w_krrot")
    fw.ts(w_krrot[:, :, 0:32], w_in_sb[:, :, 672:704], -1.0, ALU.mult)
    fw.copy(w_krrot[:, :, 32:64], w_in_sb[:, :, 640:672])
    xs2 = [fw.sb([128, NKC, T1], F32, "xs%d" % i) for i in range(2)]
    a2 = [fw.sb([128, NKC, T1], BF16, "abf%d" % i) for i in range(2)]
    sq = fw.sb([128, NKC, T1], BF16, "sq")
    rstd = fw.sb([128, T1], F32, "rstd")
    raw = fw.sb([128, 3, T1], F32, "raw")
    sq2 = fw.sb([128, 3, T1], BF16, "sq2")
    rstd2 = fw.sb([128, T1], F32, "rstd2")
    cc = fw.sb([64, T1], F32, "cc")
    ss = fw.sb([64, T1], F32, "ss")
    ra = fw.sb([64, T1], F32, "ra")
    rb = fw.sb([64, T1], F32, "rb")
    xTv = xT.ap().rearrange("(c p) n -> p c n", p=128)
    chunks = []
    for c0 in range(0, NTOK, T1):
        kind = "own" if c0 < NQ else ("other" if c0 < NKV else "ctx")
        chunks.append((c0, kind))
    pi = 0
    for ci, (c0, kind) in enumerate(chunks):
        xs = xs2[ci % 2]
        a_bf = a2[ci % 2]
        r = 1 if kind == "ctx" else 0
        fw.dma(xs[:], xTv[:, :, c0:c0 + T1])
        fw.dma(cc[:], ropeC.ap()[:, c0:c0 + T1])
        fw.dma(ss[:], ropeS.ap()[:, c0:c0 + T1])
        norm_mod(fw, P[0], xs, T1, geff0[:, :, r], ada0[:, 0:8, r], a_bf, sq, rstd, ones_bf, eps_t)

        def proj(ps_ap, wcols, m):
            for kc in range(NKC):
                fw.mm(ps_ap, wcols(kc), a_bf[:, kc, :], start=(kc == 0), stop=(kc == NKC - 1))

        def rms_group(nf, col_base, gvec, dst, dcol0):
            fw.tt(sq2[:, :nf, :], raw[:, :nf, :], raw[:, :nf, :], ALU.mult)
            for j in range(nf):
                fw.mm(P[1][:, :T1], ones_bf[:], sq2[:, j, :], start=(j == 0), stop=(j == nf - 1))
            fw.act(rstd2[:], P[1][:, :T1], AF.Sqrt, bias=eps_q[:, 0:1], scale=1.0 / (nf * 128))
            fw.recip(rstd2[:], rstd2[:])
            for j in range(nf):
                fw.stt(dst[:, j, dcol0:dcol0 + T1], raw[:, j, :], gvec[:, j:j + 1], rstd2[:], ALU.mult, ALU.mult)

        for j in range(2):
            ps = P[2 + (pi % 4)]; pi += 1
            proj(ps[:, :T1], lambda kc, j=j: w_in_sb[:, kc, 384 + j * 128:384 + (j + 1) * 128], 128)
            fw.copy(raw[:, j, :], ps[:, :T1], eng="act")
        rms_group(2, 384, kg, ckvn, c0)
        ps = P[2 + (pi % 4)]; pi += 1
        psr = P[2 + (pi % 4)]; pi += 1
        proj(ps[0:64, 0:T1], lambda kc: w_in_sb[:, kc, 640:704], 64)
        proj(psr[0:64, 0:T1], lambda kc: w_krrot[:, kc, :], 64)
        fw.tt(ra[:], ps[0:64, 0:T1], cc[:], ALU.mult)
        fw.tt(rb[:], psr[0:64, 0:T1], ss[:], ALU.mult)
        fw.tt(krT[:, c0:c0 + T1], ra[:], rb[:], ALU.add)
        if kind != "other":
            q0 = c0 if kind == "own" else NQ + (c0 - NKV)
            for j in range(3):
                ps = P[2 + (pi % 4)]; pi += 1
                proj(ps[:, :T1], lambda kc, j=j: w_in_sb[:, kc, j * 128:(j + 1) * 128], 128)
                fw.copy(raw[:, j, :], ps[:, :T1], eng="act")
            rms_group(3, 0, qg, cqn, q0)
            for j in range(8):
                ps = P[2 + (pi % 4)]; pi += 1
                proj(ps[:, :T1], lambda kc, j=j: w_in_sb[:, kc, 704 + j * 128:704 + (j + 1) * 128], 128)
                fw.act(sgate[:, j, q0:q0 + T1], ps[:, :T1], AF.Silu)
    fw.pop()

    fw.push()
    w_uq_sb = fw.sb([128, 3, 1536], BF16, "w_uq")
    for kc in range(3):
        fw.dma(w_uq_sb[:, kc, :], w_uq.ap()[kc * 128:(kc + 1) * 128, :], q="pool")
    w_ukv_sb = fw.sb([128, 2, 2048], BF16, "w_ukv")
    for kc in range(2):
        fw.dma(w_ukv_sb[:, kc, :], w_ukv.ap()[kc * 128:(kc + 1) * 128, :], q="pool")
    uqv = w_uq_sb[:].rearrange("p k (h f) -> p k h f", f=192)
    w_uqrot = fw.sb([128, 3, 8, 64], BF16, "w_uqrot")
    fw.ts(w_uqrot[:, :, :, 0:32], uqv[:, :, :, 160:192], -1.0, ALU.mult)
    fw.copy(w_uqrot[:, :, :, 32:64], uqv[:, :, :, 128:160])
    TQ = 512
    knT = [fw.sb([128, NTOK], BF16, "knT%d" % i) for i in range(2)]
    vh = [fw.sb([128, NTOK // 128, 128], BF16, "vh%d" % i) for i in range(2)]
    qnT = [fw.sb([128, NQT], BF16, "qnT%d" % i) for i in range(2)]
    qrT = [fw.sb([64, NQT], BF16, "qrT%d" % i) for i in range(2)]
    pT = [fw.sb([128, TQ], BF16, "pT%d" % i) for i in range(3)]
    ccq = fw.sb([64, NQT], F32, "ccq")
    ssq = fw.sb([64, NQT], F32, "ssq")
    fw.dma(ccq[:, 0:NQ], ropeC.ap()[:, 0:NQ]); fw.dma(ccq[:, NQ:NQT], ropeC.ap()[:, NKV:NTOK])
    fw.dma(ssq[:, 0:NQ], ropeS.ap()[:, 0:NQ]); fw.dma(ssq[:, NQ:NQT], ropeS.ap()[:, NKV:NTOK])
    ra = fw.sb([64, TQ], F32, "ra2")
    rb = fw.sb([64, TQ], F32, "rb2")
    rden = fw.sb([128, TQ], F32, "rden")
    accD = fw.sb([128, TQ], F32, "accD")
    accP = fw.sb([128, TQ], F32, "accP")
    ones_f = fw.sb([128, 128], F32, "onesf")
    fw.memset(ones_f[:], 1.0)
    onorm = fw.sb([128, TQ], F32, "onorm")
    sc_att = float(192 ** -0.5)
    NKT = NTOK // 128
    ev = 0
    pcount = 0
    for h in range(8):
        kn, v_h, qn, qr = knT[h % 2], vh[h % 2], qnT[h % 2], qrT[h % 2]
        for c0 in range(0, NTOK, 512):
            T = min(512, NTOK - c0)
            ps = P[6 + (ev % 2)]; ev += 1
            for kc in range(2):
                fw.mm(ps[:, :T], w_ukv_sb[:, kc, h * 256:h * 256 + 128], ckvn[:, kc, c0:c0 + T], start=(kc == 0), stop=(kc == 1))
            fw.copy(kn[:, c0:c0 + T], ps[:, :T], eng=("act" if ev % 2 else "dve"))
        for t0 in range(0, NKT, 4):
            nt = min(4, NKT - t0)
            ps = P[6 + (ev % 2)]; ev += 1
            for t in range(nt):
                for kc in range(2):
                    fw.mm(ps[:, t * 128:(t + 1) * 128], ckvn[:, kc, (t0 + t) * 128:(t0 + t + 1) * 128],
                          w_ukv_sb[:, kc, h * 256 + 128:h * 256 + 256], start=(kc == 0), stop=(kc == 1))
            fw.copy(v_h[:, t0:t0 + nt, :], ps[:, :nt * 128].rearrange("p (t f) -> p t f", f=128), eng=("act" if ev % 2 else "dve"))
        for c0 in range(0, NQT, 512):
            T = min(512, NQT - c0)
            ps = P[6 + (ev % 2)]; ev += 1
            for kc in range(3):
                fw.mm(ps[:, :T], w_uq_sb[:, kc, h * 192:h * 192 + 128], cqn[:, kc, c0:c0 + T], start=(kc == 0), stop=(kc == 2))
            fw.copy(qn[:, c0:c0 + T], ps[:, :T], eng=("act" if ev % 2 else "dve"))
            ps = P[6 + (ev % 2)]; ev += 1
            for kc in range(3):
                fw.mm(ps[0:64, :T], w_uq_sb[:, kc, h * 192 + 128:h * 192 + 192], cqn[:, kc, c0:c0 + T], start=(kc == 0), stop=(kc == 2))
            fw.tt(ra[:, :T], ps[0:64, :T], ccq[:, c0:c0 + T], ALU.mult)
            ps = P[6 + (ev % 2)]; ev += 1
            for kc in range(3):
                fw.mm(ps[0:64, :T], w_uqrot[:, kc, h, :], cqn[:, kc, c0:c0 + T], start=(kc == 0), stop=(kc == 2))
            fw.tt(rb[:, :T], ps[0:64, :T], ssq[:, c0:c0 + T], ALU.mult)
            fw.tt(qr[:, c0:c0 + T], ra[:, :T], rb[:, :T], ALU.add)
        qchunks = [(c0, 512, list(range(NKT))) for c0 in range(0, NQ, 512)] + [(NQ, NCX, list(range(NKV // 128, NKT)))]
        for qi, (q0, T, ktiles) in enumerate(qchunks):
            po = P[2 + (pcount % 2)]
            pd = P[4 + (pcount % 2)]
            pcount += 1
            def s_mm(ki):
                kt = ktiles[ki]
                ps = P[ki % 2]
                fw.mm(ps[:, :T], kn[:, kt * 128:(kt + 1) * 128], qn[:, q0:q0 + T], start=True, stop=False)
                fw.mm(ps[:, :T], krT[:, kt * 128:(kt + 1) * 128], qr[:, q0:q0 + T], start=False, stop=True)
            s_mm(0)
            for ki, kt in enumerate(ktiles):
                if ki + 1 < len(ktiles):
                    s_mm(ki + 1)
                ps = P[ki % 2]
                p_t = pT[ki % 3]
                fw.act(p_t[:, :T], ps[:, :T], AF.Exp, scale=sc_att)
                fw.mm(po[:, :T], v_h[:, kt, :], p_t[:, :T], start=(ki == 0), stop=(ki == len(ktiles) - 1))
                fw.mm(pd[:, :T], ones_bf[:], p_t[:, :T], start=(ki == 0), stop=(ki == len(ktiles) - 1))
            fw.recip(rden[:, :T], pd[:, :T])
            fw.tt(onorm[:, :T], po[:, :T], rden[:, :T], ALU.mult)
            fw.tt(sgate[:, h, q0:q0 + T], onorm[:, :T], sgate[:, h, q0:q0 + T], ALU.mult)
    fw.pop()

    fw.push()
    w_out_sb = fw.sb([128, NKC, D], BF16, "w_out")
    for kc in range(NKC):
        fw.dma(w_out_sb[:, kc, :], w_out.ap()[kc * 128:(kc + 1) * 128, :], q="pool")
    xs2 = [fw.sb([128, NKC, 512], F32, "xr%d" % i) for i in range(2)]
    x1v = x1T.ap().rearrange("(c p) n -> p c n", p=128)
    ochunks = [(c0, 512, c0, 0) for c0 in range(0, NQ, 512)] + [(NQ, NCX, NKV, 1)]
    for ci, (q0, T, xc0, r) in enumerate(ochunks):
        xs = xs2[ci % 2]
        fw.dma(xs[:, :, :T], xTv[:, :, xc0:xc0 + T])
        for fj in range(8):
            ps = P[fj % 4]
            for kc in range(NKC):
                fw.mm(ps[:, :T], w_out_sb[:, kc, fj * 128:(fj + 1) * 128], sgate[:, kc, q0:q0 + T], start=(kc == 0), stop=(kc == NKC - 1))
            fw.stt(xs[:, fj, :T], ps[:, :T], ada0[:, 16 + fj:17 + fj, r], xs[:, fj, :T], ALU.mult, ALU.add)
        fw.dma(x1v[:, :, q0:q0 + T], xs[:, :, :T])
    fw.pop()
    return fw


def host_inputs_A(inp, b, h):
    L = 4096
    own = slice(h * 2048, (h + 1) * 2048)
    oth = slice((1 - h) * 2048, (2 - h) * 2048)
    x = inp["x"][b]
    xT = np.concatenate([x[own].T, x[oth].T, inp["ctx"][b].T], axis=1)
    cos, sin = rope_tables(L)
    C = np.concatenate([cos[:, own], cos[:, oth], np.ones((64, 256), np.float32)], axis=1)
    S = np.concatenate([sin[:, own], sin[:, oth], np.zeros((64, 256), np.float32)], axis=1)
    return {
        "xT": np.ascontiguousarray(xT, np.float32),
        "cT": np.ascontiguousarray(np.stack([inp["c"][b], inp["c_ctx"]], axis=1), np.float32),
        "ropeC": np.ascontiguousarray(C), "ropeS": np.ascontiguousarray(S),
        "norm_g": np.ascontiguousarray(inp["norm_g"].reshape(4, 8, 128).transpose(0, 2, 1)[0:1]),
        "ada_w": np.ascontiguousarray(inp["ada_w"][0:1]),
        "ada_b": np.ascontiguousarray(inp["ada_b"].reshape(4, 24, 128).transpose(0, 2, 1)[0:1]),
        "mla_w_in": inp["mla_w_in"][0],
        "q_norm_g": np.ascontiguousarray(inp["mla_q_norm_g"][0].reshape(3, 128).T),
        "kv_norm_g": np.ascontiguousarray(inp["mla_kv_norm_g"][0].reshape(2, 128).T),
        "mla_w_uq": inp["mla_w_uq"][0], "mla_w_ukv": inp["mla_w_ukv"][0], "mla_w_out": inp["mla_w_out"][0],
    }


_ROPE = {}


def rope_tables(L):
    if L not in _ROPE:
        rows = L // 64
        row = np.repeat(np.arange(rows, dtype=np.float32), 64)
        col = np.tile(np.arange(64, dtype=np.float32), rows)
        inv = (np.float32(10000.0) ** (-np.arange(16, dtype=np.float32) / np.float32(16))).astype(np.float32)
        ang = np.concatenate([row[:, None] * inv, col[:, None] * inv], axis=-1)
        cos = np.cos(ang).astype(np.float32).T
        sin = np.sin(ang).astype(np.float32).T
        _ROPE[L] = (np.concatenate([cos, cos], 0), np.concatenate([sin, sin], 0))
    return _ROPE[L]


import numpy as np


def build_A2():
    fw = FW()
    NQ, NCX = 2048, 256
    WL = NQ + 2
    WC = NCX + 2
    x1T = fw.dram("x1T", [D, WL], F32, "ExternalInput")
    xc1T = fw.dram("xc1T", [D, NCX], F32, "ExternalInput")
    hmask = fw.dram("hmask", [128, 2], F32, "ExternalInput")
    cT = fw.dram("cT", [D, 2], F32, "ExternalInput")
    norm_g = fw.dram("norm_g", [1, 128, 8], F32, "ExternalInput")
    ada_w = fw.dram("ada_w", [1, D, 3 * D], F32, "ExternalInput")
    ada_b = fw.dram("ada_b", [1, 128, 24], F32, "ExternalInput")
    w_in = fw.dram("hy_w_in", [D, 4096], F32, "ExternalInput")
    conv_w = fw.dram("hy_conv_w", [3, 3072], F32, "ExternalInput")
    conv_b = fw.dram("hy_conv_b", [1, 3072], F32, "ExternalInput")
    U = fw.dram("U", [NQ, 4096], BF16, "ExternalOutput")
    Uc = fw.dram("Uc", [NCX, 4096], BF16, "ExternalOutput")

    P = [fw.ps("P%d" % i) for i in range(8)]
    ones_bf = fw.sb([128, 128], BF16, "ones")
    fw.memset(ones_bf[:], 1.0)
    eps_t = fw.sb([128, 1], F32, "eps")
    fw.memset(eps_t[:], EPS)
    ada = emit_ada(fw, P, cT, [1], ada_w, ada_b, norm_g)
    ada1, geff1 = ada[1]
    hm = fw.sb([128, 2], F32, "hm")
    fw.dma(hm[:], hmask.ap())

    aL = fw.sb([128, NKC, WL], BF16, "aL")
    aC = fw.sb([128, NKC, WC], BF16, "aC")
    fw.memset(aC[:, :, 0:1], 0.0)
    fw.memset(aC[:, :, WC - 1:WC], 0.0)
    fw.push()
    T1 = 512
    xs2 = [fw.sb([128, NKC, T1], F32, "xs%d" % i) for i in range(2)]
    ab = fw.sb([128, NKC, T1], BF16, "ab")
    sq = fw.sb([128, NKC, T1], BF16, "sq")
    rstd = fw.sb([128, T1], F32, "rstd")
    x1v = x1T.ap().rearrange("(c p) n -> p c n", p=128)
    xcv = xc1T.ap().rearrange("(c p) n -> p c n", p=128)
    jobs = [(x1v, c0, min(T1, WL - c0), aL, c0, 0) for c0 in range(0, WL, T1)] + [(xcv, 0, NCX, aC, 1, 1)]
    for ci, (src, c0, T, dst, d0, r) in enumerate(jobs):
        xs = xs2[ci % 2]
        fw.dma(xs[:, :, :T], src[:, :, c0:c0 + T])
        norm_mod(fw, P[0], xs, T, geff1[:, :, r], ada1[:, 0:8, r], dst[:, :, d0:d0 + T], sq, rstd, ones_bf, eps_t)
    fw.ts(aL[:, :, 0:1], aL[:, :, 0:1], hm[:, 0:1], ALU.mult)
    fw.ts(aL[:, :, WL - 1:WL], aL[:, :, WL - 1:WL], hm[:, 1:2], ALU.mult)
    fw.pop()
    fw.push()
    w_sb = fw.sb([128, NKC, 4096], BF16, "w_in")
    for kc in range(NKC):
        fw.dma(w_sb[:, kc, :], w_in.ap()[kc * 128:(kc + 1) * 128, :], q="pool")
    wt = [fw.sb([128, 3, NKC, 512], BF16, "wt%d" % i) for i in range(2)]
    cwb = [fw.sb([128, 3, 512], F32, "cwb%d" % i) for i in range(2)]
    cbb = [fw.sb([128, 512], F32, "cbb%d" % i) for i in range(2)]
    ost = [fw.sb([128, 512], BF16, "ost%d" % i) for i in range(3)]
    tiles = [(aL, t0, U, t0) for t0 in range(0, NQ, 128)] + [(aC, t0, Uc, t0) for t0 in range(0, NCX, 128)]
    n = 0
    for cc in range(8):
        c0 = cc * 512
        if cc < 6:
            w3 = wt[cc % 2]
            cw = cwb[cc % 2]
            cb = cbb[cc % 2]
            for j in range(3):
                fw.dma(cw[:, j, :], conv_w.ap()[j:j + 1, c0:c0 + 512].partition_broadcast(128))
            fw.dma(cb[:], conv_b.ap()[0:1, c0:c0 + 512].partition_broadcast(128))
            for j in range(3):
                for kc in range(NKC):
                    fw.tt(w3[:, j, kc, :], w_sb[:, kc, c0:c0 + 512], cw[:, j, :], ALU.mult,
                          eng=("pool" if (kc % 2) else "dve"))
        for (asrc, t0, dst, r0) in tiles:
            ps = P[1 + (n % 4)]
            o = ost[n % 3]
            n += 1
            if cc < 6:
                k = 0
                for j in range(3):
                    for kc in range(NKC):
                        fw.mm(ps[:], asrc[:, kc, t0 + j:t0 + j + 128], w3[:, j, kc, :], start=(k == 0), stop=(k == 23))
                        k += 1
                fw.tt(o[:], ps[:], cb[:], ALU.add)
            else:
                for kc in range(NKC):
                    fw.mm(ps[:], asrc[:, kc, t0 + 1:t0 + 129], w_sb[:, kc, c0:c0 + 512], start=(kc == 0), stop=(kc == NKC - 1))
                fw.copy(o[:], ps[:], eng="act")
            fw.dma(dst.ap()[r0:r0 + 128, c0:c0 + 512], o[:])
    fw.pop()
    return fw


def host_inputs_A2(inp, x1T_full, xc1T, b, h):
    s = h * 2048
    cols = np.zeros((D, 2050), np.float32)
    lo, hi = s - 1, s + 2049
    a, e = max(lo, 0), min(hi, 4096)
    cols[:, a - lo:e - lo] = x1T_full[:, a:e]
    hm = np.zeros((128, 2), np.float32)
    hm[:, 0] = 1.0 if lo >= 0 else 0.0
    hm[:, 1] = 1.0 if hi <= 4096 else 0.0
    return {
        "x1T": cols, "xc1T": np.ascontiguousarray(xc1T), "hmask": hm,
        "cT": np.ascontiguousarray(np.stack([inp["c"][b], inp["c_ctx"]], axis=1), np.float32),
        "norm_g": np.ascontiguousarray(inp["norm_g"].reshape(4, 8, 128).transpose(0, 2, 1)[1:2]),
        "ada_w": np.ascontiguousarray(inp["ada_w"][1:2]),
        "ada_b": np.ascontiguousarray(inp["ada_b"].reshape(4, 24, 128).transpose(0, 2, 1)[1:2]),
        "hy_w_in": inp["hy_w_in"][0], "hy_conv_w": inp["hy_conv_w"][0],
        "hy_conv_b": inp["hy_conv_b"][0].reshape(1, 3072),
    }


import math
import numpy as np

PI = float(np.pi)


def sin_act(fw, dst, arg, tmp):
    for _ in range(2):
        fw.ts(tmp, arg, PI, ALU.is_gt, -2.0 * PI, ALU.mult)
        fw.tt(arg, arg, tmp, ALU.add)
        fw.ts(tmp, arg, -PI, ALU.is_lt, 2.0 * PI, ALU.mult)
        fw.tt(arg, arg, tmp, ALU.add)
    fw.act(dst, arg, AF.Sin)


def gen_filter(fw, P, L, tabs, mlp, Kd, bias_sb):
    hdn_d, hdnrev_d, dec_d, decrev_d = tabs
    w1, w2, w3, fr, bfr, wo = mlp
    T = min(512, L)
    CH = min(2048, L)
    fw.push()
    Kx = fw.sb([128, 2, 2 * L], F32, "Kx")
    fw.memset(Kx[:, :, 2 * L - 1:2 * L], 0.0)
    h0 = fw.sb([33, L], F32, "h0")
    dc = fw.sb([128, L], F32, "dc")
    arg = fw.sb([64, L], F32, "arg")
    tmp = fw.sb([64, L], F32, "tmpf")
    hh = [fw.sb([64, L], F32, "hh%d" % i) for i in range(2)]
    kt = fw.sb([128, T], F32, "ktmp")
    n = 0
    for rev in (1, 0):
        for c0 in range(0, L, CH):
            fw.dma(h0[:, c0:c0 + CH], (hdnrev_d if rev else hdn_d).ap()[:, c0:c0 + CH])
            fw.dma(dc[:, c0:c0 + CH], (decrev_d if rev else dec_d).ap()[:, c0:c0 + CH], q="act")
        src = h0
        for k, w in enumerate((w1, w2, w3)):
            kin = 33 if k == 0 else 64
            dst = hh[k % 2]
            for t0 in range(0, L, T):
                ps = P[n % 2]; n += 1
                fw.mm(ps[0:64, :T], w[0:kin, :], src[0:kin, t0:t0 + T], start=True, stop=True)
                fw.ts(arg[:, t0:t0 + T], ps[0:64, :T], fr[:, k:k + 1], ALU.mult, bfr[:, k:k + 1], ALU.add)
            for c0 in range(0, L, CH):
                sin_act(fw, dst[:, c0:c0 + CH], arg[:, c0:c0 + CH], tmp[:, c0:c0 + CH])
            src = dst
        for o in range(2):
            for t0 in range(0, L, T):
                ps = P[2 + n % 2]; n += 1
                d = dc[:, t0:t0 + T]
                fw.mm(ps[:, :T], wo[:, o * 2 + rev, :], src[:, t0:t0 + T], start=True, stop=True)
                if rev:
                    fw.tt(Kx[:, o, t0:t0 + T], ps[:, :T], d, ALU.mult)
                else:
                    if t0 == 0:
                        fw.tt(kt[:], ps[:, :T], d, ALU.mult)
                        fw.copy(Kx[:, o, L:L + T - 1], kt[:, 1:T])
                        fw.tt(Kx[:, o, L - 1:L], Kx[:, o, L - 1:L], kt[:, 0:1], ALU.add)
                        fw.tt(Kx[:, o, L - 1:L], Kx[:, o, L - 1:L], bias_sb[:, o:o + 1], ALU.add)
                    else:
                        fw.tt(Kx[:, o, L - 1 + t0:L - 1 + t0 + T], ps[:, :T], d, ALU.mult)
    kb = [fw.sb([128, 2048], BF16, "Kb%d" % i) for i in range(2)]
    k = 0
    for o in range(2):
        for m0 in range(0, 2 * L, 2048):
            m1 = min(2 * L, m0 + 2048)
            b = kb[k % 2]
            fw.copy(b[:, :m1 - m0], Kx[:, o, m0:m1], eng=("act" if k % 2 else "dve"))
            fw.dma(Kd.ap()[:, o, m0:m1], b[:, :m1 - m0], q=("sp" if k % 2 else "act"))
            k += 1
    fw.pop()


def long_conv_stage(fw, P, NB, L, Kd, Uv, J_bf, ident_bf, Pbf, ZgT, name):
    NBI = 4 * NB
    ND = 2 * NB - 1
    fw.push()
    znat = fw.sb([128, NBI, 128], BF16, name + "znat")
    zr = fw.sb([128, 128, NBI], BF16, name + "zr")
    gA = fw.sb([128, NBI, 128], BF16, name + "gA")
    gB = fw.sb([128, NBI, 128], BF16, name + "gB")
    batched = (ND * 128 * 128 * 2) <= 100 * 1024
    if batched:
        Tall = fw.sb([128, 128, ND * 128], BF16, name + "Tall")
    else:
        Tt = [fw.sb([128, ND * 128], BF16, name + "T%d" % i) for i in range(4)]
    ost = [fw.sb([128, 512], F32, name + "ost%d" % i) for i in range(2)]
    def load_group(dst, g):
        step = max(1, NBI // 2)
        for k, n0 in enumerate(range(0, NBI, step)):
            fw.dma(dst[:, n0:n0 + step, :], Uv.ap()[:, g, n0:n0 + step, :], q=("sp" if k % 2 == 0 else "act"))

    def reverse_rows():
        cpb = 512 // 128
        k = 0
        for n0 in range(0, NBI, cpb):
            nn = min(cpb, NBI - n0)
            ps = P[4 + (k % 2)]
            fw.mm(ps[:, :nn * 128], J_bf[:], znat[:, n0:n0 + nn, :], start=True, stop=True)
            fw.copy(zr[:, :, n0:n0 + nn], ps[:, :nn * 128].rearrange("p (n c) -> p c n", c=128),
                    eng=("act" if k % 2 else "dve"))
            k += 1

    load_group(znat, 0)
    load_group(gA, 1)
    reverse_rows()
    dorder = [0] + [d for k in range(1, NB) for d in (k, -k)]
    ti = 0
    for o in range(2):
        if o == 1:
            load_group(gA, 2)
            load_group(gB, 3)
            for n0 in range(0, NBI, 16):
                n1 = min(NBI, n0 + 16)
                fw.act(gB[:, n0:n1, :], gB[:, n0:n1, :], AF.Silu)
                fw.tt(gA[:, n0:n1, :], gA[:, n0:n1, :], gB[:, n0:n1, :], ALU.mult)
            reverse_rows()
        if batched:
            for cg in range(0, 128, 32):
                src = bass.AP(Kd, (cg * 2 + o) * 2 * L, [[1, 128], [2 * 2 * L, 32], [1, ND * 128]])
                fw.dma(Tall[:, cg:cg + 32, :], src, q=("sp" if (cg // 32) % 2 == 0 else "act"))
        for c0 in range(0, 128, 4):
            ps = P[2 + ((c0 // 4) % 2)]
            for s in range(4):
                c = c0 + s
                if batched:
                    T = Tall[:, c, :]
                else:
                    T = Tt[ti % 4]
                    src = bass.AP(Kd, (c * 2 + o) * 2 * L, [[1, 128], [1, ND * 128]])
                    fw.dma(T[:], src, q=("sp" if ti % 2 == 0 else "act"))
                    ti += 1
                outv = ps[:, s * NBI:(s + 1) * NBI].rearrange("p (b i) -> p b i", b=4)
                inv = zr[:, c, :].rearrange("p (b i) -> p b i", b=4)
                for di, d in enumerate(dorder):
                    lo = max(0, d)
                    n = NB - abs(d)
                    fw.mm(outv[:, :, lo:lo + n], T[:, (d + NB - 1) * 128:(d + NB) * 128], inv[:, :, lo - d:lo - d + n],
                          start=(di == 0), stop=(di == ND - 1))
            pv = ps[:, 0:4 * NBI].rearrange("p (s n) -> p n s", s=4)
            fw.tt(znat[:, :, c0:c0 + 4], pv, gA[:, :, c0:c0 + 4], ALU.mult)
    k = 0
    for n0 in range(0, NBI, 4):
        nn = min(4, NBI - n0)
        for j in range(nn):
            fw.transpose(Pbf[:, j * 128:(j + 1) * 128], znat[:, n0 + j, :], ident_bf[:])
        o = ost[k % 2]
        k += 1
        fw.copy(o[:, :nn * 128], Pbf[:, :nn * 128])
        fw.dma(ZgT.ap()[:, n0 * 128:(n0 + nn) * 128], o[:, :nn * 128])
    fw.pop()


def build_B(stages=("fc", "fl", "cc", "cl")):
    fw = FW()
    L, LC = 4096, 256
    Uv = fw.dram("Uv", [128, 4, 4 * (L // 128), 128], BF16, "ExternalInput")
    Ucv = fw.dram("Ucv", [128, 4, 4 * (LC // 128), 128], BF16, "ExternalInput")
    f_w1 = fw.dram("f_w1", [33, 64], F32, "ExternalInput")
    f_w2 = fw.dram("f_w2", [64, 64], F32, "ExternalInput")
    f_w3 = fw.dram("f_w3", [64, 64], F32, "ExternalInput")
    f_b = fw.dram("f_b", [64, 3], F32, "ExternalInput")
    f_fr = fw.dram("f_fr", [64, 3], F32, "ExternalInput")
    f_wo = fw.dram("f_wo", [64, 4, 128], F32, "ExternalInput")
    hy_bias = fw.dram("hy_bias", [128, 2], F32, "ExternalInput")
    tabsL = [fw.dram(n, s, F32, "ExternalInput") for n, s in
             (("hdnL", [33, L]), ("hdnLr", [33, L]), ("decL", [128, L]), ("decLr", [128, L]))]
    tabsC = [fw.dram(n, s, F32, "ExternalInput") for n, s in
             (("hdnC", [33, LC]), ("hdnCr", [33, LC]), ("decC", [128, LC]), ("decCr", [128, LC]))]
    Jd = fw.dram("Jmat", [128, 128], F32, "ExternalInput")
    Id = fw.dram("Imat", [128, 128], F32, "ExternalInput")
    ZgT = fw.dram("ZgT", [128, 4 * L], F32, "ExternalOutput")
    ZgcT = fw.dram("ZgcT", [128, 4 * LC], F32, "ExternalOutput")
    KdL = fw.dram("KdL", [128, 2, 2 * L], BF16, "ExternalOutput")
    KdC = fw.dram("KdC", [128, 2, 2 * LC], BF16, "ExternalOutput")

    P = [fw.ps("P%d" % i) for i in range(6)]
    Pbf = fw.ps("Pbf", (128, 1024), BF16)
    J_bf = fw.sb([128, 128], BF16, "J")
    I_bf = fw.sb([128, 128], BF16, "I")
    fw.dma(J_bf[:], Jd.ap(), q="pool")
    fw.dma(I_bf[:], Id.ap(), q="pool")
    w1 = fw.sb([33, 64], F32, "w1"); fw.dma(w1[:], f_w1.ap())
    w2 = fw.sb([64, 64], F32, "w2"); fw.dma(w2[:], f_w2.ap())
    w3 = fw.sb([64, 64], F32, "w3"); fw.dma(w3[:], f_w3.ap())
    fb = fw.sb([64, 3], F32, "fb"); fw.dma(fb[:], f_b.ap())
    fr = fw.sb([64, 3], F32, "fr"); fw.dma(fr[:], f_fr.ap())
    wo = fw.sb([64, 4, 128], F32, "wo"); fw.dma(wo[:], f_wo.ap())
    bias_sb = fw.sb([128, 2], F32, "hbias"); fw.dma(bias_sb[:], hy_bias.ap())
    bfr = fw.sb([64, 3], F32, "bfr")
    fw.tt(bfr[:], fb[:], fr[:], ALU.mult)
    mlp = (w1, w2, w3, fr, bfr, wo)
    if "fc" in stages:
        gen_filter(fw, P, LC, tabsC, mlp, KdC, bias_sb)
    if "fl" in stages:
        gen_filter(fw, P, L, tabsL, mlp, KdL, bias_sb)
    if "cc" in stages:
        long_conv_stage(fw, P, LC // 128, LC, KdC, Ucv, J_bf, I_bf, Pbf, ZgcT, "c_")
    if "cl" in stages:
        long_conv_stage(fw, P, L // 128, L, KdL, Uv, J_bf, I_bf, Pbf, ZgT, "l_")
    return fw


_TAB = {}


def hyena_tables(Lx):
    if Lx not in _TAB:
        f32 = np.float32
        t = np.linspace(0.0, 1.0, Lx, dtype=f32)[:, None]
        wpos = (f32(2.0 * math.pi / Lx)) * np.arange(Lx, dtype=f32)[:, None]
        bands = np.linspace(1e-4, 15, 16, dtype=f32)[None, :]
        hdn = np.concatenate([t, np.cos(bands * wpos), -np.sin(bands * wpos)], axis=-1).astype(f32).T
        deltas = np.abs(np.linspace(math.log(1e-2) / 0.3, math.log(1e-2) / 1.5, 1024, dtype=f32))
        dec = np.exp(-t * deltas).astype(f32).T
        _TAB[Lx] = (np.ascontiguousarray(hdn), np.ascontiguousarray(hdn[:, ::-1]), dec)
    return _TAB[Lx]


def host_inputs_B(inp, U_all, Uc_all, cb):
    cs = slice(cb * 128, (cb + 1) * 128)
    Uv = U_all.reshape(4, 32, 128, 4, 1024)[..., cs].transpose(2, 3, 0, 1, 4).reshape(128, 4, 128, 128)
    Ucv = Uc_all.reshape(4, 2, 128, 4, 1024)[..., cs].transpose(2, 3, 0, 1, 4).reshape(128, 4, 8, 128)
    hL, hLr, dL = hyena_tables(4096)
    hC, hCr, dC = hyena_tables(256)
    eye = np.eye(128, dtype=np.float32)
    return {
        "Uv": np.ascontiguousarray(Uv), "Ucv": np.ascontiguousarray(Ucv),
        "f_w1": inp["hy_filt_w_in"][0], "f_w2": inp["hy_filt_w_hid"][0][0], "f_w3": inp["hy_filt_w_hid"][0][1],
        "f_b": np.ascontiguousarray(inp["hy_filt_b"][0].T), "f_fr": np.ascontiguousarray(inp["hy_filt_freq"][0].T),
        "f_wo": np.ascontiguousarray(inp["hy_filt_w_out"][0].reshape(64, 4, 1024)[:, :, cs]),
        "hy_bias": np.ascontiguousarray(inp["hy_bias"][0][:, cs].T),
        "hdnL": hL, "hdnLr": hLr, "decL": np.ascontiguousarray(dL[cs]), "decLr": np.ascontiguousarray(dL[cs][:, ::-1]),
        "hdnC": hC, "hdnCr": hCr, "decC": np.ascontiguousarray(dC[cs]), "decCr": np.ascontiguousarray(dC[cs][:, ::-1]),
        "Jmat": np.ascontiguousarray(eye[::-1]), "Imat": eye,
    }


import numpy as np

W = 2432
NCX = 256
WT = W + NCX
Q0, Q1 = 128, 2304
QW = Q1 - Q0
O0, O1 = 192, 2240
U0 = O0 - 15
NU = 2048 + 30


def build_C(debug=False, stop=None):
    fw = FW()
    kindo = "ExternalOutput"
    x1T = fw.dram("x1T", [D, W], F32, "ExternalInput")
    zgT = fw.dram("zgT", [D, W], F32, "ExternalInput")
    xc1T = fw.dram("xc1T", [D, NCX], F32, "ExternalInput")
    zgcT = fw.dram("zgcT", [D, NCX], F32, "ExternalInput")
    cT = fw.dram("cT", [D, 2], F32, "ExternalInput")
    norm_g = fw.dram("norm_g", [3, 128, 8], F32, "ExternalInput")
    ada_w = fw.dram("ada_w", [3, D, 3 * D], F32, "ExternalInput")
    ada_b = fw.dram("ada_b", [3, 128, 24], F32, "ExternalInput")
    hy_w_out = fw.dram("hy_w_out", [D, D], F32, "ExternalInput")
    swa_w_in = fw.dram("swa_w_in", [D, 2560], F32, "ExternalInput")
    swa_sink = fw.dram("swa_sink", [1, 16], F32, "ExternalInput")
    swa_w_out = fw.dram("swa_w_out", [D, D], F32, "ExternalInput")
    cf_w_in = fw.dram("cf_w_in", [D, 3072], F32, "ExternalInput")
    cf_dw_w = fw.dram("cf_dw_w", [128, 8, 31], F32, "ExternalInput")
    cf_vec = fw.dram("cf_vec", [128, 4, 8], F32, "ExternalInput")
    cf_w_out = fw.dram("cf_w_out", [D, D], F32, "ExternalInput")
    ropeC = fw.dram("ropeC", [128, WT], F32, "ExternalInput")
    ropeS = fw.dram("ropeS", [128, WT], F32, "ExternalInput")
    kvalid_d = fw.dram("kvalid", [128, 19], F32, "ExternalInput")
    cmask_d = fw.dram("cmask", [128, NU], F32, "ExternalInput")
    tri_d = fw.dram("negm", [128, 2, 128], F32, "ExternalInput")
    ident_d = fw.dram("ident", [128, 128], F32, "ExternalInput")
    outT = fw.dram("outT", [D, 2048], F32, "ExternalOutput")
    X2 = fw.dram("X2", [D, W], F32, kindo if debug else "Internal")
    X3 = fw.dram("X3", [D, QW], F32, kindo if debug else "Internal")

    P = [fw.ps("P%d" % i) for i in range(8)]
    ones_bf = fw.sb([128, 128], BF16, "ones")
    fw.memset(ones_bf[:], 1.0)
    ones_f = fw.sb([128, 128], F32, "onesf")
    fw.memset(ones_f[:], 1.0)
    eps_t = fw.sb([128, 1], F32, "eps")
    fw.memset(eps_t[:], EPS)
    ada = emit_ada(fw, P, cT, [1, 2, 3], ada_w, ada_b, norm_g)
    (ada1, geff1), (ada2, geff2), (ada3, geff3) = ada[1], ada[2], ada[3]
    cfv = fw.sb([128, 4, 8], F32, "cfv")
    fw.dma(cfv[:], cf_vec.ap())
    x1v = x1T.ap().rearrange("(c p) n -> p c n", p=128)
    zgv = zgT.ap().rearrange("(c p) n -> p c n", p=128)
    xcv = xc1T.ap().rearrange("(c p) n -> p c n", p=128)
    zgcv = zgcT.ap().rearrange("(c p) n -> p c n", p=128)
    X2v = X2.ap().rearrange("(c p) n -> p c n", p=128)
    X3v = X3.ap().rearrange("(c p) n -> p c n", p=128)
    outv = outT.ap().rearrange("(c p) n -> p c n", p=128)

    og = fw.sb([128, NKC, QW], BF16, "og")
    a3 = og
    fw.push()
    a2 = fw.sb([128, NKC, WT], BF16, "a2")

    fw.push()
    wo1 = fw.sb([128, NKC, D], BF16, "wo1")
    for kc in range(NKC):
        fw.dma(wo1[:, kc, :], hy_w_out.ap()[kc * 128:(kc + 1) * 128, :], q="pool")
    xs2 = [fw.sb([128, NKC, 512], F32, "xs%d" % i) for i in range(2)]
    zg2 = [fw.sb([128, NKC, 512], BF16, "zg%d" % i) for i in range(2)]
    sq = fw.sb([128, NKC, 512], BF16, "sq")
    rstd = fw.sb([128, 512], F32, "rstd")
    jobs = [(x1v, zgv, c0, min(512, W - c0), c0, 0, True) for c0 in range(0, W, 512)] + [(xcv, zgcv, 0, NCX, W, 1, False)]
    for ci, (xsrc, zsrc, c0, T, d0, r, store) in enumerate(jobs):
        xs, zg = xs2[ci % 2], zg2[ci % 2]
        fw.dma(xs[:, :, :T], xsrc[:, :, c0:c0 + T])
        fw.dma(zg[:, :, :T], zsrc[:, :, c0:c0 + T], q="pool")
        for fj in range(8):
            ps = P[2 + fj % 4]
            for kc in range(NKC):
                fw.mm(ps[:, :T], wo1[:, kc, fj * 128:(fj + 1) * 128], zg[:, kc, :T], start=(kc == 0), stop=(kc == NKC - 1))
            fw.stt(xs[:, fj, :T], ps[:, :T], ada1[:, 16 + fj:17 + fj, r], xs[:, fj, :T], ALU.mult, ALU.add)
        if store:
            fw.dma(X2v[:, :, c0:c0 + T], xs[:, :, :T])
        norm_mod(fw, P[0], xs, T, geff2[:, :, r], ada2[:, 0:8, r], a2[:, :, d0:d0 + T], sq, rstd, ones_bf, eps_t)
    fw.pop()

    if stop == "c1":
        return fw
    fw.push()
    CC = fw.sb([128, WT], F32, "CC")
    SS = fw.sb([128, WT], F32, "SS")
    fw.dma(CC[:], ropeC.ap())
    fw.dma(SS[:], ropeS.ap())
    kval = fw.sb([128, 19], F32, "kval")
    fw.dma(kval[:], kvalid_d.ap())
    negm = fw.sb([128, 2, 128], BF16, "negm")
    fw.dma(negm[:], tri_d.ap(), q="pool")
    ident_bf = fw.sb([128, 128], BF16, "ident")
    fw.dma(ident_bf[:], ident_d.ap(), q="pool")
    kvm = fw.sb([128, 19, 128], BF16, "kvm")
    for t in range(19):
        fw.ts(kvm[:, t, :], ones_bf[:], kval[:, t:t + 1], ALU.mult)
    esink = fw.sb([128, 16], F32, "esink")
    fw.dma(esink[:], swa_sink.ap().partition_broadcast(128))
    fw.act(esink[:], esink[:], AF.Exp)
    k2 = [fw.sb([128, WT], BF16, "k2_%d" % i) for i in range(4)]
    NT = WT // 128
    Vd = fw.sb([128, NT, 512], BF16, "Vd")
    fw.push()
    wkv = fw.sb([128, NKC, 512], BF16, "wkv")
    for kc in range(NKC):
        fw.dma(wkv[:, kc, :], swa_w_in.ap()[kc * 128:(kc + 1) * 128, 1024:1536], q="pool")
    wk_d = fw.sb([128, NKC, 4, 2, 64], BF16, "wk_d")
    wkr_d = fw.sb([128, NKC, 4, 2, 64], BF16, "wkr_d")
    wv_d = fw.sb([128, NKC, 4, 2, 64], BF16, "wv_d")
    kview = wkv[:, :, 0:256].rearrange("p k (h f) -> p k h f", f=64)
    vview = wkv[:, :, 256:512].rearrange("p k (h f) -> p k h f", f=64)
    for dup in range(2):
        fw.copy(wk_d[:, :, :, dup, :], kview)
        fw.copy(wv_d[:, :, :, dup, :], vview, eng="act")
        fw.ts(wkr_d[:, :, :, dup, 0:32], kview[:, :, :, 32:64], -1.0, ALU.mult)
        fw.copy(wkr_d[:, :, :, dup, 32:64], kview[:, :, :, 0:32], eng="act")
    ra = fw.sb([128, 512], F32, "ra")
    rb = fw.sb([128, 512], F32, "rb")
    ev = 0
    for hk in range(4):
        for c0 in range(0, WT, 512):
            T = min(512, WT - c0)
            ps = P[2 + ev % 2]; ev += 1
            for kc in range(NKC):
                fw.mm(ps[:, :T], wk_d[:, kc, hk].rearrange("p a f -> p (a f)"), a2[:, kc, c0:c0 + T], start=(kc == 0), stop=(kc == NKC - 1))
            fw.tt(ra[:, :T], ps[:, :T], CC[:, c0:c0 + T], ALU.mult)
            ps = P[2 + ev % 2]; ev += 1
            for kc in range(NKC):
                fw.mm(ps[:, :T], wkr_d[:, kc, hk].rearrange("p a f -> p (a f)"), a2[:, kc, c0:c0 + T], start=(kc == 0), stop=(kc == NKC - 1))
            fw.tt(rb[:, :T], ps[:, :T], SS[:, c0:c0 + T], ALU.mult)
            fw.tt(k2[hk][:, c0:c0 + T], ra[:, :T], rb[:, :T], ALU.add)
    for t in range(NT):
        ps = P[2 + ev % 2]; ev += 1
        for kc in range(NKC):
            fw.mm(ps[:], a2[:, kc, t * 128:(t + 1) * 128], wv_d[:, kc].rearrange("p h a f -> p (h a f)"), start=(kc == 0), stop=(kc == NKC - 1))
        if t < 19:
            fw.ts(Vd[:, t, :], ps[:], kval[:, t:t + 1], ALU.mult)
        else:
            fw.copy(Vd[:, t, :], ps[:], eng="act")
    fw.pop()
    if stop == "c2a":
        return fw
    ra = fw.sb([128, 512], F32, "ra_g")
    rb = fw.sb([128, 512], F32, "rb_g")
    wq = fw.sb([128, NKC, 256], BF16, "wq")
    wqr = fw.sb([128, NKC, 4, 2, 32], BF16, "wqr")
    wg = fw.sb([128, NKC, 256], BF16, "wg")
    qz = fw.sb([128, 4, QW], BF16, "qz")
    for g in range(4):
        fw.memset(qz[:, g, :], 0.0)
    sg = fw.sb([128, 2, QW], BF16, "sg")
    pT = [fw.sb([128, 512], BF16, "pT%d" % i) for i in range(3)]
    den = fw.sb([128, 512], F32, "den")
    onrm = fw.sb([128, 512], F32, "onrm")
    sc_att = 0.125
    pc = 0
    for hk in range(4):
        for kc in range(NKC):
            fw.dma(wq[:, kc, :], swa_w_in.ap()[kc * 128:(kc + 1) * 128, hk * 256:(hk + 1) * 256], q="pool")
            fw.dma(wg[:, kc, :], swa_w_in.ap()[kc * 128:(kc + 1) * 128, 1536 + hk * 256:1536 + (hk + 1) * 256], q="pool")
        wqv = wq[:].rearrange("p k (h a f) -> p k h a f", a=2, f=32)
        fw.ts(wqr[:, :, :, 0, :], wqv[:, :, :, 1, :], -1.0, ALU.mult)
        fw.copy(wqr[:, :, :, 1, :], wqv[:, :, :, 0, :])
        wqr_f = wqr[:].rearrange("p k h a f -> p k (h a f)")
        for cc in range(2):
            for c0 in range(0, QW, 512):
                T = min(512, QW - c0)
                ps = P[2 + ev % 2]; ev += 1
                for kc in range(NKC):
                    fw.mm(ps[:, :T], wq[:, kc, cc * 128:(cc + 1) * 128], a2[:, kc, Q0 + c0:Q0 + c0 + T], start=(kc == 0), stop=(kc == NKC - 1))
                fw.tt(ra[:, :T], ps[:, :T], CC[:, Q0 + c0:Q0 + c0 + T], ALU.mult)
                ps = P[2 + ev % 2]; ev += 1
                for kc in range(NKC):
                    fw.mm(ps[:, :T], wqr_f[:, kc, cc * 128:(cc + 1) * 128], a2[:, kc, Q0 + c0:Q0 + c0 + T], start=(kc == 0), stop=(kc == NKC - 1))
                fw.tt(rb[:, :T], ps[:, :T], SS[:, Q0 + c0:Q0 + c0 + T], ALU.mult)
                for half in range(2):
                    hr = slice(half * 64, (half + 1) * 64)
                    fw.tt(qz[hr, cc * 2 + half, c0:c0 + T], ra[hr, :T], rb[hr, :T], ALU.add)
                ps = P[2 + ev % 2]; ev += 1
                for kc in range(NKC):
                    fw.mm(ps[:, :T], wg[:, kc, cc * 128:(cc + 1) * 128], a2[:, kc, Q0 + c0:Q0 + c0 + T], start=(kc == 0), stop=(kc == NKC - 1))
                fw.act(sg[:, cc, c0:c0 + T], ps[:, :T], AF.Silu)
        for j in range(1, 18):
            qc = (j - 1) * 128
            po = P[4 + pc % 2]
            pd = P[6 + pc % 2]
            pc += 1
            ktl = [(j - 1, 0), (j, None), (j + 1, 1), (19, -1), (20, -1)]
            def s_mm(ki):
                kt, tsel = ktl[ki]
                ps = P[ki % 2]
                masked = tsel in (0, 1)
                for g in range(4):
                    fw.mm(ps[:, g * 128:(g + 1) * 128], k2[hk][:, kt * 128:(kt + 1) * 128],
                          qz[:, g, qc:qc + 128], start=True, stop=not masked)
                    if masked:
                        fw.mm(ps[:, g * 128:(g + 1) * 128], ident_bf[:], negm[:, tsel, :], start=False, stop=True)
            s_mm(0)
            for ki, (kt, tsel) in enumerate(ktl):
                if ki + 1 < len(ktl):
                    s_mm(ki + 1)
                ps = P[ki % 2]
                p_t = pT[ki % 3]
                fw.act(p_t[:], ps[:], AF.Exp, scale=sc_att)
                p_m = p_t
                fw.mm(po[:], Vd[:, kt, hk * 128:(hk + 1) * 128], p_m[:], start=(ki == 0), stop=(ki == 4))
                fw.mm(pd[:], (ones_bf[:] if tsel == -1 else kvm[:, kt, :]), p_m[:], start=(ki == 0), stop=(ki == 4))
            for g in range(4):
                h = hk * 4 + g
                fw.ts(den[:, g * 128:(g + 1) * 128], pd[:, g * 128:(g + 1) * 128], esink[:, h:h + 1], ALU.add)
            fw.recip(den[:], den[:])
            fw.tt(onrm[:], po[:], den[:], ALU.mult)
            for g in range(4):
                hf = (g % 2) * 64
                cch = g // 2
                fw.tt(og[hf:hf + 64, hk * 2 + cch, qc:qc + 128], onrm[hf:hf + 64, g * 128:(g + 1) * 128], sg[hf:hf + 64, cch, qc:qc + 128], ALU.mult)
    fw.pop()
    fw.pop()
    if stop == "c2":
        return fw

    fw.push()
    wo2 = fw.sb([128, NKC, D], BF16, "wo2")
    for kc in range(NKC):
        fw.dma(wo2[:, kc, :], swa_w_out.ap()[kc * 128:(kc + 1) * 128, :], q="pool")
    xs2 = [fw.sb([128, NKC, 512], F32, "xs3_%d" % i) for i in range(2)]
    sq = fw.sb([128, NKC, 512], BF16, "sq3")
    rstd = fw.sb([128, 512], F32, "rstd3")
    for ci, c0 in enumerate(range(0, QW, 512)):
        T = min(512, QW - c0)
        xs = xs2[ci % 2]
        fw.dma(xs[:, :, :T], X2v[:, :, Q0 + c0:Q0 + c0 + T])
        for fj in range(8):
            ps = P[2 + fj % 4]
            for kc in range(NKC):
                fw.mm(ps[:, :T], wo2[:, kc, fj * 128:(fj + 1) * 128], og[:, kc, c0:c0 + T], start=(kc == 0), stop=(kc == NKC - 1))
            fw.stt(xs[:, fj, :T], ps[:, :T], ada2[:, 16 + fj:17 + fj, 0], xs[:, fj, :T], ALU.mult, ALU.add)
        fw.dma(X3v[:, :, c0:c0 + T], xs[:, :, :T])
        norm_mod(fw, P[0], xs, T, geff3[:, :, 0], ada3[:, 0:8, 0], a3[:, :, c0:c0 + T], sq, rstd, ones_bf, eps_t)
    fw.pop()

    if stop == "c3":
        return fw
    fw.push()
    UB = U0 - Q0
    OB = O0 - Q0
    ug = fw.sb([128, NKC, 2048], BF16, "ug")
    fw.push()
    uc = fw.sb([128, NKC, 2048], F32, "uc")
    fw.push()
    cm = fw.sb([128, NU], F32, "cmask")
    fw.dma(cm[:], cmask_d.ap())
    dww = fw.sb([128, 8, 31], F32, "dww")
    fw.dma(dww[:], cf_dw_w.ap())
    wab = [fw.sb([128, NKC, 2, 128], BF16, "wab%d" % i) for i in range(2)]
    usb = [fw.sb([128, NU + 2], BF16, "usb%d" % i) for i in range(2)]
    dgs = [fw.sb([128, 31, 128], BF16, "dg%d" % i) for i in range(2)]
    identc = fw.sb([128, 128], BF16, "identc")
    fw.dma(identc[:], ident_d.ap(), q="pool")
    sig = fw.sb([128, 512], F32, "sig")
    for fj in range(8):
        w2 = wab[fj % 2]
        u = usb[fj % 2]
        for kc in range(NKC):
            fw.dma(w2[:, kc, 0, :], cf_w_in.ap()[kc * 128:(kc + 1) * 128, fj * 128:(fj + 1) * 128], q="pool")
            fw.dma(w2[:, kc, 1, :], cf_w_in.ap()[kc * 128:(kc + 1) * 128, 1024 + fj * 128:1024 + (fj + 1) * 128], q="pool")
        for c0 in range(0, NU, 512):
            T = min(512, NU - c0)
            pa, pb = P[2 + (c0 // 512) % 2], P[4 + (c0 // 512) % 2]
            for kc in range(NKC):
                fw.mm(pa[:, :T], w2[:, kc, 0, :], a3[:, kc, UB + c0:UB + c0 + T], start=(kc == 0), stop=(kc == NKC - 1))
            for kc in range(NKC):
                fw.mm(pb[:, :T], w2[:, kc, 1, :], a3[:, kc, UB + c0:UB + c0 + T], start=(kc == 0), stop=(kc == NKC - 1))
            fw.act(sig[:, :T], pb[:, :T], AF.Sigmoid)
            fw.tt(sig[:, :T], sig[:, :T], cm[:, c0:c0 + T], ALU.mult)
            fw.tt(u[:, c0:c0 + T], pa[:, :T], sig[:, :T], ALU.mult)
        dg = dgs[fj % 2]
        for j in range(31):
            fw.ts(dg[:, j, :], identc[:], dww[:, fj, j:j + 1], ALU.mult)
        for c0 in range(0, 2048, 512):
            pc4 = P[6 + (c0 // 512) % 2]
            for j in range(31):
                fw.mm(pc4[:], dg[:, j, :], u[:, c0 + j:c0 + j + 512], start=(j == 0), stop=(j == 30))
            fw.act(uc[:, fj, c0:c0 + 512], pc4[:], AF.Identity, bias=cfv[:, 0, fj:fj + 1], scale=1.0)
    fw.pop()
    fw.push()
    mean = fw.sb([128, 2048], F32, "mean")
    rs = fw.sb([128, 2048], F32, "rs")
    sqf = fw.sb([128, 512], F32, "sqf")
    for c0 in range(0, 2048, 512):
        for fj in range(8):
            fw.mm(P[2][:], ones_f[:], uc[:, fj, c0:c0 + 512], start=(fj == 0), stop=(fj == 7))
        for fj in range(8):
            fw.tt(sqf[:], uc[:, fj, c0:c0 + 512], uc[:, fj, c0:c0 + 512], ALU.mult)
            fw.mm(P[3][:], ones_f[:], sqf[:], start=(fj == 0), stop=(fj == 7))
        fw.ts(mean[:, c0:c0 + 512], P[2][:], 1.0 / D, ALU.mult)
        fw.tt(sqf[:], mean[:, c0:c0 + 512], mean[:, c0:c0 + 512], ALU.mult)
        fw.stt(sqf[:], P[3][:], 1.0 / D, sqf[:], ALU.mult, ALU.subtract)
        fw.act(rs[:, c0:c0 + 512], sqf[:], AF.Sqrt, bias=eps_t[:, 0:1], scale=1.0)
        fw.recip(rs[:, c0:c0 + 512], rs[:, c0:c0 + 512])
    wgt = [fw.sb([128, NKC, 128], BF16, "wgt%d" % i) for i in range(2)]
    sg3 = fw.sb([128, 512], F32, "sg3")
    for fj in range(8):
        w2 = wgt[fj % 2]
        for kc in range(NKC):
            fw.dma(w2[:, kc, :], cf_w_in.ap()[kc * 128:(kc + 1) * 128, 2048 + fj * 128:2048 + (fj + 1) * 128], q="pool")
        for c0 in range(0, 2048, 512):
            ps = P[4 + (c0 // 512) % 2]
            for kc in range(NKC):
                fw.mm(ps[:], w2[:, kc, :], a3[:, kc, OB + c0:OB + c0 + 512], start=(kc == 0), stop=(kc == NKC - 1))
            fw.act(sg3[:], ps[:], AF.Silu)
            fw.tt(uc[:, fj, c0:c0 + 512], uc[:, fj, c0:c0 + 512], mean[:, c0:c0 + 512], ALU.subtract)
            fw.tt(uc[:, fj, c0:c0 + 512], uc[:, fj, c0:c0 + 512], rs[:, c0:c0 + 512], ALU.mult)
            fw.act(uc[:, fj, c0:c0 + 512], uc[:, fj, c0:c0 + 512], AF.Silu, bias=cfv[:, 2, fj:fj + 1], scale=cfv[:, 1, fj:fj + 1])
            fw.tt(ug[:, fj, c0:c0 + 512], uc[:, fj, c0:c0 + 512], sg3[:], ALU.mult)
    fw.pop()
    fw.pop()
    wo3 = fw.sb([128, NKC, D], BF16, "wo3")
    for kc in range(NKC):
        fw.dma(wo3[:, kc, :], cf_w_out.ap()[kc * 128:(kc + 1) * 128, :], q="pool")
    xs2 = [fw.sb([128, NKC, 512], F32, "xs4_%d" % i) for i in range(2)]
    sq = fw.sb([128, NKC, 512], BF16, "sq4")
    rstd = fw.sb([128, 512], F32, "rstd4")
    for ci, c0 in enumerate(range(0, 2048, 512)):
        xs = xs2[ci % 2]
        fw.dma(xs[:], X3v[:, :, OB + c0:OB + c0 + 512])
        for fj in range(8):
            ps = P[4 + fj % 4]
            for kc in range(NKC):
                fw.mm(ps[:], wo3[:, kc, fj * 128:(fj + 1) * 128], ug[:, kc, c0:c0 + 512], start=(kc == 0), stop=(kc == NKC - 1))
            fw.stt(xs[:, fj, :], ps[:], ada3[:, 16 + fj:17 + fj, 0], xs[:, fj, :], ALU.mult, ALU.add)
        fw.act(sq[:], xs[:], AF.Square)
        for kc in range(NKC):
            fw.mm(P[0][:], ones_bf[:], sq[:, kc, :], start=(kc == 0), stop=(kc == NKC - 1))
        fw.act(rstd[:], P[0][:], AF.Sqrt, bias=eps_t[:, 0:1], scale=1.0 / D)
        fw.recip(rstd[:], rstd[:])
        for kc in range(NKC):
            fw.stt(xs[:, kc, :], xs[:, kc, :], cfv[:, 3, kc:kc + 1], rstd[:], ALU.mult, ALU.mult)
        fw.dma(outv[:, :, c0:c0 + 512], xs[:])
    fw.pop()
    return fw


def host_inputs_C(inp, x1T_full, zgT_full, xc1T, zgcT, b, h):
    s = h * 2048
    lo, hi = s - 192, s - 192 + W
    a, e = max(lo, 0), min(hi, 4096)

    def win(src):
        o = np.zeros((D, W), np.float32)
        o[:, a - lo:e - lo] = src[:, a:e]
        return o
    cos, sin = rope_tables(4096)
    C = np.ones((64, WT), np.float32)
    S = np.zeros((64, WT), np.float32)
    C[:, a - lo:e - lo] = cos[:, a:e]
    S[:, a - lo:e - lo] = sin[:, a:e]
    valid = np.zeros(W, np.float32)
    valid[a - lo:e - lo] = 1.0
    kq = np.arange(128)
    tri = np.stack([(kq[None, :] <= kq[:, None]), (kq[:, None] <= kq[None, :])], axis=1).astype(np.float32)
    fm = lambda v: np.ascontiguousarray(v.reshape(8, 128).T)
    return {
        "x1T": win(x1T_full), "zgT": win(zgT_full), "xc1T": np.ascontiguousarray(xc1T), "zgcT": np.ascontiguousarray(zgcT),
        "cT": np.ascontiguousarray(np.stack([inp["c"][b], inp["c_ctx"]], axis=1), np.float32),
        "norm_g": np.ascontiguousarray(inp["norm_g"].reshape(4, 8, 128).transpose(0, 2, 1)[1:4]),
        "ada_w": np.ascontiguousarray(inp["ada_w"][1:4]),
        "ada_b": np.ascontiguousarray(inp["ada_b"].reshape(4, 24, 128).transpose(0, 2, 1)[1:4]),
        "hy_w_out": inp["hy_w_out"][0], "swa_w_in": inp["swa_w_in"][0],
        "swa_sink": inp["swa_sink"][0].reshape(1, 16), "swa_w_out": inp["swa_w_out"][0],
        "cf_w_in": inp["cf_w_in"][0],
        "cf_dw_w": np.ascontiguousarray(inp["cf_dw_w"][0].T.reshape(8, 128, 31).transpose(1, 0, 2)),
        "cf_vec": np.ascontiguousarray(np.stack([fm(inp["cf_dw_b"][0]), fm(inp["cf_ln_g"][0]), fm(inp["cf_ln_b"][0]), fm(inp["final_g"])], axis=1)),
        "cf_w_out": inp["cf_w_out"][0],
        "ropeC": np.ascontiguousarray(np.concatenate([C, C], 0)), "ropeS": np.ascontiguousarray(np.concatenate([S, S], 0)),
        "kvalid": np.ascontiguousarray(valid.reshape(19, 128).T),
        "cmask": np.ascontiguousarray(np.broadcast_to(valid[U0:U0 + NU], (128, NU))),
        "negm": np.ascontiguousarray((tri - 1.0) * 30000.0), "ident": np.eye(128, dtype=np.float32),
    }


import numpy as np

_NC = {}


def _prog(name, builder):
    if name not in _NC:
        _NC[name] = builder().finish()
    return _NC[name]


def kernel(**inputs):
    inp = {k: np.asarray(v) for k, v in inputs.items()}
    cores = list(range(8))
    rA = run_bass_kernel_spmd(_prog("A", build_A), [host_inputs_A(inp, c // 2, c % 2) for c in cores], core_ids=cores).results
    x1T = [np.concatenate([rA[2 * b]["x1T"][:, :2048], rA[2 * b + 1]["x1T"][:, :2048]], axis=1) for b in range(4)]
    xc1T = [rA[2 * b]["x1T"][:, 2048:] for b in range(4)]
    rA2 = run_bass_kernel_spmd(_prog("A2", build_A2), [host_inputs_A2(inp, x1T[c // 2], xc1T[c // 2], c // 2, c % 2) for c in cores],
                               core_ids=cores).results
    U_all = np.stack([np.concatenate([rA2[2 * b]["U"], rA2[2 * b + 1]["U"]], axis=0) for b in range(4)])
    Uc_all = np.stack([rA2[2 * b]["Uc"] for b in range(4)])
    rB = run_bass_kernel_spmd(_prog("B", build_B), [host_inputs_B(inp, U_all, Uc_all, c) for c in cores], core_ids=cores).results
    zgT = [np.concatenate([rB[c]["ZgT"][:, b * 4096:(b + 1) * 4096] for c in cores], axis=0) for b in range(4)]
    zgcT = [np.concatenate([rB[c]["ZgcT"][:, b * 256:(b + 1) * 256] for c in cores], axis=0) for b in range(4)]
    rC = run_bass_kernel_spmd(_prog("C", build_C), [host_inputs_C(inp, x1T[c // 2], zgT[c // 2], xc1T[c // 2], zgcT[c // 2], c // 2, c % 2) for c in cores],
                              core_ids=cores).results
    out = np.empty((4, 4096, 1024), np.float32)
    for c in cores:
        out[c // 2, (c % 2) * 2048:(c % 2 + 1) * 2048, :] = rC[c]["outT"].T
    return out
```

```python
import numpy as np
import concourse.bass as bass
import concourse.mybir as mybir
from concourse.bass_utils import run_bass_kernel_spmd

F32 = mybir.dt.float32
BF16 = mybir.dt.bfloat16
AF = mybir.ActivationFunctionType
ALU = mybir.AluOpType
AX = mybir.AxisListType


class _Op:
    __slots__ = ("eng", "fn", "deps", "needs_inc", "signal", "is_dma", "idx", "is_barrier")

    def __init__(self, eng, fn, is_dma):
        self.eng = eng
        self.fn = fn
        self.deps = set()
        self.needs_inc = False
        self.signal = None
        self.is_dma = is_dma
        self.is_barrier = False


class FW:
    N_DSEM = 40

    def __init__(self):
        nc = self.nc = bass.Bass("TRN2", target_bir_lowering=False)
        self.eng = dict(pe=nc.tensor, act=nc.scalar, dve=nc.vector, pool=nc.gpsimd, sp=nc.sync)
        self.esem = {k: nc.alloc_semaphore("s_" + k) for k in ("pe", "act", "dve", "pool")}
        self.dsem = [nc.alloc_semaphore("d%d" % i) for i in range(self.N_DSEM)]
        self.dcnt = [0] * self.N_DSEM
        self.dlast = [None] * self.N_DSEM
        self.dnext = 0
        self.ops = []
        self.reg = {}
        self.n_alloc = 0
        self.last_op = {}
        self.dma_pending = []
        self.scopes = []

    def sb(self, shape, dt=F32, name=None):
        self.n_alloc += 1
        nm = (name or "sb") + "_%d" % self.n_alloc
        if self.scopes:
            g = self.nc.sbuf_tensor(nm, list(shape), dt)
            t = g.__enter__()
            self.scopes[-1].append(g)
            return t
        return self.nc.alloc_sbuf_tensor(nm, list(shape), dt)

    def push(self):
        self.scopes.append([])

    def pop(self):
        self.barrier()
        for g in reversed(self.scopes.pop()):
            g.__exit__(None, None, None)

    def barrier(self):
        o = _Op("sp", None, False)
        o.is_barrier = True
        o.deps = set(self.last_op.values()) | set(self.dma_pending)
        self.dma_pending = []
        self.ops.append(o)
        self.reg = {}

    def ps(self, name=None, shape=(128, 512), dt=F32):
        self.n_alloc += 1
        return self.nc.alloc_psum_tensor(name or ("ps%d" % self.n_alloc), list(shape), dt)

    def dram(self, name, shape, dt=F32, kind="Internal"):
        return self.nc.dram_tensor(name, list(shape), dt, kind=kind)

    @staticmethod
    def _box(ap):
        name = ap.tensor.name
        sp = str(ap.space)
        aps = ap.ap
        off = int(ap.offset)
        if "DRAM" in sp:
            lo = hi = off
            for st, cnt in aps:
                if st > 0:
                    hi += st * (cnt - 1)
                else:
                    lo += st * (cnt - 1)
            return (name, 0, 1, lo, hi + 1)
        if "PSUM" in sp:
            return (name, 0, 128, 0, 1 << 30)
        pst, pn = aps[0]
        p0 = off // pst
        f0 = off % pst
        lo = hi = f0
        for st, cnt in aps[1:]:
            if st > 0:
                hi += st * (cnt - 1)
            else:
                lo += st * (cnt - 1)
        return (name, p0, p0 + pn, lo, hi + 1)

    def _track(self, op, opi, reads, writes):
        deps = op.deps
        rkey = opi if op.is_dma else op.eng
        for ap in reads:
            b = self._box(ap)
            for r in self.reg.get(b[0], ()):
                if r[0] < b[2] and b[1] < r[1] and r[2] < b[4] and b[3] < r[3]:
                    if r[4] is not None:
                        deps.add(r[4])
                    r[5][rkey] = opi
        for ap in writes:
            b = self._box(ap)
            recs = self.reg.get(b[0], [])
            new = []
            for r in recs:
                if r[0] < b[2] and b[1] < r[1] and r[2] < b[4] and b[3] < r[3]:
                    if r[4] is not None:
                        deps.add(r[4])
                    deps.update(r[5].values())
                    if b[1] <= r[0] and r[1] <= b[2] and b[3] <= r[2] and r[3] <= b[4]:
                        continue
                new.append(r)
            new.append([b[1], b[2], b[3], b[4], opi, {}])
            self.reg[b[0]] = new
        deps.discard(opi)

    def op(self, eng, fn, reads, writes):
        o = _Op(eng, fn, False)
        opi = len(self.ops)
        self.ops.append(o)
        self.last_op[eng] = opi
        self._track(o, opi, reads, writes)
        return o

    def dma(self, out, in_, q="sp", **kw):
        o = _Op(q, None, True)
        opi = len(self.ops)
        s = self.dnext
        self.dnext = (self.dnext + 1) % self.N_DSEM
        self.dcnt[s] += 1
        o.signal = (self.dsem[s], 16 * self.dcnt[s])
        if self.dlast[s] is not None:
            o.deps.add(self.dlast[s])
        self.dlast[s] = opi
        e = self.eng[q]
        o.fn = lambda: e.dma_start(out=out, in_=in_, **kw)
        self.ops.append(o)
        self.dma_pending.append(opi)
        self._track(o, opi, [in_], [out])
        return o

    def collective(self, kind, out, in_, groups):
        o = _Op("pool", None, True)
        opi = len(self.ops)
        s = self.dnext
        self.dnext = (self.dnext + 1) % self.N_DSEM
        self.dcnt[s] += 1
        o.signal = (self.dsem[s], 16 * self.dcnt[s])
        if self.dlast[s] is not None:
            o.deps.add(self.dlast[s])
        self.dlast[s] = opi
        e = self.eng["pool"]
        aop = ALU.bypass if kind in ("AllGather", "AllToAll") else ALU.add
        o.fn = lambda: e.collective_compute(kind, aop, replica_groups=groups, ins=[in_], outs=[out])
        self.ops.append(o)
        self.dma_pending.append(opi)
        self._track(o, opi, [in_], [out])
        return o

    def finish(self):
        ops = self.ops
        for o in ops:
            for d in o.deps:
                p = ops[d]
                if p.is_dma:
                    continue
                if p.eng == "pe" and o.eng == "pe" and not o.is_dma and not o.is_barrier:
                    continue
                p.needs_inc = True
        cnt = {k: 0 for k in self.esem}
        for o in ops:
            if not o.is_dma and o.needs_inc and not o.is_barrier:
                cnt[o.eng] += 1
                o.signal = (self.esem[o.eng], cnt[o.eng])
        waited = {k: {} for k in self.eng}
        for o in ops:
            if o.is_barrier:
                for en, e in self.eng.items():
                    w = waited[en]
                    need = {}
                    for d in o.deps:
                        p = ops[d]
                        if p.signal is None:
                            continue
                        sem, val = p.signal
                        if w.get(sem.name, 0) < val and need.get(sem.name, (None, 0))[1] < val:
                            need[sem.name] = (sem, val)
                    for sem, val in need.values():
                        e.wait_ge(sem, val)
                        w[sem.name] = val
                continue
            e = self.eng[o.eng]
            w = waited[o.eng]
            need = {}
            for d in o.deps:
                p = ops[d]
                if p.signal is None:
                    continue
                if (not p.is_dma) and (not p.is_barrier) and p.eng == "pe" and o.eng == "pe" and not o.is_dma:
                    continue
                sem, val = p.signal
                if w.get(sem.name, 0) < val and need.get(sem.name, (None, 0))[1] < val:
                    need[sem.name] = (sem, val)
            for sem, val in need.values():
                e.wait_ge(sem, val)
                w[sem.name] = val
            ins = o.fn()
            if o.signal is not None:
                ins.then_inc(o.signal[0], 16 if o.is_dma else 1)
        sp = self.eng["sp"]
        for i, s in enumerate(self.dsem):
            if self.dcnt[i]:
                sp.wait_ge(s, 16 * self.dcnt[i])
        for k, s in self.esem.items():
            if cnt[k]:
                sp.wait_ge(s, cnt[k])
        self.stats = dict(n_ops=len(ops), cnt=cnt)
        return self.nc

    def mm(self, out, lhsT, rhs, start=True, stop=True):
        pe = self.eng["pe"]
        return self.op("pe", lambda: pe.matmul(out, lhsT, rhs, start=start, stop=stop),
                       [lhsT, rhs] + ([] if start else [out]), [out])

    def transpose(self, out, in_, ident):
        pe = self.eng["pe"]
        return self.op("pe", lambda: pe.transpose(out, in_, ident), [in_, ident], [out])

    def act(self, out, in_, func, bias=None, scale=None, accum_out=None):
        a = self.eng["act"]
        kw = {}
        reads = [in_]
        writes = [out]
        if bias is not None:
            kw["bias"] = bias
            if not isinstance(bias, (int, float)):
                reads.append(bias)
        if scale is not None:
            kw["scale"] = scale
            if not isinstance(scale, (int, float)):
                reads.append(scale)
        if accum_out is not None:
            kw["accum_out"] = accum_out
            writes.append(accum_out)
        return self.op("act", lambda: a.activation(out, in_, func, **kw), reads, writes)

    def tt(self, out, in0, in1, op, eng="dve"):
        e = self.eng[eng]
        return self.op(eng, lambda: e.tensor_tensor(out, in0, in1, op), [in0, in1], [out])

    def ts(self, out, in0, s1, op0, s2=None, op1=None, eng="dve", accum_out=None):
        e = self.eng[eng]
        reads = [in0]
        for s in (s1, s2):
            if s is not None and not isinstance(s, (int, float)):
                reads.append(s)
        writes = [out]
        kw = {}
        if accum_out is not None:
            kw["accum_out"] = accum_out
            writes.append(accum_out)
        if op1 is None:
            return self.op(eng, lambda: e.tensor_scalar(out, in0, s1, None, op0, **kw), reads, writes)
        return self.op(eng, lambda: e.tensor_scalar(out, in0, s1, s2, op0, op1, **kw), reads, writes)

    def stt(self, out, in0, scalar, in1, op0, op1, eng="dve"):
        e = self.eng[eng]
        reads = [in0, in1]
        if not isinstance(scalar, (int, float)):
            reads.append(scalar)
        return self.op(eng, lambda: e.scalar_tensor_tensor(out, in0, scalar, in1, op0, op1), reads, [out])

    def copy(self, out, in_, eng="dve"):
        e = self.eng[eng]
        if eng == "act":
            return self.op(eng, lambda: e.copy(out, in_), [in_], [out])
        return self.op(eng, lambda: e.tensor_copy(out, in_), [in_], [out])

    def memset(self, ap, val, eng="dve"):
        e = self.eng[eng]
        return self.op(eng, lambda: e.memset(ap, val), [], [ap])

    def recip(self, out, in_):
        e = self.eng["dve"]
        return self.op("dve", lambda: e.reciprocal(out, in_), [in_], [out])

    def reduce(self, out, in_, op, axis=AX.X):
        e = self.eng["dve"]
        return self.op("dve", lambda: e.tensor_reduce(out, in_, axis, op), [in_], [out])


import numpy as np

D = 1024
NKC = 8
EPS = 1e-6


def emit_ada(fw, P, cT, layers, ada_w, ada_b, norm_g, ident2_d):
    outs = {}
    for li in layers:
        outs[li] = (fw.sb([128, 24, 2], F32, "ada_l%d" % li), fw.sb([128, 8, 2], F32, "geff_l%d" % li))
    fw.push()
    cs = fw.sb([128, NKC, 2], F32, "cs")
    fw.dma(cs[:], cT.ap().rearrange("(c p) r -> p c r", p=128))
    sc = fw.sb([128, NKC, 2], F32, "silu_c")
    fw.act(sc[:], cs[:], AF.Silu)
    id2 = fw.sb([2, 2], F32, "id2")
    fw.dma(id2[:], ident2_d.ap())
    stg = [fw.sb([128, NKC, 512], F32, "adastg%d" % i) for i in range(3)]
    row = [fw.sb([2, 3 * D], F32, "adarow%d" % i) for i in range(2)]
    q = 0
    for n, li in enumerate(layers):
        ada, geff = outs[li]
        rw = row[n % 2]
        for cchunk in range(6):
            st = stg[q % 3]
            fw.dma(st[:], ada_w.ap()[n].rearrange("(c p) n -> p c n", p=128)[:, :, cchunk * 512:(cchunk + 1) * 512],
                   q=("sp" if q % 2 == 0 else "act"))
            ps = P[q % 2]
            q += 1
            for kc in range(NKC):
                fw.mm(ps[0:2, :], sc[:, kc, :], st[:, kc, :], start=(kc == 0), stop=(kc == NKC - 1))
            fw.copy(rw[:, cchunk * 512:(cchunk + 1) * 512], ps[0:2, :])
        pt = P[2 + n % 2]
        for j in range(24):
            fw.mm(pt[:, 2 * j:2 * j + 2], rw[0:2, j * 128:(j + 1) * 128], id2[:], start=True, stop=True)
        bsb = fw.sb([128, 24], F32, "adab")
        fw.dma(bsb[:], ada_b.ap()[n])
        gsb = fw.sb([128, 8], F32, "ng")
        fw.dma(gsb[:], norm_g.ap()[n])
        psv = pt[:, 0:48].rearrange("p (j r) -> p j r", r=2)
        for r in range(2):
            fw.tt(ada[:, :, r], psv[:, :, r], bsb[:], ALU.add)
        for r in range(2):
            fw.stt(geff[:, :, r], ada[:, 8:16, r], 1.0, gsb[:], ALU.add, ALU.mult)
    fw.pop()
    return outs


def norm_mod(fw, ps_ss, xs, T, geff_r, shift_r, a_bf, sq, rstd, ones_bf, eps_t):
    fw.act(sq[:, :, :T], xs[:, :, :T], AF.Square)
    for kc in range(NKC):
        fw.mm(ps_ss[:, :T], ones_bf[:], sq[:, kc, :T], start=(kc == 0), stop=(kc == NKC - 1))
    fw.act(rstd[:, :T], ps_ss[:, :T], AF.Sqrt, bias=eps_t[:, 0:1], scale=1.0 / D)
    fw.recip(rstd[:, :T], rstd[:, :T])
    for kc in range(NKC):
        fw.tt(xs[:, kc, :T], xs[:, kc, :T], rstd[:, :T], ALU.mult)
        fw.act(a_bf[:, kc, :T], xs[:, kc, :T], AF.Identity, bias=shift_r[:, kc:kc + 1], scale=geff_r[:, kc:kc + 1])


def build_A():
    fw = FW()
    NQ, NKV, NCX = 2048, 4096, 256
    NTOK = NKV + NCX
    NQT = NQ + NCX
    T1 = 256
    xT = fw.dram("xT", [D, NTOK], F32, "ExternalInput")
    cT = fw.dram("cT", [D, 2], F32, "ExternalInput")
    ropeC = fw.dram("ropeC", [64, NTOK], F32, "ExternalInput")
    ropeS = fw.dram("ropeS", [64, NTOK], F32, "ExternalInput")
    norm_g = fw.dram("norm_g", [1, 128, 8], F32, "ExternalInput")
    ada_w = fw.dram("ada_w", [1, D, 3 * D], F32, "ExternalInput")
    ada_b = fw.dram("ada_b", [1, 128, 24], F32, "ExternalInput")
    w_in = fw.dram("mla_w_in", [D, 1728], F32, "ExternalInput")
    qng = fw.dram("q_norm_g", [128, 3], F32, "ExternalInput")
    kvng = fw.dram("kv_norm_g", [128, 2], F32, "ExternalInput")
    w_uq = fw.dram("mla_w_uq", [384, 1536], F32, "ExternalInput")
    w_ukv = fw.dram("mla_w_ukv", [256, 2048], F32, "ExternalInput")
    w_out = fw.dram("mla_w_out", [D, D], F32, "ExternalInput")
    x1T = fw.dram("x1T", [D, NQT], F32, "ExternalOutput")

    P = [fw.ps("P%d" % i) for i in range(8)]
    ones_bf = fw.sb([128, 128], BF16, "ones")
    fw.memset(ones_bf[:], 1.0)
    eps_t = fw.sb([128, 1], F32, "eps")
    fw.memset(eps_t[:], EPS)
    eps_q = fw.sb([128, 1], F32, "epsq")
    fw.memset(eps_q[:], EPS)

    ident2 = fw.dram("ident2", [2, 2], F32, "ExternalInput")
    ada = emit_ada(fw, P, cT, [0], ada_w, ada_b, norm_g, ident2)
    ada0, geff0 = ada[0]

    cqn = fw.sb([128, 3, NQT], BF16, "cqn")
    ckvn = fw.sb([128, 2, NTOK], BF16, "ckvn")
    krT = fw.sb([64, NTOK], BF16, "krT")
    sgate = fw.sb([128, 8, NQT], BF16, "sgate")
    qg = fw.sb([128, 3], F32, "qg")
    kg = fw.sb([128, 2], F32, "kg")
    fw.dma(qg[:], qng.ap())
    fw.dma(kg[:], kvng.ap())

    fw.push()
    w_in_sb = fw.sb([128, NKC, 1728], BF16, "w_in")
    for kc in range(NKC):
        fw.dma(w_in_sb[:, kc, :], w_in.ap()[kc * 128:(kc + 1) * 128, :], q="pool")
    w_krrot = fw.sb([128, NKC, 64], BF16, "w_krrot")
    fw.ts(w_krrot[:, :, 0:32], w_in_sb[:, :, 672:704], -1.0, ALU.mult)
    fw.copy(w_krrot[:, :, 32:64], w_in_sb[:, :, 640:672])
    xs2 = [fw.sb([128, NKC, T1], F32, "xs%d" % i) for i in range(2)]
    a2 = [fw.sb([128, NKC, T1], BF16, "abf%d" % i) for i in range(2)]
    sq = fw.sb([128, NKC, T1], BF16, "sq")
    rstd = fw.sb([128, T1], F32, "rstd")
    raw = fw.sb([128, 3, T1], F32, "raw")
    sq2 = fw.sb([128, 3, T1], BF16, "sq2")
    rstd2 = fw.sb([128, T1], F32, "rstd2")
    cc = fw.sb([64, T1], F32, "cc")
    ss = fw.sb([64, T1], F32, "ss")
    ra = fw.sb([64, T1], F32, "ra")
    rb = fw.sb([64, T1], F32, "rb")
    xTv = xT.ap().rearrange("(c p) n -> p c n", p=128)
    chunks = []
    for c0 in range(0, NTOK, T1):
        kind = "own" if c0 < NQ else ("other" if c0 < NKV else "ctx")
        chunks.append((c0, kind))
    pi = 0
    for ci, (c0, kind) in enumerate(chunks):
        xs = xs2[ci % 2]
        a_bf = a2[ci % 2]
        r = 1 if kind == "ctx" else 0
        fw.dma(xs[:], xTv[:, :, c0:c0 + T1])
        fw.dma(cc[:], ropeC.ap()[:, c0:c0 + T1])
        fw.dma(ss[:], ropeS.ap()[:, c0:c0 + T1])
        norm_mod(fw, P[0], xs, T1, geff0[:, :, r], ada0[:, 0:8, r], a_bf, sq, rstd, ones_bf, eps_t)

        def proj(ps_ap, wcols, m):
            for kc in range(NKC):
                fw.mm(ps_ap, wcols(kc), a_bf[:, kc, :], start=(kc == 0), stop=(kc == NKC - 1))

        def rms_group(nf, col_base, gvec, dst, dcol0):
            fw.tt(sq2[:, :nf, :], raw[:, :nf, :], raw[:, :nf, :], ALU.mult)
            for j in range(nf):
                fw.mm(P[1][:, :T1], ones_bf[:], sq2[:, j, :], start=(j == 0), stop=(j == nf - 1))
            fw.act(rstd2[:], P[1][:, :T1], AF.Sqrt, bias=eps_q[:, 0:1], scale=1.0 / (nf * 128))
            fw.recip(rstd2[:], rstd2[:])
            for j in range(nf):
                fw.stt(dst[:, j, dcol0:dcol0 + T1], raw[:, j, :], gvec[:, j:j + 1], rstd2[:], ALU.mult, ALU.mult)

        for j in range(2):
            ps = P[2 + (pi % 4)]; pi += 1
            proj(ps[:, :T1], lambda kc, j=j: w_in_sb[:, kc, 384 + j * 128:384 + (j + 1) * 128], 128)
            fw.copy(raw[:, j, :], ps[:, :T1], eng="act")
        rms_group(2, 384, kg, ckvn, c0)
        ps = P[2 + (pi % 4)]; pi += 1
        psr = P[2 + (pi % 4)]; pi += 1
        proj(ps[0:64, 0:T1], lambda kc: w_in_sb[:, kc, 640:704], 64)
        proj(psr[0:64, 0:T1], lambda kc: w_krrot[:, kc, :], 64)
        fw.tt(ra[:], ps[0:64, 0:T1], cc[:], ALU.mult)
        fw.tt(rb[:], psr[0:64, 0:T1], ss[:], ALU.mult)
        fw.tt(krT[:, c0:c0 + T1], ra[:], rb[:], ALU.add)
        if kind != "other":
            q0 = c0 if kind == "own" else NQ + (c0 - NKV)
            for j in range(3):
                ps = P[2 + (pi % 4)]; pi += 1
                proj(ps[:, :T1], lambda kc, j=j: w_in_sb[:, kc, j * 128:(j + 1) * 128], 128)
                fw.copy(raw[:, j, :], ps[:, :T1], eng="act")
            rms_group(3, 0, qg, cqn, q0)
            for j in range(8):
                ps = P[2 + (pi % 4)]; pi += 1
                proj(ps[:, :T1], lambda kc, j=j: w_in_sb[:, kc, 704 + j * 128:704 + (j + 1) * 128], 128)
                fw.act(sgate[:, j, q0:q0 + T1], ps[:, :T1], AF.Silu)
    fw.pop()

    fw.push()
    w_uq_sb = fw.sb([128, 3, 1536], BF16, "w_uq")
    for kc in range(3):
        fw.dma(w_uq_sb[:, kc, :], w_uq.ap()[kc * 128:(kc + 1) * 128, :], q="pool")
    w_ukv_sb = fw.sb([128, 2, 2048], BF16, "w_ukv")
    for kc in range(2):
        fw.dma(w_ukv_sb[:, kc, :], w_ukv.ap()[kc * 128:(kc + 1) * 128, :], q="pool")
    uqv = w_uq_sb[:].rearrange("p k (h f) -> p k h f", f=192)
    w_uqrot = fw.sb([128, 3, 8, 64], BF16, "w_uqrot")
    fw.ts(w_uqrot[:, :, :, 0:32], uqv[:, :, :, 160:192], -1.0, ALU.mult)
    fw.copy(w_uqrot[:, :, :, 32:64], uqv[:, :, :, 128:160])
    TQ = 512
    knT = [fw.sb([128, NTOK], BF16, "knT%d" % i) for i in range(2)]
    vh = [fw.sb([128, NTOK // 128, 128], BF16, "vh%d" % i) for i in range(2)]
    qnT = [fw.sb([128, NQT], BF16, "qnT%d" % i) for i in range(2)]
    qrT = [fw.sb([64, NQT], BF16, "qrT%d" % i) for i in range(2)]
    pT = [fw.sb([128, TQ], BF16, "pT%d" % i) for i in range(3)]
    ccq = fw.sb([64, NQT], F32, "ccq")
    ssq = fw.sb([64, NQT], F32, "ssq")
    fw.dma(ccq[:, 0:NQ], ropeC.ap()[:, 0:NQ]); fw.dma(ccq[:, NQ:NQT], ropeC.ap()[:, NKV:NTOK])
    fw.dma(ssq[:, 0:NQ], ropeS.ap()[:, 0:NQ]); fw.dma(ssq[:, NQ:NQT], ropeS.ap()[:, NKV:NTOK])
    ra = fw.sb([64, TQ], F32, "ra2")
    rb = fw.sb([64, TQ], F32, "rb2")
    rden = fw.sb([128, TQ], F32, "rden")
    accD = fw.sb([128, TQ], F32, "accD")
    accP = fw.sb([128, TQ], F32, "accP")
    ones_f = fw.sb([128, 128], F32, "onesf")
    fw.memset(ones_f[:], 1.0)
    onorm = fw.sb([128, TQ], F32, "onorm")
    sc_att = float(192 ** -0.5)
    NKT = NTOK // 128
    ev = 0
    pcount = 0
    for h in range(8):
        kn, v_h, qn, qr = knT[h % 2], vh[h % 2], qnT[h % 2], qrT[h % 2]
        for c0 in range(0, NTOK, 512):
            T = min(512, NTOK - c0)
            ps = P[6 + (ev % 2)]; ev += 1
            for kc in range(2):
                fw.mm(ps[:, :T], w_ukv_sb[:, kc, h * 256:h * 256 + 128], ckvn[:, kc, c0:c0 + T], start=(kc == 0), stop=(kc == 1))
            fw.copy(kn[:, c0:c0 + T], ps[:, :T], eng=("act" if ev % 2 else "dve"))
        for t0 in range(0, NKT, 4):
            nt = min(4, NKT - t0)
            ps = P[6 + (ev % 2)]; ev += 1
            for t in range(nt):
                for kc in range(2):
                    fw.mm(ps[:, t * 128:(t + 1) * 128], ckvn[:, kc, (t0 + t) * 128:(t0 + t + 1) * 128],
                          w_ukv_sb[:, kc, h * 256 + 128:h * 256 + 256], start=(kc == 0), stop=(kc == 1))
            fw.copy(v_h[:, t0:t0 + nt, :], ps[:, :nt * 128].rearrange("p (t f) -> p t f", f=128), eng=("act" if ev % 2 else "dve"))
        for c0 in range(0, NQT, 512):
            T = min(512, NQT - c0)
            ps = P[6 + (ev % 2)]; ev += 1
            for kc in range(3):
                fw.mm(ps[:, :T], w_uq_sb[:, kc, h * 192:h * 192 + 128], cqn[:, kc, c0:c0 + T], start=(kc == 0), stop=(kc == 2))
            fw.copy(qn[:, c0:c0 + T], ps[:, :T], eng=("act" if ev % 2 else "dve"))
            ps = P[6 + (ev % 2)]; ev += 1
            for kc in range(3):
                fw.mm(ps[0:64, :T], w_uq_sb[:, kc, h * 192 + 128:h * 192 + 192], cqn[:, kc, c0:c0 + T], start=(kc == 0), stop=(kc == 2))
            fw.tt(ra[:, :T], ps[0:64, :T], ccq[:, c0:c0 + T], ALU.mult)
            ps = P[6 + (ev % 2)]; ev += 1
            for kc in range(3):
                fw.mm(ps[0:64, :T], w_uqrot[:, kc, h, :], cqn[:, kc, c0:c0 + T], start=(kc == 0), stop=(kc == 2))
            fw.tt(rb[:, :T], ps[0:64, :T], ssq[:, c0:c0 + T], ALU.mult)
            fw.tt(qr[:, c0:c0 + T], ra[:, :T], rb[:, :T], ALU.add)
        qchunks = [(c0, 512, list(range(NKT))) for c0 in range(0, NQ, 512)] + [(NQ, NCX, list(range(NKV // 128, NKT)))]
        for qi, (q0, T, ktiles) in enumerate(qchunks):
            po = P[2 + (pcount % 2)]
            pd = P[4 + (pcount % 2)]
            pcount += 1
            def s_mm(ki):
                kt = ktiles[ki]
                ps = P[ki % 2]
                fw.mm(ps[:, :T], kn[:, kt * 128:(kt + 1) * 128], qn[:, q0:q0 + T], start=True, stop=False)
                fw.mm(ps[:, :T], krT[:, kt * 128:(kt + 1) * 128], qr[:, q0:q0 + T], start=False, stop=True)
            s_mm(0)
            for ki, kt in enumerate(ktiles):
                if ki + 1 < len(ktiles):
                    s_mm(ki + 1)
                ps = P[ki % 2]
                p_t = pT[ki % 3]
                fw.act(p_t[:, :T], ps[:, :T], AF.Exp, scale=sc_att)
                fw.mm(po[:, :T], v_h[:, kt, :], p_t[:, :T], start=(ki == 0), stop=(ki == len(ktiles) - 1))
                fw.mm(pd[:, :T], ones_bf[:], p_t[:, :T], start=(ki == 0), stop=(ki == len(ktiles) - 1))
            fw.recip(rden[:, :T], pd[:, :T])
            fw.tt(onorm[:, :T], po[:, :T], rden[:, :T], ALU.mult)
            fw.tt(sgate[:, h, q0:q0 + T], onorm[:, :T], sgate[:, h, q0:q0 + T], ALU.mult)
    fw.pop()

    fw.push()
    w_out_sb = fw.sb([128, NKC, D], BF16, "w_out")
    for kc in range(NKC):
        fw.dma(w_out_sb[:, kc, :], w_out.ap()[kc * 128:(kc + 1) * 128, :], q="pool")
    xs2 = [fw.sb([128, NKC, 512], F32, "xr%d" % i) for i in range(2)]
    x1v = x1T.ap().rearrange("(c p) n -> p c n", p=128)
    ochunks = [(c0, 512, c0, 0) for c0 in range(0, NQ, 512)] + [(NQ, NCX, NKV, 1)]
    for ci, (q0, T, xc0, r) in enumerate(ochunks):
        xs = xs2[ci % 2]
        fw.dma(xs[:, :, :T], xTv[:, :, xc0:xc0 + T])
        for fj in range(8):
            ps = P[fj % 4]
            for kc in range(NKC):
                fw.mm(ps[:, :T], w_out_sb[:, kc, fj * 128:(fj + 1) * 128], sgate[:, kc, q0:q0 + T], start=(kc == 0), stop=(kc == NKC - 1))
            fw.stt(xs[:, fj, :T], ps[:, :T], ada0[:, 16 + fj:17 + fj, r], xs[:, fj, :T], ALU.mult, ALU.add)
        fw.dma(x1v[:, :, q0:q0 + T], xs[:, :, :T])
    fw.pop()
    return fw


def host_inputs_A(inp, b, h):
    L = 4096
    own = slice(h * 2048, (h + 1) * 2048)
    oth = slice((1 - h) * 2048, (2 - h) * 2048)
    x = inp["x"][b]
    xT = np.concatenate([x[own].T, x[oth].T, inp["ctx"][b].T], axis=1)
    cos, sin = rope_tables(L)
    C = np.concatenate([cos[:, own], cos[:, oth], np.ones((64, 256), np.float32)], axis=1)
    S = np.concatenate([sin[:, own], sin[:, oth], np.zeros((64, 256), np.float32)], axis=1)
    return {
        "xT": np.ascontiguousarray(xT, np.float32),
        "cT": np.ascontiguousarray(np.stack([inp["c"][b], inp["c_ctx"]], axis=1), np.float32),
        "ropeC": np.ascontiguousarray(C), "ropeS": np.ascontiguousarray(S),
        "norm_g": np.ascontiguousarray(inp["norm_g"].reshape(4, 8, 128).transpose(0, 2, 1)[0:1]),
        "ada_w": np.ascontiguousarray(inp["ada_w"][0:1]),
        "ada_b": np.ascontiguousarray(inp["ada_b"].reshape(4, 24, 128).transpose(0, 2, 1)[0:1]),
        "ident2": np.eye(2, dtype=np.float32),
        "mla_w_in": inp["mla_w_in"][0],
        "q_norm_g": np.ascontiguousarray(inp["mla_q_norm_g"][0].reshape(3, 128).T),
        "kv_norm_g": np.ascontiguousarray(inp["mla_kv_norm_g"][0].reshape(2, 128).T),
        "mla_w_uq": inp["mla_w_uq"][0], "mla_w_ukv": inp["mla_w_ukv"][0], "mla_w_out": inp["mla_w_out"][0],
    }


_ROPE = {}


def rope_tables(L):
    if L not in _ROPE:
        rows = L // 64
        row = np.repeat(np.arange(rows, dtype=np.float32), 64)
        col = np.tile(np.arange(64, dtype=np.float32), rows)
        inv = (np.float32(10000.0) ** (-np.arange(16, dtype=np.float32) / np.float32(16))).astype(np.float32)
        ang = np.concatenate([row[:, None] * inv, col[:, None] * inv], axis=-1)
        cos = np.cos(ang).astype(np.float32).T
        sin = np.sin(ang).astype(np.float32).T
        _ROPE[L] = (np.concatenate([cos, cos], 0), np.concatenate([sin, sin], 0))
    return _ROPE[L]


import numpy as np


def build_A2():
    fw = FW()
    NQ, NCX = 2048, 256
    WL = NQ + 2
    WC = NCX + 2
    x1T = fw.dram("x1T", [D, WL], F32, "ExternalInput")
    xc1T = fw.dram("xc1T", [D, NCX], F32, "ExternalInput")
    hmask = fw.dram("hmask", [128, 2], F32, "ExternalInput")
    cT = fw.dram("cT", [D, 2], F32, "ExternalInput")
    norm_g = fw.dram("norm_g", [1, 128, 8], F32, "ExternalInput")
    ada_w = fw.dram("ada_w", [1, D, 3 * D], F32, "ExternalInput")
    ada_b = fw.dram("ada_b", [1, 128, 24], F32, "ExternalInput")
    w_in = fw.dram("hy_w_in", [D, 4096], F32, "ExternalInput")
    conv_w = fw.dram("hy_conv_w", [3, 3072], F32, "ExternalInput")
    conv_b = fw.dram("hy_conv_b", [1, 3072], F32, "ExternalInput")
    U = fw.dram("U", [NQ, 4096], BF16, "ExternalOutput")
    Uc = fw.dram("Uc", [NCX, 4096], BF16, "ExternalOutput")

    P = [fw.ps("P%d" % i) for i in range(8)]
    ones_bf = fw.sb([128, 128], BF16, "ones")
    fw.memset(ones_bf[:], 1.0)
    eps_t = fw.sb([128, 1], F32, "eps")
    fw.memset(eps_t[:], EPS)
    ident2 = fw.dram("ident2", [2, 2], F32, "ExternalInput")
    ada = emit_ada(fw, P, cT, [1], ada_w, ada_b, norm_g, ident2)
    ada1, geff1 = ada[1]
    hm = fw.sb([128, 2], F32, "hm")
    fw.dma(hm[:], hmask.ap())

    aL = fw.sb([128, NKC, WL], BF16, "aL")
    aC = fw.sb([128, NKC, WC], BF16, "aC")
    fw.memset(aC[:, :, 0:1], 0.0)
    fw.memset(aC[:, :, WC - 1:WC], 0.0)
    fw.push()
    T1 = 512
    xs2 = [fw.sb([128, NKC, T1], F32, "xs%d" % i) for i in range(2)]
    ab = fw.sb([128, NKC, T1], BF16, "ab")
    sq = fw.sb([128, NKC, T1], BF16, "sq")
    rstd = fw.sb([128, T1], F32, "rstd")
    x1v = x1T.ap().rearrange("(c p) n -> p c n", p=128)
    xcv = xc1T.ap().rearrange("(c p) n -> p c n", p=128)
    jobs = [(x1v, c0, min(T1, WL - c0), aL, c0, 0) for c0 in range(0, WL, T1)] + [(xcv, 0, NCX, aC, 1, 1)]
    for ci, (src, c0, T, dst, d0, r) in enumerate(jobs):
        xs = xs2[ci % 2]
        fw.dma(xs[:, :, :T], src[:, :, c0:c0 + T])
        norm_mod(fw, P[0], xs, T, geff1[:, :, r], ada1[:, 0:8, r], dst[:, :, d0:d0 + T], sq, rstd, ones_bf, eps_t)
    fw.ts(aL[:, :, 0:1], aL[:, :, 0:1], hm[:, 0:1], ALU.mult)
    fw.ts(aL[:, :, WL - 1:WL], aL[:, :, WL - 1:WL], hm[:, 1:2], ALU.mult)
    fw.pop()
    fw.push()
    w_sb = fw.sb([128, NKC, 4096], BF16, "w_in")
    for kc in range(NKC):
        fw.dma(w_sb[:, kc, :], w_in.ap()[kc * 128:(kc + 1) * 128, :], q="pool")
    wt = [fw.sb([128, 3, NKC, 512], BF16, "wt%d" % i) for i in range(2)]
    cwb = [fw.sb([128, 3, 512], F32, "cwb%d" % i) for i in range(2)]
    cbb = [fw.sb([128, 512], F32, "cbb%d" % i) for i in range(2)]
    ost = [fw.sb([128, 512], BF16, "ost%d" % i) for i in range(3)]
    tiles = [(aL, t0, U, t0) for t0 in range(0, NQ, 128)] + [(aC, t0, Uc, t0) for t0 in range(0, NCX, 128)]
    n = 0
    for cc in range(8):
        c0 = cc * 512
        if cc < 6:
            w3 = wt[cc % 2]
            cw = cwb[cc % 2]
            cb = cbb[cc % 2]
            for j in range(3):
                fw.dma(cw[:, j, :], conv_w.ap()[j:j + 1, c0:c0 + 512].partition_broadcast(128))
            fw.dma(cb[:], conv_b.ap()[0:1, c0:c0 + 512].partition_broadcast(128))
            for j in range(3):
                for kc in range(NKC):
                    fw.tt(w3[:, j, kc, :], w_sb[:, kc, c0:c0 + 512], cw[:, j, :], ALU.mult,
                          eng=("pool" if (kc % 2) else "dve"))
        for (asrc, t0, dst, r0) in tiles:
            ps = P[1 + (n % 4)]
            o = ost[n % 3]
            n += 1
            if cc < 6:
                k = 0
                for j in range(3):
                    for kc in range(NKC):
                        fw.mm(ps[:], asrc[:, kc, t0 + j:t0 + j + 128], w3[:, j, kc, :], start=(k == 0), stop=(k == 23))
                        k += 1
                fw.tt(o[:], ps[:], cb[:], ALU.add)
            else:
                for kc in range(NKC):
                    fw.mm(ps[:], asrc[:, kc, t0 + 1:t0 + 129], w_sb[:, kc, c0:c0 + 512], start=(kc == 0), stop=(kc == NKC - 1))
                fw.copy(o[:], ps[:], eng="act")
            fw.dma(dst.ap()[r0:r0 + 128, c0:c0 + 512], o[:])
    fw.pop()
    return fw


def host_inputs_A2(inp, x1T_full, xc1T, b, h):
    s = h * 2048
    cols = np.zeros((D, 2050), np.float32)
    lo, hi = s - 1, s + 2049
    a, e = max(lo, 0), min(hi, 4096)
    cols[:, a - lo:e - lo] = x1T_full[:, a:e]
    hm = np.zeros((128, 2), np.float32)
    hm[:, 0] = 1.0 if lo >= 0 else 0.0
    hm[:, 1] = 1.0 if hi <= 4096 else 0.0
    return {
        "x1T": cols, "xc1T": np.ascontiguousarray(xc1T), "hmask": hm,
        "cT": np.ascontiguousarray(np.stack([inp["c"][b], inp["c_ctx"]], axis=1), np.float32),
        "norm_g": np.ascontiguousarray(inp["norm_g"].reshape(4, 8, 128).transpose(0, 2, 1)[1:2]),
        "ada_w": np.ascontiguousarray(inp["ada_w"][1:2]),
        "ada_b": np.ascontiguousarray(inp["ada_b"].reshape(4, 24, 128).transpose(0, 2, 1)[1:2]),
        "ident2": np.eye(2, dtype=np.float32),
        "hy_w_in": inp["hy_w_in"][0], "hy_conv_w": inp["hy_conv_w"][0],
        "hy_conv_b": inp["hy_conv_b"][0].reshape(1, 3072),
    }


import math
import numpy as np

PI = float(np.pi)


def sin_act(fw, dst, arg, tmp):
    for _ in range(2):
        fw.ts(tmp, arg, PI, ALU.is_gt, -2.0 * PI, ALU.mult)
        fw.tt(arg, arg, tmp, ALU.add)
        fw.ts(tmp, arg, -PI, ALU.is_lt, 2.0 * PI, ALU.mult)
        fw.tt(arg, arg, tmp, ALU.add)
    fw.act(dst, arg, AF.Sin)


def gen_filter(fw, P, L, tabs, mlp, Kd, bias_sb):
    hdn_d, hdnrev_d, dec_d, decrev_d = tabs
    w1, w2, w3, fr, bfr, woA, woB = mlp
    H = L // 2
    T = min(512, H)
    CH = min(2048, H)
    fw.push()
    Kx = fw.sb([128, 2, 2 * L], F32, "Kx")
    fw.memset(Kx[:, :, 2 * L - 1:2 * L], 0.0)
    h0 = fw.sb([66, H], F32, "h0")
    dc = fw.sb([128, L], F32, "dc")
    arg = fw.sb([128, H], F32, "arg")
    tmp = fw.sb([128, H], F32, "tmpf")
    hh = [fw.sb([128, H], F32, "hh%d" % i) for i in range(2)]
    kt = fw.sb([128, T], F32, "ktmp")
    n = 0
    for rev in (1, 0):
        fw.dma(h0[:], (hdnrev_d if rev else hdn_d).ap())
        for c0 in range(0, L, 2048):
            c1 = min(L, c0 + 2048)
            fw.dma(dc[:, c0:c1], (decrev_d if rev else dec_d).ap()[:, c0:c1], q="act")
        src = h0
        for k, w in enumerate((w1, w2, w3)):
            kin = 66 if k == 0 else 128
            dst = hh[k % 2]
            for t0 in range(0, H, T):
                ps = P[n % 2]; n += 1
                fw.mm(ps[:, :T], w[0:kin, :], src[0:kin, t0:t0 + T], start=True, stop=True)
                fw.ts(arg[:, t0:t0 + T], ps[:, :T], fr[:, k:k + 1], ALU.mult, bfr[:, k:k + 1], ALU.add)
            for c0 in range(0, H, CH):
                sin_act(fw, dst[:, c0:c0 + CH], arg[:, c0:c0 + CH], tmp[:, c0:c0 + CH])
            src = dst
        for o in range(2):
            for half, wo in enumerate((woA, woB)):
                for t0 in range(0, H, T):
                    tg = half * H + t0
                    ps = P[2 + n % 2]; n += 1
                    d = dc[:, tg:tg + T]
                    fw.mm(ps[:, :T], wo[:, o * 2 + rev, :], src[:, t0:t0 + T], start=True, stop=True)
                    if rev:
                        fw.tt(Kx[:, o, tg:tg + T], ps[:, :T], d, ALU.mult)
                    else:
                        if tg == 0:
                            fw.tt(kt[:], ps[:, :T], d, ALU.mult)
                            fw.copy(Kx[:, o, L:L + T - 1], kt[:, 1:T])
                            fw.tt(Kx[:, o, L - 1:L], Kx[:, o, L - 1:L], kt[:, 0:1], ALU.add)
                            fw.tt(Kx[:, o, L - 1:L], Kx[:, o, L - 1:L], bias_sb[:, o:o + 1], ALU.add)
                        else:
                            fw.tt(Kx[:, o, L - 1 + tg:L - 1 + tg + T], ps[:, :T], d, ALU.mult)
    kb = [fw.sb([128, 2048], BF16, "Kb%d" % i) for i in range(2)]
    k = 0
    for o in range(2):
        for m0 in range(0, 2 * L, 2048):
            m1 = min(2 * L, m0 + 2048)
            b = kb[k % 2]
            fw.copy(b[:, :m1 - m0], Kx[:, o, m0:m1], eng=("act" if k % 2 else "dve"))
            fw.dma(Kd.ap()[:, o, m0:m1], b[:, :m1 - m0], q=("sp" if k % 2 else "act"))
            k += 1
    fw.pop()


def long_conv_stage(fw, P, NB, L, Kd, Uv, J_bf, ident_bf, Pbf, ZgT, name):
    NBI = 4 * NB
    ND = 2 * NB - 1
    fw.push()
    znat = fw.sb([128, NBI, 128], BF16, name + "znat")
    zr = fw.sb([128, 128, NBI], BF16, name + "zr")
    gA = fw.sb([128, NBI, 128], BF16, name + "gA")
    gB = fw.sb([128, NBI, 128], BF16, name + "gB")
    batched = (ND * 128 * 128 * 2) <= 100 * 1024
    if batched:
        Tall = fw.sb([128, 128, ND * 128], BF16, name + "Tall")
    else:
        Tt = [fw.sb([128, ND * 128], BF16, name + "T%d" % i) for i in range(4)]
    ost = [fw.sb([128, 512], F32, name + "ost%d" % i) for i in range(2)]
    def load_group(dst, g):
        step = max(1, NBI // 2)
        for k, n0 in enumerate(range(0, NBI, step)):
            fw.dma(dst[:, n0:n0 + step, :], Uv.ap()[:, g, n0:n0 + step, :], q=("sp" if k % 2 == 0 else "act"))

    def reverse_rows():
        cpb = 512 // 128
        k = 0
        for n0 in range(0, NBI, cpb):
            nn = min(cpb, NBI - n0)
            ps = P[4 + (k % 2)]
            fw.mm(ps[:, :nn * 128], J_bf[:], znat[:, n0:n0 + nn, :], start=True, stop=True)
            fw.copy(zr[:, :, n0:n0 + nn], ps[:, :nn * 128].rearrange("p (n c) -> p c n", c=128),
                    eng=("act" if k % 2 else "dve"))
            k += 1

    load_group(znat, 0)
    load_group(gA, 1)
    reverse_rows()
    dorder = [0] + [d for k in range(1, NB) for d in (k, -k)]
    ti = 0
    for o in range(2):
        if o == 1:
            load_group(gA, 2)
            load_group(gB, 3)
            for n0 in range(0, NBI, 16):
                n1 = min(NBI, n0 + 16)
                fw.act(gB[:, n0:n1, :], gB[:, n0:n1, :], AF.Silu)
                fw.tt(gA[:, n0:n1, :], gA[:, n0:n1, :], gB[:, n0:n1, :], ALU.mult)
            reverse_rows()
        if batched:
            for cg in range(0, 128, 32):
                src = bass.AP(Kd, (cg * 2 + o) * 2 * L, [[1, 128], [2 * 2 * L, 32], [1, ND * 128]])
                fw.dma(Tall[:, cg:cg + 32, :], src, q=("sp" if (cg // 32) % 2 == 0 else "act"))
        for c0 in range(0, 128, 4):
            ps = P[2 + ((c0 // 4) % 2)]
            for s in range(4):
                c = c0 + s
                if batched:
                    T = Tall[:, c, :]
                else:
                    T = Tt[ti % 4]
                    src = bass.AP(Kd, (c * 2 + o) * 2 * L, [[1, 128], [1, ND * 128]])
                    fw.dma(T[:], src, q=("sp" if ti % 2 == 0 else "act"))
                    ti += 1
                outv = ps[:, s * NBI:(s + 1) * NBI].rearrange("p (b i) -> p b i", b=4)
                inv = zr[:, c, :].rearrange("p (b i) -> p b i", b=4)
                for di, d in enumerate(dorder):
                    lo = max(0, d)
                    n = NB - abs(d)
                    fw.mm(outv[:, :, lo:lo + n], T[:, (d + NB - 1) * 128:(d + NB) * 128], inv[:, :, lo - d:lo - d + n],
                          start=(di == 0), stop=(di == ND - 1))
            pv = ps[:, 0:4 * NBI].rearrange("p (s n) -> p n s", s=4)
            fw.tt(znat[:, :, c0:c0 + 4], pv, gA[:, :, c0:c0 + 4], ALU.mult)
    k = 0
    for n0 in range(0, NBI, 4):
        nn = min(4, NBI - n0)
        for j in range(nn):
            fw.transpose(Pbf[:, j * 128:(j + 1) * 128], znat[:, n0 + j, :], ident_bf[:])
        o = ost[k % 2]
        k += 1
        fw.copy(o[:, :nn * 128], Pbf[:, :nn * 128])
        fw.dma(ZgT.ap()[:, n0 * 128:(n0 + nn) * 128], o[:, :nn * 128])
    fw.pop()


def build_B(stages=("fc", "fl", "cc", "cl")):
    fw = FW()
    L, LC = 4096, 256
    Uv = fw.dram("Uv", [128, 4, 4 * (L // 128), 128], BF16, "ExternalInput")
    Ucv = fw.dram("Ucv", [128, 4, 4 * (LC // 128), 128], BF16, "ExternalInput")
    f_w1 = fw.dram("f_w1", [66, 128], F32, "ExternalInput")
    f_w2 = fw.dram("f_w2", [128, 128], F32, "ExternalInput")
    f_w3 = fw.dram("f_w3", [128, 128], F32, "ExternalInput")
    f_b = fw.dram("f_b", [128, 3], F32, "ExternalInput")
    f_fr = fw.dram("f_fr", [128, 3], F32, "ExternalInput")
    f_wo = fw.dram("f_wo", [2, 128, 4, 128], F32, "ExternalInput")
    hy_bias = fw.dram("hy_bias", [128, 2], F32, "ExternalInput")
    tabsL = [fw.dram(n, s, F32, "ExternalInput") for n, s in
             (("hdnL", [66, L // 2]), ("hdnLr", [66, L // 2]), ("decL", [128, L]), ("decLr", [128, L]))]
    tabsC = [fw.dram(n, s, F32, "ExternalInput") for n, s in
             (("hdnC", [66, LC // 2]), ("hdnCr", [66, LC // 2]), ("decC", [128, LC]), ("decCr", [128, LC]))]
    Jd = fw.dram("Jmat", [128, 128], F32, "ExternalInput")
    Id = fw.dram("Imat", [128, 128], F32, "ExternalInput")
    ZgT = fw.dram("ZgT", [128, 4 * L], F32, "ExternalOutput")
    ZgcT = fw.dram("ZgcT", [128, 4 * LC], F32, "ExternalOutput")
    KdL = fw.dram("KdL", [128, 2, 2 * L], BF16, "ExternalOutput")
    KdC = fw.dram("KdC", [128, 2, 2 * LC], BF16, "ExternalOutput")

    P = [fw.ps("P%d" % i) for i in range(6)]
    Pbf = fw.ps("Pbf", (128, 1024), BF16)
    J_bf = fw.sb([128, 128], BF16, "J")
    I_bf = fw.sb([128, 128], BF16, "I")
    fw.dma(J_bf[:], Jd.ap(), q="pool")
    fw.dma(I_bf[:], Id.ap(), q="pool")
    w1 = fw.sb([66, 128], F32, "w1"); fw.dma(w1[:], f_w1.ap())
    w2 = fw.sb([128, 128], F32, "w2"); fw.dma(w2[:], f_w2.ap())
    w3 = fw.sb([128, 128], F32, "w3"); fw.dma(w3[:], f_w3.ap())
    fb = fw.sb([128, 3], F32, "fb"); fw.dma(fb[:], f_b.ap())
    fr = fw.sb([128, 3], F32, "fr"); fw.dma(fr[:], f_fr.ap())
    woA = fw.sb([128, 4, 128], F32, "woA"); fw.dma(woA[:], f_wo.ap()[0])
    woB = fw.sb([128, 4, 128], F32, "woB"); fw.dma(woB[:], f_wo.ap()[1])
    bias_sb = fw.sb([128, 2], F32, "hbias"); fw.dma(bias_sb[:], hy_bias.ap())
    bfr = fw.sb([128, 3], F32, "bfr")
    fw.tt(bfr[:], fb[:], fr[:], ALU.mult)
    mlp = (w1, w2, w3, fr, bfr, woA, woB)
    if "fc" in stages:
        gen_filter(fw, P, LC, tabsC, mlp, KdC, bias_sb)
    if "fl" in stages:
        gen_filter(fw, P, L, tabsL, mlp, KdL, bias_sb)
    if "cc" in stages:
        long_conv_stage(fw, P, LC // 128, LC, KdC, Ucv, J_bf, I_bf, Pbf, ZgcT, "c_")
    if "cl" in stages:
        long_conv_stage(fw, P, L // 128, L, KdL, Uv, J_bf, I_bf, Pbf, ZgT, "l_")
    return fw


_TAB = {}


def hyena_tables(Lx):
    if Lx not in _TAB:
        f32 = np.float32
        t = np.linspace(0.0, 1.0, Lx, dtype=f32)[:, None]
        wpos = (f32(2.0 * math.pi / Lx)) * np.arange(Lx, dtype=f32)[:, None]
        bands = np.linspace(1e-4, 15, 16, dtype=f32)[None, :]
        hdn = np.concatenate([t, np.cos(bands * wpos), -np.sin(bands * wpos)], axis=-1).astype(f32).T
        deltas = np.abs(np.linspace(math.log(1e-2) / 0.3, math.log(1e-2) / 1.5, 1024, dtype=f32))
        dec = np.exp(-t * deltas).astype(f32).T
        _TAB[Lx] = (np.ascontiguousarray(hdn), np.ascontiguousarray(hdn[:, ::-1]), dec)
    return _TAB[Lx]


def _bdiag(w):
    k, m = w.shape
    o = np.zeros((2 * k, 2 * m), np.float32)
    o[:k, :m] = w
    o[k:, m:] = w
    return o


def _wo_halves(wo):
    o = np.zeros((2, 128, 4, 128), np.float32)
    o[0, :64] = wo
    o[1, 64:] = wo
    return o


def _stack2(t):
    h = t.shape[1] // 2
    return np.ascontiguousarray(np.concatenate([t[:, :h], t[:, h:]], axis=0))


def host_inputs_B(inp, U_all, Uc_all, cb):
    cs = slice(cb * 128, (cb + 1) * 128)
    Uv = U_all.reshape(4, 32, 128, 4, 1024)[..., cs].transpose(2, 3, 0, 1, 4).reshape(128, 4, 128, 128)
    Ucv = Uc_all.reshape(4, 2, 128, 4, 1024)[..., cs].transpose(2, 3, 0, 1, 4).reshape(128, 4, 8, 128)
    hL, hLr, dL = hyena_tables(4096)
    hC, hCr, dC = hyena_tables(256)
    eye = np.eye(128, dtype=np.float32)
    return {
        "Uv": np.ascontiguousarray(Uv), "Ucv": np.ascontiguousarray(Ucv),
        "f_w1": _bdiag(inp["hy_filt_w_in"][0]), "f_w2": _bdiag(inp["hy_filt_w_hid"][0][0]), "f_w3": _bdiag(inp["hy_filt_w_hid"][0][1]),
        "f_b": np.ascontiguousarray(np.concatenate([inp["hy_filt_b"][0].T] * 2, axis=0)),
        "f_fr": np.ascontiguousarray(np.concatenate([inp["hy_filt_freq"][0].T] * 2, axis=0)),
        "f_wo": _wo_halves(inp["hy_filt_w_out"][0].reshape(64, 4, 1024)[:, :, cs]),
        "hy_bias": np.ascontiguousarray(inp["hy_bias"][0][:, cs].T),
        "hdnL": _stack2(hL), "hdnLr": _stack2(hLr), "decL": np.ascontiguousarray(dL[cs]), "decLr": np.ascontiguousarray(dL[cs][:, ::-1]),
        "hdnC": _stack2(hC), "hdnCr": _stack2(hCr), "decC": np.ascontiguousarray(dC[cs]), "decCr": np.ascontiguousarray(dC[cs][:, ::-1]),
        "Jmat": np.ascontiguousarray(eye[::-1]), "Imat": eye,
    }


import numpy as np

W = 2432
NCX = 256
WT = W + NCX
Q0, Q1 = 128, 2304
QW = Q1 - Q0
O0, O1 = 192, 2240
U0 = O0 - 15
NU = 2048 + 30


def build_C(debug=False, stop=None):
    fw = FW()
    kindo = "ExternalOutput"
    x1T = fw.dram("x1T", [D, W], F32, "ExternalInput")
    zgT = fw.dram("zgT", [D, W], F32, "ExternalInput")
    xc1T = fw.dram("xc1T", [D, NCX], F32, "ExternalInput")
    zgcT = fw.dram("zgcT", [D, NCX], F32, "ExternalInput")
    cT = fw.dram("cT", [D, 2], F32, "ExternalInput")
    norm_g = fw.dram("norm_g", [3, 128, 8], F32, "ExternalInput")
    ada_w = fw.dram("ada_w", [3, D, 3 * D], F32, "ExternalInput")
    ada_b = fw.dram("ada_b", [3, 128, 24], F32, "ExternalInput")
    hy_w_out = fw.dram("hy_w_out", [D, D], F32, "ExternalInput")
    swa_w_in = fw.dram("swa_w_in", [D, 2560], F32, "ExternalInput")
    swa_sink = fw.dram("swa_sink", [1, 16], F32, "ExternalInput")
    swa_w_out = fw.dram("swa_w_out", [D, D], F32, "ExternalInput")
    cf_w_in = fw.dram("cf_w_in", [D, 3072], F32, "ExternalInput")
    cf_dw_w = fw.dram("cf_dw_w", [128, 8, 31], F32, "ExternalInput")
    cf_vec = fw.dram("cf_vec", [128, 4, 8], F32, "ExternalInput")
    cf_w_out = fw.dram("cf_w_out", [D, D], F32, "ExternalInput")
    ropeC = fw.dram("ropeC", [128, WT], F32, "ExternalInput")
    ropeS = fw.dram("ropeS", [128, WT], F32, "ExternalInput")
    kvalid_d = fw.dram("kvalid", [128, 19], F32, "ExternalInput")
    cmask_d = fw.dram("cmask", [128, NU], F32, "ExternalInput")
    tri_d = fw.dram("negm", [128, 2, 128], F32, "ExternalInput")
    ident_d = fw.dram("ident", [128, 128], F32, "ExternalInput")
    outT = fw.dram("outT", [D, 2048], F32, "ExternalOutput")
    X2 = fw.dram("X2", [D, W], F32, kindo if debug else "Internal")
    X3 = fw.dram("X3", [D, QW], F32, kindo if debug else "Internal")

    P = [fw.ps("P%d" % i) for i in range(8)]
    ones_bf = fw.sb([128, 128], BF16, "ones")
    fw.memset(ones_bf[:], 1.0)
    ones_f = fw.sb([128, 128], F32, "onesf")
    fw.memset(ones_f[:], 1.0)
    eps_t = fw.sb([128, 1], F32, "eps")
    fw.memset(eps_t[:], EPS)
    ident2 = fw.dram("ident2", [2, 2], F32, "ExternalInput")
    ada = emit_ada(fw, P, cT, [1, 2, 3], ada_w, ada_b, norm_g, ident2)
    (ada1, geff1), (ada2, geff2), (ada3, geff3) = ada[1], ada[2], ada[3]
    cfv = fw.sb([128, 4, 8], F32, "cfv")
    fw.dma(cfv[:], cf_vec.ap())
    x1v = x1T.ap().rearrange("(c p) n -> p c n", p=128)
    zgv = zgT.ap().rearrange("(c p) n -> p c n", p=128)
    xcv = xc1T.ap().rearrange("(c p) n -> p c n", p=128)
    zgcv = zgcT.ap().rearrange("(c p) n -> p c n", p=128)
    X2v = X2.ap().rearrange("(c p) n -> p c n", p=128)
    X3v = X3.ap().rearrange("(c p) n -> p c n", p=128)
    outv = outT.ap().rearrange("(c p) n -> p c n", p=128)

    og = fw.sb([128, NKC, QW], BF16, "og")
    a3 = og
    fw.push()
    a2 = fw.sb([128, NKC, WT], BF16, "a2")

    fw.push()
    wo1 = fw.sb([128, NKC, D], BF16, "wo1")
    for kc in range(NKC):
        fw.dma(wo1[:, kc, :], hy_w_out.ap()[kc * 128:(kc + 1) * 128, :], q="pool")
    xs2 = [fw.sb([128, NKC, 512], F32, "xs%d" % i) for i in range(2)]
    zg2 = [fw.sb([128, NKC, 512], BF16, "zg%d" % i) for i in range(2)]
    sq = fw.sb([128, NKC, 512], BF16, "sq")
    rstd = fw.sb([128, 512], F32, "rstd")
    jobs = [(x1v, zgv, c0, min(512, W - c0), c0, 0, True) for c0 in range(0, W, 512)] + [(xcv, zgcv, 0, NCX, W, 1, False)]
    for ci, (xsrc, zsrc, c0, T, d0, r, store) in enumerate(jobs):
        xs, zg = xs2[ci % 2], zg2[ci % 2]
        fw.dma(xs[:, :, :T], xsrc[:, :, c0:c0 + T])
        fw.dma(zg[:, :, :T], zsrc[:, :, c0:c0 + T], q="pool")
        for fj in range(8):
            ps = P[2 + fj % 4]
            for kc in range(NKC):
                fw.mm(ps[:, :T], wo1[:, kc, fj * 128:(fj + 1) * 128], zg[:, kc, :T], start=(kc == 0), stop=(kc == NKC - 1))
            fw.stt(xs[:, fj, :T], ps[:, :T], ada1[:, 16 + fj:17 + fj, r], xs[:, fj, :T], ALU.mult, ALU.add)
        if store:
            fw.dma(X2v[:, :, c0:c0 + T], xs[:, :, :T])
        norm_mod(fw, P[0], xs, T, geff2[:, :, r], ada2[:, 0:8, r], a2[:, :, d0:d0 + T], sq, rstd, ones_bf, eps_t)
    fw.pop()

    if stop == "c1":
        return fw
    fw.push()
    CC = fw.sb([128, WT], F32, "CC")
    SS = fw.sb([128, WT], F32, "SS")
    fw.dma(CC[:], ropeC.ap())
    fw.dma(SS[:], ropeS.ap())
    kval = fw.sb([128, 19], F32, "kval")
    fw.dma(kval[:], kvalid_d.ap())
    negm = fw.sb([128, 2, 128], BF16, "negm")
    fw.dma(negm[:], tri_d.ap(), q="pool")
    ident_bf = fw.sb([128, 128], BF16, "ident")
    fw.dma(ident_bf[:], ident_d.ap(), q="pool")
    kvm = fw.sb([128, 19, 128], BF16, "kvm")
    for t in range(19):
        fw.ts(kvm[:, t, :], ones_bf[:], kval[:, t:t + 1], ALU.mult)
    esink = fw.sb([128, 16], F32, "esink")
    fw.dma(esink[:], swa_sink.ap().partition_broadcast(128))
    fw.act(esink[:], esink[:], AF.Exp)
    k2 = [fw.sb([128, WT], BF16, "k2_%d" % i) for i in range(4)]
    NT = WT // 128
    Vd = fw.sb([128, NT, 512], BF16, "Vd")
    fw.push()
    wkv = fw.sb([128, NKC, 512], BF16, "wkv")
    for kc in range(NKC):
        fw.dma(wkv[:, kc, :], swa_w_in.ap()[kc * 128:(kc + 1) * 128, 1024:1536], q="pool")
    wk_d = fw.sb([128, NKC, 4, 2, 64], BF16, "wk_d")
    wkr_d = fw.sb([128, NKC, 4, 2, 64], BF16, "wkr_d")
    wv_d = fw.sb([128, NKC, 4, 2, 64], BF16, "wv_d")
    kview = wkv[:, :, 0:256].rearrange("p k (h f) -> p k h f", f=64)
    vview = wkv[:, :, 256:512].rearrange("p k (h f) -> p k h f", f=64)
    for dup in range(2):
        fw.copy(wk_d[:, :, :, dup, :], kview)
        fw.copy(wv_d[:, :, :, dup, :], vview, eng="act")
        fw.ts(wkr_d[:, :, :, dup, 0:32], kview[:, :, :, 32:64], -1.0, ALU.mult)
        fw.copy(wkr_d[:, :, :, dup, 32:64], kview[:, :, :, 0:32], eng="act")
    ra = fw.sb([128, 512], F32, "ra")
    rb = fw.sb([128, 512], F32, "rb")
    ev = 0
    for hk in range(4):
        for c0 in range(0, WT, 512):
            T = min(512, WT - c0)
            ps = P[2 + ev % 2]; ev += 1
            for kc in range(NKC):
                fw.mm(ps[:, :T], wk_d[:, kc, hk].rearrange("p a f -> p (a f)"), a2[:, kc, c0:c0 + T], start=(kc == 0), stop=(kc == NKC - 1))
            fw.tt(ra[:, :T], ps[:, :T], CC[:, c0:c0 + T], ALU.mult)
            ps = P[2 + ev % 2]; ev += 1
            for kc in range(NKC):
                fw.mm(ps[:, :T], wkr_d[:, kc, hk].rearrange("p a f -> p (a f)"), a2[:, kc, c0:c0 + T], start=(kc == 0), stop=(kc == NKC - 1))
            fw.tt(rb[:, :T], ps[:, :T], SS[:, c0:c0 + T], ALU.mult)
            fw.tt(k2[hk][:, c0:c0 + T], ra[:, :T], rb[:, :T], ALU.add)
    for t in range(NT):
        ps = P[2 + ev % 2]; ev += 1
        for kc in range(NKC):
            fw.mm(ps[:], a2[:, kc, t * 128:(t + 1) * 128], wv_d[:, kc].rearrange("p h a f -> p (h a f)"), start=(kc == 0), stop=(kc == NKC - 1))
        if t < 19:
            fw.ts(Vd[:, t, :], ps[:], kval[:, t:t + 1], ALU.mult)
        else:
            fw.copy(Vd[:, t, :], ps[:], eng="act")
    fw.pop()
    if stop == "c2a":
        return fw
    ra = fw.sb([128, 512], F32, "ra_g")
    rb = fw.sb([128, 512], F32, "rb_g")
    wq = fw.sb([128, NKC, 256], BF16, "wq")
    wqr = fw.sb([128, NKC, 4, 2, 32], BF16, "wqr")
    wg = fw.sb([128, NKC, 256], BF16, "wg")
    qz = fw.sb([128, 4, QW], BF16, "qz")
    for g in range(4):
        fw.memset(qz[:, g, :], 0.0)
    sg = fw.sb([128, 2, QW], BF16, "sg")
    pT = [fw.sb([128, 512], BF16, "pT%d" % i) for i in range(3)]
    den = fw.sb([128, 512], F32, "den")
    onrm = fw.sb([128, 512], F32, "onrm")
    sc_att = 0.125
    pc = 0
    for hk in range(4):
        for kc in range(NKC):
            fw.dma(wq[:, kc, :], swa_w_in.ap()[kc * 128:(kc + 1) * 128, hk * 256:(hk + 1) * 256], q="pool")
            fw.dma(wg[:, kc, :], swa_w_in.ap()[kc * 128:(kc + 1) * 128, 1536 + hk * 256:1536 + (hk + 1) * 256], q="pool")
        wqv = wq[:].rearrange("p k (h a f) -> p k h a f", a=2, f=32)
        fw.ts(wqr[:, :, :, 0, :], wqv[:, :, :, 1, :], -1.0, ALU.mult)
        fw.copy(wqr[:, :, :, 1, :], wqv[:, :, :, 0, :])
        wqr_f = wqr[:].rearrange("p k h a f -> p k (h a f)")
        for cc in range(2):
            for c0 in range(0, QW, 512):
                T = min(512, QW - c0)
                ps = P[2 + ev % 2]; ev += 1
                for kc in range(NKC):
                    fw.mm(ps[:, :T], wq[:, kc, cc * 128:(cc + 1) * 128], a2[:, kc, Q0 + c0:Q0 + c0 + T], start=(kc == 0), stop=(kc == NKC - 1))
                fw.tt(ra[:, :T], ps[:, :T], CC[:, Q0 + c0:Q0 + c0 + T], ALU.mult)
                ps = P[2 + ev % 2]; ev += 1
                for kc in range(NKC):
                    fw.mm(ps[:, :T], wqr_f[:, kc, cc * 128:(cc + 1) * 128], a2[:, kc, Q0 + c0:Q0 + c0 + T], start=(kc == 0), stop=(kc == NKC - 1))
                fw.tt(rb[:, :T], ps[:, :T], SS[:, Q0 + c0:Q0 + c0 + T], ALU.mult)
                for half in range(2):
                    hr = slice(half * 64, (half + 1) * 64)
                    fw.tt(qz[hr, cc * 2 + half, c0:c0 + T], ra[hr, :T], rb[hr, :T], ALU.add)
                ps = P[2 + ev % 2]; ev += 1
                for kc in range(NKC):
                    fw.mm(ps[:, :T], wg[:, kc, cc * 128:(cc + 1) * 128], a2[:, kc, Q0 + c0:Q0 + c0 + T], start=(kc == 0), stop=(kc == NKC - 1))
                fw.act(sg[:, cc, c0:c0 + T], ps[:, :T], AF.Silu)
        for j in range(1, 18):
            qc = (j - 1) * 128
            po = P[4 + pc % 2]
            pd = P[6 + pc % 2]
            pc += 1
            ktl = [(j - 1, 0), (j, None), (j + 1, 1), (19, -1), (20, -1)]
            def s_mm(ki):
                kt, tsel = ktl[ki]
                ps = P[ki % 2]
                masked = tsel in (0, 1)
                for g in range(4):
                    fw.mm(ps[:, g * 128:(g + 1) * 128], k2[hk][:, kt * 128:(kt + 1) * 128],
                          qz[:, g, qc:qc + 128], start=True, stop=not masked)
                    if masked:
                        fw.mm(ps[:, g * 128:(g + 1) * 128], ident_bf[:], negm[:, tsel, :], start=False, stop=True)
            s_mm(0)
            for ki, (kt, tsel) in enumerate(ktl):
                if ki + 1 < len(ktl):
                    s_mm(ki + 1)
                ps = P[ki % 2]
                p_t = pT[ki % 3]
                fw.act(p_t[:], ps[:], AF.Exp, scale=sc_att)
                p_m = p_t
                fw.mm(po[:], Vd[:, kt, hk * 128:(hk + 1) * 128], p_m[:], start=(ki == 0), stop=(ki == 4))
                fw.mm(pd[:], (ones_bf[:] if tsel == -1 else kvm[:, kt, :]), p_m[:], start=(ki == 0), stop=(ki == 4))
            for g in range(4):
                h = hk * 4 + g
                fw.ts(den[:, g * 128:(g + 1) * 128], pd[:, g * 128:(g + 1) * 128], esink[:, h:h + 1], ALU.add)
            fw.recip(den[:], den[:])
            fw.tt(onrm[:], po[:], den[:], ALU.mult)
            for g in range(4):
                hf = (g % 2) * 64
                cch = g // 2
                fw.tt(og[hf:hf + 64, hk * 2 + cch, qc:qc + 128], onrm[hf:hf + 64, g * 128:(g + 1) * 128], sg[hf:hf + 64, cch, qc:qc + 128], ALU.mult)
    fw.pop()
    fw.pop()
    if stop == "c2":
        return fw

    fw.push()
    wo2 = fw.sb([128, NKC, D], BF16, "wo2")
    for kc in range(NKC):
        fw.dma(wo2[:, kc, :], swa_w_out.ap()[kc * 128:(kc + 1) * 128, :], q="pool")
    xs2 = [fw.sb([128, NKC, 512], F32, "xs3_%d" % i) for i in range(2)]
    sq = fw.sb([128, NKC, 512], BF16, "sq3")
    rstd = fw.sb([128, 512], F32, "rstd3")
    for ci, c0 in enumerate(range(0, QW, 512)):
        T = min(512, QW - c0)
        xs = xs2[ci % 2]
        fw.dma(xs[:, :, :T], X2v[:, :, Q0 + c0:Q0 + c0 + T])
        for fj in range(8):
            ps = P[2 + fj % 4]
            for kc in range(NKC):
                fw.mm(ps[:, :T], wo2[:, kc, fj * 128:(fj + 1) * 128], og[:, kc, c0:c0 + T], start=(kc == 0), stop=(kc == NKC - 1))
            fw.stt(xs[:, fj, :T], ps[:, :T], ada2[:, 16 + fj:17 + fj, 0], xs[:, fj, :T], ALU.mult, ALU.add)
        fw.dma(X3v[:, :, c0:c0 + T], xs[:, :, :T])
        norm_mod(fw, P[0], xs, T, geff3[:, :, 0], ada3[:, 0:8, 0], a3[:, :, c0:c0 + T], sq, rstd, ones_bf, eps_t)
    fw.pop()

    if stop == "c3":
        return fw
    fw.push()
    UB = U0 - Q0
    OB = O0 - Q0
    ug = fw.sb([128, NKC, 2048], BF16, "ug")
    fw.push()
    uc = fw.sb([128, NKC, 2048], F32, "uc")
    fw.push()
    cm = fw.sb([128, NU], F32, "cmask")
    fw.dma(cm[:], cmask_d.ap())
    dww = fw.sb([128, 8, 31], F32, "dww")
    fw.dma(dww[:], cf_dw_w.ap())
    wab = [fw.sb([128, NKC, 2, 128], BF16, "wab%d" % i) for i in range(2)]
    usb = [fw.sb([128, NU + 2], BF16, "usb%d" % i) for i in range(2)]
    dgs = [fw.sb([128, 31, 128], BF16, "dg%d" % i) for i in range(2)]
    identc = fw.sb([128, 128], BF16, "identc")
    fw.dma(identc[:], ident_d.ap(), q="pool")
    sig = fw.sb([128, 512], F32, "sig")
    for fj in range(8):
        w2 = wab[fj % 2]
        u = usb[fj % 2]
        for kc in range(NKC):
            fw.dma(w2[:, kc, 0, :], cf_w_in.ap()[kc * 128:(kc + 1) * 128, fj * 128:(fj + 1) * 128], q="pool")
            fw.dma(w2[:, kc, 1, :], cf_w_in.ap()[kc * 128:(kc + 1) * 128, 1024 + fj * 128:1024 + (fj + 1) * 128], q="pool")
        for c0 in range(0, NU, 512):
            T = min(512, NU - c0)
            pa, pb = P[2 + (c0 // 512) % 2], P[4 + (c0 // 512) % 2]
            for kc in range(NKC):
                fw.mm(pa[:, :T], w2[:, kc, 0, :], a3[:, kc, UB + c0:UB + c0 + T], start=(kc == 0), stop=(kc == NKC - 1))
            for kc in range(NKC):
                fw.mm(pb[:, :T], w2[:, kc, 1, :], a3[:, kc, UB + c0:UB + c0 + T], start=(kc == 0), stop=(kc == NKC - 1))
            fw.act(sig[:, :T], pb[:, :T], AF.Sigmoid)
            fw.tt(sig[:, :T], sig[:, :T], cm[:, c0:c0 + T], ALU.mult)
            fw.tt(u[:, c0:c0 + T], pa[:, :T], sig[:, :T], ALU.mult)
        dg = dgs[fj % 2]
        for j in range(31):
            fw.ts(dg[:, j, :], identc[:], dww[:, fj, j:j + 1], ALU.mult)
        for c0 in range(0, 2048, 512):
            pc4 = P[6 + (c0 // 512) % 2]
            for j in range(31):
                fw.mm(pc4[:], dg[:, j, :], u[:, c0 + j:c0 + j + 512], start=(j == 0), stop=(j == 30))
            fw.act(uc[:, fj, c0:c0 + 512], pc4[:], AF.Identity, bias=cfv[:, 0, fj:fj + 1], scale=1.0)
    fw.pop()
    fw.push()
    mean = fw.sb([128, 2048], F32, "mean")
    rs = fw.sb([128, 2048], F32, "rs")
    sqf = fw.sb([128, 512], F32, "sqf")
    for c0 in range(0, 2048, 512):
        for fj in range(8):
            fw.mm(P[2][:], ones_f[:], uc[:, fj, c0:c0 + 512], start=(fj == 0), stop=(fj == 7))
        for fj in range(8):
            fw.tt(sqf[:], uc[:, fj, c0:c0 + 512], uc[:, fj, c0:c0 + 512], ALU.mult)
            fw.mm(P[3][:], ones_f[:], sqf[:], start=(fj == 0), stop=(fj == 7))
        fw.ts(mean[:, c0:c0 + 512], P[2][:], 1.0 / D, ALU.mult)
        fw.tt(sqf[:], mean[:, c0:c0 + 512], mean[:, c0:c0 + 512], ALU.mult)
        fw.stt(sqf[:], P[3][:], 1.0 / D, sqf[:], ALU.mult, ALU.subtract)
        fw.act(rs[:, c0:c0 + 512], sqf[:], AF.Sqrt, bias=eps_t[:, 0:1], scale=1.0)
        fw.recip(rs[:, c0:c0 + 512], rs[:, c0:c0 + 512])
    wgt = [fw.sb([128, NKC, 128], BF16, "wgt%d" % i) for i in range(2)]
    sg3 = fw.sb([128, 512], F32, "sg3")
    for fj in range(8):
        w2 = wgt[fj % 2]
        for kc in range(NKC):
            fw.dma(w2[:, kc, :], cf_w_in.ap()[kc * 128:(kc + 1) * 128, 2048 + fj * 128:2048 + (fj + 1) * 128], q="pool")
        for c0 in range(0, 2048, 512):
            ps = P[4 + (c0 // 512) % 2]
            for kc in range(NKC):
                fw.mm(ps[:], w2[:, kc, :], a3[:, kc, OB + c0:OB + c0 + 512], start=(kc == 0), stop=(kc == NKC - 1))
            fw.act(sg3[:], ps[:], AF.Silu)
            fw.tt(uc[:, fj, c0:c0 + 512], uc[:, fj, c0:c0 + 512], mean[:, c0:c0 + 512], ALU.subtract)
            fw.tt(uc[:, fj, c0:c0 + 512], uc[:, fj, c0:c0 + 512], rs[:, c0:c0 + 512], ALU.mult)
            fw.act(uc[:, fj, c0:c0 + 512], uc[:, fj, c0:c0 + 512], AF.Silu, bias=cfv[:, 2, fj:fj + 1], scale=cfv[:, 1, fj:fj + 1])
            fw.tt(ug[:, fj, c0:c0 + 512], uc[:, fj, c0:c0 + 512], sg3[:], ALU.mult)
    fw.pop()
    fw.pop()
    wo3 = fw.sb([128, NKC, D], BF16, "wo3")
    for kc in range(NKC):
        fw.dma(wo3[:, kc, :], cf_w_out.ap()[kc * 128:(kc + 1) * 128, :], q="pool")
    xs2 = [fw.sb([128, NKC, 512], F32, "xs4_%d" % i) for i in range(2)]
    sq = fw.sb([128, NKC, 512], BF16, "sq4")
    rstd = fw.sb([128, 512], F32, "rstd4")
    for ci, c0 in enumerate(range(0, 2048, 512)):
        xs = xs2[ci % 2]
        fw.dma(xs[:], X3v[:, :, OB + c0:OB + c0 + 512])
        for fj in range(8):
            ps = P[4 + fj % 4]
            for kc in range(NKC):
                fw.mm(ps[:], wo3[:, kc, fj * 128:(fj + 1) * 128], ug[:, kc, c0:c0 + 512], start=(kc == 0), stop=(kc == NKC - 1))
            fw.stt(xs[:, fj, :], ps[:], ada3[:, 16 + fj:17 + fj, 0], xs[:, fj, :], ALU.mult, ALU.add)
        fw.act(sq[:], xs[:], AF.Square)
        for kc in range(NKC):
            fw.mm(P[0][:], ones_bf[:], sq[:, kc, :], start=(kc == 0), stop=(kc == NKC - 1))
        fw.act(rstd[:], P[0][:], AF.Sqrt, bias=eps_t[:, 0:1], scale=1.0 / D)
        fw.recip(rstd[:], rstd[:])
        for kc in range(NKC):
            fw.stt(xs[:, kc, :], xs[:, kc, :], cfv[:, 3, kc:kc + 1], rstd[:], ALU.mult, ALU.mult)
        fw.dma(outv[:, :, c0:c0 + 512], xs[:])
    fw.pop()
    return fw


def host_inputs_C(inp, x1T_full, zgT_full, xc1T, zgcT, b, h):
    s = h * 2048
    lo, hi = s - 192, s - 192 + W
    a, e = max(lo, 0), min(hi, 4096)

    def win(src):
        o = np.zeros((D, W), np.float32)
        o[:, a - lo:e - lo] = src[:, a:e]
        return o
    cos, sin = rope_tables(4096)
    C = np.ones((64, WT), np.float32)
    S = np.zeros((64, WT), np.float32)
    C[:, a - lo:e - lo] = cos[:, a:e]
    S[:, a - lo:e - lo] = sin[:, a:e]
    valid = np.zeros(W, np.float32)
    valid[a - lo:e - lo] = 1.0
    kq = np.arange(128)
    tri = np.stack([(kq[None, :] <= kq[:, None]), (kq[:, None] <= kq[None, :])], axis=1).astype(np.float32)
    fm = lambda v: np.ascontiguousarray(v.reshape(8, 128).T)
    return {
        "x1T": win(x1T_full), "zgT": win(zgT_full), "xc1T": np.ascontiguousarray(xc1T), "zgcT": np.ascontiguousarray(zgcT),
        "cT": np.ascontiguousarray(np.stack([inp["c"][b], inp["c_ctx"]], axis=1), np.float32),
        "norm_g": np.ascontiguousarray(inp["norm_g"].reshape(4, 8, 128).transpose(0, 2, 1)[1:4]),
        "ada_w": np.ascontiguousarray(inp["ada_w"][1:4]),
        "ada_b": np.ascontiguousarray(inp["ada_b"].reshape(4, 24, 128).transpose(0, 2, 1)[1:4]),
        "hy_w_out": inp["hy_w_out"][0], "swa_w_in": inp["swa_w_in"][0],
        "swa_sink": inp["swa_sink"][0].reshape(1, 16), "swa_w_out": inp["swa_w_out"][0],
        "cf_w_in": inp["cf_w_in"][0],
        "cf_dw_w": np.ascontiguousarray(inp["cf_dw_w"][0].T.reshape(8, 128, 31).transpose(1, 0, 2)),
        "cf_vec": np.ascontiguousarray(np.stack([fm(inp["cf_dw_b"][0]), fm(inp["cf_ln_g"][0]), fm(inp["cf_ln_b"][0]), fm(inp["final_g"])], axis=1)),
        "cf_w_out": inp["cf_w_out"][0],
        "ropeC": np.ascontiguousarray(np.concatenate([C, C], 0)), "ropeS": np.ascontiguousarray(np.concatenate([S, S], 0)),
        "kvalid": np.ascontiguousarray(valid.reshape(19, 128).T),
        "cmask": np.ascontiguousarray(np.broadcast_to(valid[U0:U0 + NU], (128, NU))),
        "negm": np.ascontiguousarray((tri - 1.0) * 30000.0), "ident": np.eye(128, dtype=np.float32), "ident2": np.eye(2, dtype=np.float32),
    }


import numpy as np

_NC = {}


def _prog(name, builder):
    if name not in _NC:
        _NC[name] = builder().finish()
    return _NC[name]


def kernel(**inputs):
    inp = {k: np.asarray(v) for k, v in inputs.items()}
    cores = list(range(8))
    rA = run_bass_kernel_spmd(_prog("A", build_A), [host_inputs_A(inp, c // 2, c % 2) for c in cores], core_ids=cores).results
    x1T = [np.concatenate([rA[2 * b]["x1T"][:, :2048], rA[2 * b + 1]["x1T"][:, :2048]], axis=1) for b in range(4)]
    xc1T = [rA[2 * b]["x1T"][:, 2048:] for b in range(4)]
    rA2 = run_bass_kernel_spmd(_prog("A2", build_A2), [host_inputs_A2(inp, x1T[c // 2], xc1T[c // 2], c // 2, c % 2) for c in cores],
                               core_ids=cores).results
    U_all = np.stack([np.concatenate([rA2[2 * b]["U"], rA2[2 * b + 1]["U"]], axis=0) for b in range(4)])
    Uc_all = np.stack([rA2[2 * b]["Uc"] for b in range(4)])
    rB = run_bass_kernel_spmd(_prog("B", build_B), [host_inputs_B(inp, U_all, Uc_all, c) for c in cores], core_ids=cores).results
    zgT = [np.concatenate([rB[c]["ZgT"][:, b * 4096:(b + 1) * 4096] for c in cores], axis=0) for b in range(4)]
    zgcT = [np.concatenate([rB[c]["ZgcT"][:, b * 256:(b + 1) * 256] for c in cores], axis=0) for b in range(4)]
    rC = run_bass_kernel_spmd(_prog("C", build_C), [host_inputs_C(inp, x1T[c // 2], zgT[c // 2], xc1T[c // 2], zgcT[c // 2], c // 2, c % 2) for c in cores],
                              core_ids=cores).results
    out = np.empty((4, 4096, 1024), np.float32)
    for c in cores:
        out[c // 2, (c % 2) * 2048:(c % 2 + 1) * 2048, :] = rC[c]["outT"].T
    return out
```
